# Optimizing a Trainium2 kernel written in Bass

```python
import math
import jax, jax.numpy as jnp
from jax import lax
import numpy as np

D_MODEL = 2048
BATCH = 4
SEQ = 4096
DEPTH = 1

DN_HEADS = 16
DN_DK = 128
DN_DV = 128
DN_CONV = 4
DN_CHUNK = 64
DN_QK = DN_HEADS * DN_DK
DN_VW = DN_HEADS * DN_DV
MB_HEADS = 16
MB_HD = 128
MB_W = MB_HEADS * MB_HD
MB_BLOCK = 256
MB_TOPK = 3
MB_Q_CHUNK = 32
RP_BUCKETS = 32
RP_MAX_DIST = 1024
D_FF = 5632
NORM_EPS = 1e-6
N_ADA = 9
IN_SIZES = (2 * DN_QK + DN_VW, DN_VW, DN_HEADS, DN_HEADS, MB_W, MB_W, MB_W, D_MODEL, D_MODEL)
IN_WIDTH = 2 * DN_QK + 2 * DN_VW + 2 * DN_HEADS + 3 * MB_W + 2 * D_MODEL

kernel_name = "hybrid_deltanet_moba_macaron_adaln"


def rms_norm(x, gain):
    xf = x.astype(jnp.float32)
    y = xf * lax.rsqrt(jnp.mean(xf * xf, axis=-1, keepdims=True) + NORM_EPS)
    return (y * gain.astype(jnp.float32)).astype(x.dtype)


def l2norm(x):
    xf = x.astype(jnp.float32)
    return xf * lax.rsqrt(jnp.sum(xf * xf, axis=-1, keepdims=True) + NORM_EPS)


def modulate(xn, shift, scale):
    return xn * (1.0 + scale) + shift


def swiglu(x, w1, w3, w2):
    return (jax.nn.silu(x @ w1) * (x @ w3)) @ w2


def causal_depthwise_conv(x, w):
    K = w.shape[0]
    T = x.shape[1]
    xp = jnp.pad(x, ((0, 0), (K - 1, 0), (0, 0)))
    return sum(xp[:, j:j + T, :] * w[j] for j in range(K))


def gated_delta_chunked(q, k, v, g, beta):
    B, T, H, DK = q.shape
    DV = v.shape[-1]
    C = DN_CHUNK
    N = T // C
    f32 = jnp.float32

    def to_chunks(a):
        a = a.astype(f32).reshape((B, N, C, H) + a.shape[3:])
        return jnp.moveaxis(a, 3, 1)

    q, k, v, g, beta = (to_chunks(a) for a in (q, k, v, g, beta))
    q = q * (DK ** -0.5)
    gc = jnp.cumsum(g, axis=-1)
    idx = jnp.arange(C)
    incl = idx[:, None] >= idx[None, :]
    strict = idx[:, None] > idx[None, :]
    decay = jnp.exp(jnp.where(incl, gc[..., :, None] - gc[..., None, :], -jnp.inf))
    k_beta = k * beta[..., None]
    Lmat = jnp.where(strict, jnp.einsum('bhncd,bhnsd->bhncs', k_beta, k) * decay, 0.0)
    eye = jnp.eye(C, dtype=f32)
    Tm = lax.linalg.triangular_solve(eye + Lmat, jnp.broadcast_to(eye, Lmat.shape),
                                     left_side=True, lower=True, unit_diagonal=True)
    u = jnp.einsum('bhncs,bhnsd->bhncd', Tm, v * beta[..., None])
    w = jnp.einsum('bhncs,bhnsd->bhncd', Tm, k_beta * jnp.exp(gc)[..., None])
    qk = jnp.einsum('bhncd,bhnsd->bhncs', q, k) * decay
    q_dec = q * jnp.exp(gc)[..., None]
    k_dec = k * jnp.exp(gc[..., -1:] - gc)[..., None]
    g_tot = jnp.exp(gc[..., -1])

    def step(S, xs):
        qk_c, qd_c, kd_c, u_c, w_c, gt_c = xs
        v_new = u_c - jnp.einsum('bhcd,bhde->bhce', w_c, S)
        o = jnp.einsum('bhcd,bhde->bhce', qd_c, S) + jnp.einsum('bhcs,bhse->bhce', qk_c, v_new)
        S = S * gt_c[..., None, None] + jnp.einsum('bhcd,bhce->bhde', kd_c, v_new)
        return S, o

    xs = tuple(jnp.moveaxis(a, 2, 0) for a in (qk, q_dec, k_dec, u, w, g_tot))
    S0 = jnp.zeros((B, H, DK, DV), f32)
    _, o = lax.scan(step, S0, xs)
    return jnp.transpose(o, (1, 0, 3, 2, 4)).reshape(B, T, H, DV)


def t5_bucket(dist):
    max_exact = RP_BUCKETS // 2
    d = jnp.maximum(dist, 0)
    large = max_exact + (jnp.log(jnp.maximum(d, 1).astype(jnp.float32) / max_exact)
                         / math.log(RP_MAX_DIST / max_exact) * (RP_BUCKETS - max_exact)).astype(jnp.int32)
    large = jnp.minimum(large, RP_BUCKETS - 1)
    return jnp.where(d < max_exact, d, large)


def moba_attention(q, k, v, rel_bias):
    B, T, H, D = q.shape
    blk = MB_BLOCK
    nb = -(-T // blk)
    Tp = nb * blk
    f32 = jnp.float32
    q = jnp.transpose(q, (0, 2, 1, 3)).astype(f32) * (D ** -0.5)
    pad = ((0, 0), (0, 0), (0, Tp - T), (0, 0))
    kb = jnp.pad(jnp.transpose(k, (0, 2, 1, 3)).astype(f32), pad).reshape(B, H, nb, blk, D)
    vb = jnp.pad(jnp.transpose(v, (0, 2, 1, 3)).astype(f32), pad).reshape(B, H, nb, blk, D)
    k_mean = jnp.mean(kb, axis=3)
    pos = jnp.arange(T, dtype=jnp.int32)
    qblk = pos // blk
    gate = jnp.einsum('bhtd,bhjd->bhtj', q, k_mean)
    past = jnp.arange(nb)[None, :] < qblk[:, None]
    gate = jnp.where(past, gate, -jnp.inf)
    n_sel = min(MB_TOPK, nb)
    _, sel = lax.top_k(gate, n_sel)
    sel_valid = sel < qblk[:, None]
    rb = rel_bias.T.astype(f32)
    QC = MB_Q_CHUNK
    nq = T // QC

    def chunks(a):
        a = a.reshape(a.shape[:2] + (nq, QC) + a.shape[3:])
        return jnp.moveaxis(a, 2, 0)

    xs = (chunks(q), chunks(sel), chunks(sel_valid), pos.reshape(nq, QC))
    offs = jnp.arange(blk, dtype=jnp.int32)
    bi = jnp.arange(B)[:, None, None, None]
    hi = jnp.arange(H)[None, :, None, None]
    hi5 = jnp.arange(H)[None, :, None, None, None]

    def attend(args):
        qc, sc, vc, qp = args
        own = qp[0] // blk
        k_own = lax.dynamic_index_in_dim(kb, own, axis=2, keepdims=False)
        v_own = lax.dynamic_index_in_dim(vb, own, axis=2, keepdims=False)
        k_sel = kb[bi, hi, sc]
        v_sel = vb[bi, hi, sc]
        s_sel = jnp.einsum('bhqd,bhqskd->bhqsk', qc, k_sel)
        dist_sel = qp[None, None, :, None, None] - (sc[..., None] * blk + offs)
        s_sel = jnp.where(vc[..., None], s_sel + rb[hi5, t5_bucket(dist_sel)], -jnp.inf)
        dist_own = qp[:, None] - (own * blk + offs)[None, :]
        s_own = jnp.einsum('bhqd,bhkd->bhqk', qc, k_own) + rb[:, t5_bucket(dist_own)]
        s_own = jnp.where(dist_own >= 0, s_own, -jnp.inf)
        ns = sc.shape[-1] * blk
        logits = jnp.concatenate([s_sel.reshape(B, H, QC, ns), s_own], axis=-1)
        p = jax.nn.softmax(logits, axis=-1)
        p_sel = p[..., :ns].reshape(B, H, QC, sc.shape[-1], blk)
        p_own = p[..., ns:]
        return (jnp.einsum('bhqsk,bhqskd->bhqd', p_sel, v_sel)
                + jnp.einsum('bhqk,bhkd->bhqd', p_own, v_own))

    o = lax.map(attend, xs)
    o = jnp.transpose(o, (1, 0, 3, 2, 4)).reshape(B, T, H * D)
    return o


def hybrid_mixer(u, w_in, conv_w, a_log, dt_bias, dn_norm_g, q_norm_g, k_norm_g, rel_bias,
                 w_proj_a, w_proj_b, w_out):
    B, T, _ = u.shape
    f32 = jnp.float32
    proj = u @ w_in
    cuts, acc = [], 0
    for s in IN_SIZES[:-1]:
        acc += s
        cuts.append(acc)
    dn_qkv, dn_z, dn_b, dn_a, mb_q, mb_k, mb_v, gate_a, gate_b = jnp.split(proj, cuts, axis=-1)
    dn_qkv = jax.nn.silu(causal_depthwise_conv(dn_qkv, conv_w))
    q, k, v = jnp.split(dn_qkv, [DN_QK, 2 * DN_QK], axis=-1)
    q = l2norm(q.reshape(B, T, DN_HEADS, DN_DK))
    k = l2norm(k.reshape(B, T, DN_HEADS, DN_DK))
    v = v.reshape(B, T, DN_HEADS, DN_DV)
    beta = jax.nn.sigmoid(dn_b.astype(f32))
    g = -jnp.exp(a_log.astype(f32)) * jax.nn.softplus(dn_a.astype(f32) + dt_bias.astype(f32))
    o = gated_delta_chunked(q, k, v, g, beta)
    o = rms_norm(o, dn_norm_g) * jax.nn.silu(dn_z.reshape(B, T, DN_HEADS, DN_DV).astype(f32))
    y_a = o.reshape(B, T, DN_VW).astype(u.dtype)
    qm = rms_norm(mb_q.reshape(B, T, MB_HEADS, MB_HD), q_norm_g)
    km = rms_norm(mb_k.reshape(B, T, MB_HEADS, MB_HD), k_norm_g)
    vm = mb_v.reshape(B, T, MB_HEADS, MB_HD)
    y_b = moba_attention(qm, km, vm, rel_bias).astype(u.dtype)
    merged = jax.nn.sigmoid(gate_a) * (y_a @ w_proj_a) + jax.nn.sigmoid(gate_b) * (y_b @ w_proj_b)
    return merged @ w_out


def setup_inputs(seed: int = 0) -> dict:
    key = jax.random.key(seed)
    ks = jax.random.split(key, 26)
    f32 = jnp.float32
    L, D = DEPTH, D_MODEL
    conv_ch = 2 * DN_QK + DN_VW

    def nrm(k, shape, scale):
        return jax.random.normal(k, shape, f32) * scale

    def gain(k, shape):
        return 1.0 + 0.02 * jax.random.normal(k, shape, f32)

    dt = jnp.exp(jax.random.uniform(ks[12], (L, DN_HEADS), f32, math.log(1e-3), math.log(1e-1)))
    return {
        "x": nrm(ks[0], (BATCH, SEQ, D), 1.0),
        "c": nrm(ks[1], (BATCH, D), 1.0),
        "ada_w": nrm(ks[2], (L, D, N_ADA * D), 0.5 * D ** -0.5),
        "ada_b": nrm(ks[3], (L, N_ADA * D), 0.01),
        "norm1_g": gain(ks[4], (L, D)),
        "ffn1_w1": nrm(ks[5], (L, D, D_FF), D ** -0.5),
        "ffn1_w3": nrm(ks[6], (L, D, D_FF), D ** -0.5),
        "ffn1_w2": nrm(ks[7], (L, D_FF, D), D_FF ** -0.5),
        "norm2_g": gain(ks[8], (L, D)),
        "w_in": nrm(ks[9], (L, D, IN_WIDTH), D ** -0.5),
        "dn_conv_w": nrm(ks[10], (L, DN_CONV, conv_ch), 0.5),
        "dn_a_log": jnp.log(jax.random.uniform(ks[11], (L, DN_HEADS), f32, 1.0, 16.0)),
        "dn_dt_bias": dt + jnp.log(-jnp.expm1(-dt)),
        "dn_norm_g": gain(ks[13], (L, DN_DV)),
        "mb_q_norm_g": gain(ks[14], (L, MB_HD)),
        "mb_k_norm_g": gain(ks[15], (L, MB_HD)),
        "rel_bias": nrm(ks[16], (RP_BUCKETS, MB_HEADS), 0.5),
        "w_proj_a": nrm(ks[17], (L, DN_VW, D), DN_VW ** -0.5),
        "w_proj_b": nrm(ks[18], (L, MB_W, D), MB_W ** -0.5),
        "w_out": nrm(ks[19], (L, D, D), D ** -0.5),
        "norm3_g": gain(ks[20], (L, D)),
        "ffn2_w1": nrm(ks[21], (L, D, D_FF), D ** -0.5),
        "ffn2_w3": nrm(ks[22], (L, D, D_FF), D ** -0.5),
        "ffn2_w2": nrm(ks[23], (L, D_FF, D), D_FF ** -0.5),
    }


def reference(x, c, ada_w, ada_b, norm1_g, ffn1_w1, ffn1_w3, ffn1_w2, norm2_g, w_in, dn_conv_w,
              dn_a_log, dn_dt_bias, dn_norm_g, mb_q_norm_g, mb_k_norm_g, rel_bias, w_proj_a,
              w_proj_b, w_out, norm3_g, ffn2_w1, ffn2_w3, ffn2_w2):
    h = x
    for l in range(DEPTH):
        ada = jax.nn.silu(c) @ ada_w[l] + ada_b[l]
        sh1, sc1, g1, sh2, sc2, g2, sh3, sc3, g3 = jnp.split(ada[:, None, :], N_ADA, axis=-1)
        u = modulate(rms_norm(h, norm1_g[l]), sh1, sc1)
        h = h + 0.5 * g1 * swiglu(u, ffn1_w1[l], ffn1_w3[l], ffn1_w2[l])
        u = modulate(rms_norm(h, norm2_g[l]), sh2, sc2)
        h = h + g2 * hybrid_mixer(u, w_in[l], dn_conv_w[l], dn_a_log[l], dn_dt_bias[l], dn_norm_g[l],
                                  mb_q_norm_g[l], mb_k_norm_g[l], rel_bias, w_proj_a[l], w_proj_b[l],
                                  w_out[l])
        u = modulate(rms_norm(h, norm3_g[l]), sh3, sc3)
        h = h + 0.5 * g3 * swiglu(u, ffn2_w1[l], ffn2_w3[l], ffn2_w2[l])
    return h
```

```python
import math
import numpy as np
from concourse.bass_utils import run_bass_kernel_spmd
import numpy as np
import concourse.bass as bass
import concourse.mybir as mybir
from contextlib import ExitStack

F32 = mybir.dt.float32
BF16 = mybir.dt.bfloat16
AF = mybir.ActivationFunctionType
ALU = mybir.AluOpType
AX = mybir.AxisListType

ENGS = ("pe", "act", "dve", "pool", "sp")


class Buf:
    __slots__ = ("name", "w", "r", "multi")

    def __init__(self, name="", multi=False):
        self.name = name
        self.multi = multi
        self.w = None
        self.r = {}


class Track:
    def __init__(self, sem, name):
        self.sem = sem
        self.n = 0
        self.name = name


class FW:
    def __init__(self, nc, es):
        self.nc = nc
        self.es = es
        self.es_glob = es
        self.h = {"pe": nc.tensor, "act": nc.scalar, "dve": nc.vector, "pool": nc.gpsimd, "sp": nc.sync}
        self.sem = {e: es.enter_context(nc.semaphore("sem_" + e)) for e in ENGS}
        self.cnt = {e: 0 for e in ENGS}
        self.seen = {e: {} for e in ENGS}
        self.prog = {e: [] for e in ENGS}
        self.segs = []
        self.tracks = []
        self.ninst = 0

    def track(self, name):
        t = Track(self.es_glob.enter_context(self.nc.semaphore("trk_" + name)), name)
        self.tracks.append(t)
        return t

    def sb(self, name, shape, dt):
        self.uid = getattr(self, "uid", 0) + 1
        return self.es.enter_context(self.nc.sbuf_tensor(f"{name}_{self.uid}", list(shape), dt))

    def ps(self, name, shape, dt=F32):
        return self.es.enter_context(self.nc.psum_tensor(name, list(shape), dt))

    def _deps(self, eng, reads, writes):
        deps = {}
        def add(d):
            if d is None:
                return
            kind, key, val = d
            if kind == 'e' and key == eng and eng in ('pe', 'sp'):
                return
            k = (kind, key if kind == 'e' else id(key))
            if k not in deps or deps[k][2] < val:
                deps[k] = d
        for b in reads:
            add(b.w)
        for b in writes:
            if b.multi:
                continue
            add(b.w)
            for d in b.r.values():
                add(d)
        waits = []
        seen = self.seen[eng]
        for k, (kind, key, val) in deps.items():
            if seen.get(k, 0) >= val:
                continue
            seen[k] = val
            sem = self.sem[key] if kind == 'e' else key.sem
            waits.append((sem, val))
        return waits

    def op(self, eng, name, kw, reads=(), writes=(), sig=True):
        waits = self._deps(eng, reads, writes)
        sem = self.sem[eng]
        fn = lambda h, name=name, kw=kw: getattr(h, name)(**kw)
        if sig:
            self.cnt[eng] += 1
            idx = self.cnt[eng]
            def run(h, fn=fn, waits=waits, sem=sem):
                for (s, v) in waits:
                    h.wait_ge(s, v)
                fn(h).then_inc(sem, 1)
        else:
            idx = self.cnt[eng] + 1
            def run(h, fn=fn, waits=waits):
                for (s, v) in waits:
                    h.wait_ge(s, v)
                fn(h)
        self.prog[eng].append(run)
        tok = ('e', eng, idx)
        for b in reads:
            b.r[('e', eng)] = tok
        for b in writes:
            b.w = tok
            b.r = {}
        self.ninst += 1
        return tok

    def dma(self, q, track, out, in_, reads=(), writes=()):
        if callable(out) or callable(in_):
            q = "pool"
        waits = self._deps(q, reads, writes)
        track.n += 16
        val = track.n
        def run(h, waits=waits, out=out, in_=in_, sem=track.sem):
            for (s, v) in waits:
                h.wait_ge(s, v)
            o_ = out(h) if callable(out) else out
            i_ = in_(h) if callable(in_) else in_
            h.dma_start(out=o_, in_=i_).then_inc(sem, 16)
        self.prog[q].append(run)
        tok = ('d', track, val)
        for b in reads:
            b.r[('d', id(track))] = tok
        for b in writes:
            b.w = tok
            b.r = {}
        self.ninst += 1
        return tok

    def barrier(self):
        for e in ENGS:
            waits = []
            seen = self.seen[e]
            for e2 in ENGS:
                if e2 == e or self.cnt[e2] == 0:
                    continue
                k = ('e', e2)
                if seen.get(k, 0) < self.cnt[e2]:
                    seen[k] = self.cnt[e2]
                    waits.append((self.sem[e2], self.cnt[e2]))
            for t in self.tracks:
                k = ('d', id(t))
                if t.n > 0 and seen.get(k, 0) < t.n:
                    seen[k] = t.n
                    waits.append((t.sem, t.n))
            def run(h, waits=waits):
                for (s, v) in waits:
                    h.wait_ge(s, v)
            self.prog[e].append(run)

    def core_barrier(self):
        self.barrier()
        self.segs.append(self.prog)
        self.prog = {e: [] for e in ENGS}

    def finish(self, out_bufs):
        waits = self._deps("sp", out_bufs, ())
        def run(h, waits=waits):
            for (s, v) in waits:
                h.wait_ge(s, v)
        self.prog["sp"].append(run)

    def emit(self):
        nc = self.nc
        segs = self.segs + [self.prog]
        for si, prog in enumerate(segs):
            self.cur_seg = si
            if si > 0:
                nc.all_core_barrier()
            with nc.Block() as block:
                @block.tensor
                def _(h, prog=prog):
                    for f in prog["pe"]:
                        f(h)
                @block.scalar
                def _(h, prog=prog):
                    for f in prog["act"]:
                        f(h)
                @block.vector
                def _(h, prog=prog):
                    for f in prog["dve"]:
                        f(h)
                @block.gpsimd
                def _(h, prog=prog):
                    for f in prog["pool"]:
                        f(h)
                @block.sync
                def _(h, prog=prog):
                    for f in prog["sp"]:
                        f(h)


D = 2048
T = 4096
FF = 5632
KC = D // 128
FFC = FF // 128
TT = 512
NTILE = T // TT
EPS = 1e-6
NH = 16
NCH = T // 64
WP = 18432
LE = 4736
WSH = 4609
NEG = -30000.0


def MM(out, **kw):
    return dict(out=out, **kw)


def TRP(out, in_, identity):
    return dict(out=out, in_=in_, identity=identity)


def MS(ap, constant):
    return dict(ap=ap, constant=constant)


class PsumPool:
    def __init__(self, fw, n, name="ps"):
        self.t = [fw.ps(f"{name}{i}", [128, 512]) for i in range(n)]
        self.b = [Buf(f"{name}{i}") for i in range(n)]
        self.i = 0
        self.n = n

    def get(self):
        i = self.i
        self.i = (self.i + 1) % self.n
        return self.t[i], self.b[i]


class Rot:
    def __init__(self, fw, n, shape, dt, name):
        self.t = [fw.sb(f"{name}{i}", shape, dt) for i in range(n)]
        self.b = [Buf(f"{name}{i}") for i in range(n)]
        self.i = 0
        self.n = n

    def get(self):
        i = self.i
        self.i = (self.i + 1) % self.n
        return self.t[i], self.b[i]


class WSlots:
    def __init__(self, fw, trks, shape, name, dt=BF16, q="pool"):
        n = len(trks)
        self.fw = fw
        self.t = [fw.sb(f"{name}{i}", shape, dt) for i in range(n)]
        self.b = [Buf(f"{name}{i}") for i in range(n)]
        self.trk = trks
        self.i = 0
        self.n = n
        self.q = q

    def load(self, src_ap, dst_slice=None):
        i = self.i
        self.i = (self.i + 1) % self.n
        dst = self.t[i][:] if dst_slice is None else dst_slice(self.t[i])
        self.fw.dma(self.q, self.trk[i], dst, src_ap, writes=[self.b[i]])
        return self.t[i], self.b[i]


def build(stage=99, NT=NTILE):
    NPRE = NT // 2
    nc = bass.Bass("TRN2", target_bir_lowering=False)
    def din(name, shape, dt=F32):
        return nc.dram_tensor(name, list(shape), dt, kind="ExternalInput").ap()
    dbg = stage != 99
    def dscr(name, shape, dt=F32):
        return nc.dram_tensor(name, list(shape), dt, kind=("ExternalOutput" if dbg else "Internal")).ap()
    xT = din("xT", [D, T])
    cT = din("cT", [128, KC])
    ada_w = din("ada_w", [D, 9 * D])
    ada_bT = din("ada_bT", [128, 9 * KC])
    ngT = din("ngT", [128, 3 * KC])
    w1a = din("ffn1_w1", [D, FF]); w3a = din("ffn1_w3", [D, FF]); w2a = din("ffn1_w2", [FF, D])
    w1b = din("ffn2_w1", [D, FF]); w3b = din("ffn2_w3", [D, FF]); w2b = din("ffn2_w2", [FF, D])
    winp = din("winp", [D, WP + 32])
    convw = din("convw", [128, 48, 4])
    alog = din("alog", [1, NH]); dtb = din("dtb", [1, NH])
    hg = din("hg", [128, 3])
    relb = din("relb", [32, NH])
    wpa = din("wpa", [D, D]); wpb = din("wpb", [D, D]); wout = din("wout", [D, D])
    ones_in = din("ones", [128, 128]); ident_in = din("ident", [128, 128])
    tri_in = din("tri", [64, 64]); mstrict_in = din("mstrict", [64, 64])
    oh_in = din("oh", [32, LE]); negm_in = din("negm", [1, LE])
    pastneg_in = din("pastneg", [128, 16, 16]); pastflag_in = din("pastflag", [128, 16, 16])
    esel_in = din("esel", [16, 16, 128])
    jrev_in = din("jrev", [128, 128])
    flag_in = din("flagv", [128, 1])
    extram_in = din("extram", [128, 16, 16])
    outT = nc.dram_tensor("outT", [D, T // 2], F32, kind="ExternalOutput").ap()
    h1T = dscr("h1T", [D, T]); qkvT = dscr("qkvT", [3 * D, T]); zT = dscr("zT", [D, T])
    qmT = dscr("qmT", [D, T]); kmT = dscr("kmT", [D, T]); vm = dscr("vm", [T, D], BF16)
    ba = dscr("ba", [T, 32]); gaT = dscr("gaT", [D, T]); gbT = dscr("gbT", [D, T])
    yaT = dscr("yaT", [D, T], BF16); ybT = dscr("ybT", [D, T], BF16)
    E_d = dscr("E_d", [NH, LE])
    Bh1, Bqkv, Bz, Bqm, Bkm, Bvm, Bba, Bga, Bgb, Bya, Byb, BE, Bout = [Buf(n, multi=(n != "E")) for n in
        ("h1", "qkv", "z", "qm", "km", "vm", "ba", "ga", "gb", "ya", "yb", "E", "out")]

    def fmv(ap):
        return ap.rearrange("(c p) t -> p c t", p=128)

    es = ExitStack()
    with es:
        fw = FW(nc, es)
        trk = {n: fw.track(n) for n in ("const", "x", "o", "a0", "a1", "a2", "a3", "b0", "b1", "w0", "w1",
                                         "l0", "l1", "l2", "l3", "l4", "s0", "s1", "s2")}
        pp = PsumPool(fw, 6)
        trk_of = {}
        def btrack(B):
            if id(B) not in trk_of:
                trk_of[id(B)] = (fw.track(f"bt{len(trk_of)}"), B)
            return trk_of[id(B)][0]
        caches = {}
        def wload(slots, key, nblk, idx, src_ap):
            shp = slots.t[0].shape
            elems = int(shp[1]) * int(shp[2])
            if key not in caches:
                caches[key] = (nc.dram_tensor("wc_" + key, [nblk, 128, elems], BF16, kind="Internal").ap(),
                               [Buf(f"wc_{key}{i}") for i in range(nblk)], [False] * nblk)
            cd, cb, filled = caches[key]
            cview = cd[idx].rearrange("p (c n) -> p c n", c=int(shp[1]))
            if filled[idx]:
                i = slots.i
                slots.i = (slots.i + 1) % slots.n
                fw.dma(slots.q, slots.trk[i], slots.t[i][:], cview, reads=[cb[idx]], writes=[slots.b[i]])
                return slots.t[i], slots.b[i]
            t_, b_ = slots.load(src_ap)
            fw.dma("sp", btrack(b_), cview, t_[:], reads=[b_], writes=[cb[idx]])
            filled[idx] = True
            return t_, b_
        Bconst = Buf("const")
        def cload(name, shape, src, dt=F32, q="sp"):
            t = fw.sb(name, shape, dt)
            fw.dma(q, trk["const"], t[:], src, writes=[Bconst])
            return t
        ones_f = cload("ones_f", [128, 128], ones_in)
        ident = cload("ident", [128, 128], ident_in)
        ones_b = cload("ones_b", [128, 128], ones_in, BF16, "pool")
        hgt = cload("hgt", [128, 3], hg)
        flagv = cload("flagv", [128, 1], flag_in)
        adaT = fw.sb("adaT", [128, 9 * KC], F32); Bada = Buf("ada")
        mod = fw.sb("mod", [128, 9 * KC], F32); Bmod = Buf("mod")
        qgs = fw.sb("qgs", [128, 1], F32)
        fw.op("dve", "tensor_scalar", dict(out=qgs[:], in0=hgt[:, 1:2], scalar1=128.0 ** -0.5, scalar2=None, op0=ALU.mult),
              reads=[Bconst], writes=[Bconst])
        with ExitStack() as es0:
            fw.es = es0
            cs = fw.sb("cs", [128, KC], F32); Bcs = Buf("cs")
            abT = fw.sb("abT", [128, 9 * KC], F32)
            gT = fw.sb("gT", [128, 3 * KC], F32)
            fw.dma("sp", trk["const"], cs[:], cT, writes=[Bcs])
            fw.dma("sp", trk["const"], abT[:], ada_bT, writes=[Bconst])
            fw.dma("sp", trk["const"], gT[:], ngT, writes=[Bconst])
            fw.op("act", "activation", dict(out=cs[:], in_=cs[:], func=AF.Silu), reads=[Bcs], writes=[Bcs])
            aw = WSlots(fw, [trk["w0"], trk["w1"]], [128, KC, 512], "aw", dt=F32, q="sp")
            pada = fw.ps("pada", [128, 9 * KC]); Bpada = Buf("pada")
            awv = ada_w.rearrange("(c p) n -> p c n", p=128)
            arow = fw.sb("arow", [1, 9 * D], F32); Barow = Buf("arow")
            for blk in range(9 * D // 512):
                wt, wb = aw.load(awv[:, :, blk * 512:(blk + 1) * 512])
                prow, Bprow = pp.get()
                for kc in range(KC):
                    fw.op("pe", "matmul", MM(prow[0:1, :], lhsT=cs[:, kc:kc + 1], rhs=wt[:, kc, :], start=(kc == 0), stop=(kc == KC - 1)),
                          reads=[wb, Bcs], writes=[Bprow], sig=(kc == KC - 1))
                fw.op("act", "activation", dict(out=arow[0:1, blk * 512:(blk + 1) * 512], in_=prow[0:1, :], func=AF.Copy),
                      reads=[Bprow], writes=[Barow])
            for col in range(9 * KC):
                fw.op("pe", "matmul", MM(pada[:, col:col + 1], lhsT=arow[0:1, col * 128:(col + 1) * 128], rhs=ones_f[0:1, 0:1], start=True, stop=True),
                      reads=[Barow, Bconst], writes=[Bpada], sig=(col == 9 * KC - 1))
            fw.op("dve", "tensor_tensor", dict(out=adaT[:], in0=pada[:], in1=abT[:], op=ALU.add),
                  reads=[Bpada, Bconst], writes=[Bada])
            for s in range(3):
                sh = adaT[:, (3 * s) * KC:(3 * s + 1) * KC]
                sc = adaT[:, (3 * s + 1) * KC:(3 * s + 2) * KC]
                gt = adaT[:, (3 * s + 2) * KC:(3 * s + 3) * KC]
                A = mod[:, (3 * s) * KC:(3 * s + 1) * KC]
                Bv = mod[:, (3 * s + 1) * KC:(3 * s + 2) * KC]
                G = mod[:, (3 * s + 2) * KC:(3 * s + 3) * KC]
                gn = gT[:, s * KC:(s + 1) * KC]
                fw.op("dve", "scalar_tensor_tensor", dict(
                    out=A, in0=sc, scalar=1.0, in1=gn, op0=ALU.add, op1=ALU.mult), reads=[Bada, Bconst], writes=[Bmod])
                fw.op("dve", "tensor_copy", dict(out=Bv, in_=sh), reads=[Bada], writes=[Bmod])
                fw.op("dve", "tensor_scalar", dict(
                    out=G, in0=gt, scalar1=(1.0 if s == 1 else 0.5), scalar2=None, op0=ALU.mult), reads=[Bada], writes=[Bmod])
            fw.barrier()
        fw.es = es

        class Main:
            pass

        def alloc_main():
            m = Main()
            m.xt = fw.sb("xt", [128, KC, TT], F32); m.Bxt = Buf("xt")
            m.ub = fw.sb("ub", [128, KC, TT], BF16); m.Bub = Buf("ub")
            m.actb = fw.sb("actb", [128, FFC, TT], BF16); m.Bact = Buf("act")
            m.sq = fw.sb("sq", [128, TT], BF16); m.Bsq = Buf("sq")
            m.rstd = fw.sb("rstd", [128, TT], F32); m.Brstd = Buf("rstd")
            m.tmp = Rot(fw, 3, [128, TT], F32, "tmp")
            m.wsA = WSlots(fw, [trk["a0"], trk["a1"], trk["a2"], trk["a3"]], [128, KC, 256], "wsA")
            m.wsB = WSlots(fw, [trk["b0"], trk["b1"]], [128, FFC, 128], "wsB")
            return m

        def rmsnorm_mod(m, s):
            src, Bsrc = m.xt, m.Bxt
            pss, Bpss = pp.get()
            for c in range(KC):
                fw.op("act", "activation", dict(out=m.sq[:], in_=src[:, c, :], func=AF.Square),
                      reads=[Bsrc], writes=[m.Bsq])
                fw.op("pe", "matmul", MM(pss[:], lhsT=ones_b[:], rhs=m.sq[:], start=(c == 0), stop=(c == KC - 1)),
                      reads=[m.Bsq, Bconst], writes=[Bpss])
            fw.op("act", "activation", dict(out=m.rstd[:], in_=pss[:], func=AF.Sqrt, scale=1.0 / D, bias=EPS),
                  reads=[Bpss], writes=[m.Brstd])
            fw.op("dve", "reciprocal", dict(out=m.rstd[:], in_=m.rstd[:]), reads=[m.Brstd], writes=[m.Brstd])
            A = mod[:, (3 * s) * KC:(3 * s + 1) * KC]
            Bv = mod[:, (3 * s + 1) * KC:(3 * s + 2) * KC]
            for c in range(KC):
                tb, Btb = m.tmp.get()
                fw.op("dve", "scalar_tensor_tensor", dict(
                    out=tb[:], in0=src[:, c, :], scalar=A[:, c:c + 1], in1=m.rstd[:], op0=ALU.mult, op1=ALU.mult),
                    reads=[Bsrc, m.Brstd, Bmod], writes=[Btb])
                fw.op("act", "activation", dict(out=m.ub[:, c, :], in_=tb[:], func=AF.Identity,
                                                             bias=Bv[:, c:c + 1], scale=1.0),
                      reads=[Btb, Bmod], writes=[m.Bub])

        def ffn(m, wv1, wv3, wv2, s, ck=""):
            G = mod[:, (3 * s + 2) * KC:(3 * s + 3) * KC]
            for blk in range(FF // 256):
                w1t, w1bb = wload(m.wsA, ck + "w1", FF // 256, blk, wv1[:, :, blk * 256:(blk + 1) * 256])
                w3t, w3bb = wload(m.wsA, ck + "w3", FF // 256, blk, wv3[:, :, blk * 256:(blk + 1) * 256])
                for j in range(2):
                    ffc = blk * 2 + j
                    p1, Bp1 = pp.get()
                    p3, Bp3 = pp.get()
                    for kc in range(KC):
                        fw.op("pe", "matmul", MM(
                            p1[:], lhsT=w1t[:, kc, j * 128:(j + 1) * 128], rhs=m.ub[:, kc, :], start=(kc == 0), stop=(kc == KC - 1)),
                            reads=[w1bb, m.Bub], writes=[Bp1], sig=(kc == KC - 1))
                    for kc in range(KC):
                        fw.op("pe", "matmul", MM(
                            p3[:], lhsT=w3t[:, kc, j * 128:(j + 1) * 128], rhs=m.ub[:, kc, :], start=(kc == 0), stop=(kc == KC - 1)),
                            reads=[w3bb, m.Bub], writes=[Bp3], sig=(kc == KC - 1))
                    tb, Btb = m.tmp.get()
                    fw.op("act", "activation", dict(out=tb[:], in_=p1[:], func=AF.Silu),
                          reads=[Bp1], writes=[Btb])
                    fw.op("dve", "tensor_tensor", dict(
                        out=m.actb[:, ffc, :], in0=tb[:], in1=p3[:], op=ALU.mult), reads=[Btb, Bp3], writes=[m.Bact])
            for dc in range(KC):
                w2t, w2bb = wload(m.wsB, ck + "w2", KC, dc, wv2[:, :, dc * 128:(dc + 1) * 128])
                po, Bpo = pp.get()
                for fc in range(FFC):
                    fw.op("pe", "matmul", MM(
                        po[:], lhsT=w2t[:, fc, :], rhs=m.actb[:, fc, :], start=(fc == 0), stop=(fc == FFC - 1)),
                        reads=[w2bb, m.Bact], writes=[Bpo], sig=(fc == FFC - 1))
                fw.op("dve", "scalar_tensor_tensor", dict(
                    out=m.xt[:, dc, :], in0=po[:], scalar=G[:, dc:dc + 1], in1=m.xt[:, dc, :], op0=ALU.mult, op1=ALU.add),
                    reads=[Bpo, Bmod, m.Bxt], writes=[m.Bxt])

        xTv = fmv(xT)
        wv = lambda w: w.rearrange("(c p) n -> p c n", p=128)

        with ExitStack() as esA:
            fw.es = esA
            m = alloc_main()
            cw = fw.sb("cw", [128, 48, 4], F32)
            fw.dma("sp", trk["const"], cw[:], convw, writes=[Bconst])
            halo = fw.sb("halo", [128, 48, 3], F32); Bhalo = Buf("halo")
            fw.op("dve", "memset", MS(halo[:], 0.0), writes=[Bhalo])
            cbuf = Rot(fw, 2, [128, TT + 3], F32, "cbuf")
            acc = Rot(fw, 2, [128, TT], F32, "acc")
            ost = Rot(fw, 4, [128, TT], F32, "ost")
            vst = fw.sb("vst", [128, 4, D], BF16); Bvst = Buf("vst")
            bast = fw.sb("bast", [128, 4, 32], F32); Bbast = Buf("bast")
            wba = fw.sb("wba", [128, KC, 32], BF16); Bwba = Buf("wba")
            fw.dma("pool", trk["const"], wba[:], wv(winp)[:, :, WP:WP + 32], writes=[Bwba])
            stq = ["s0", "s1", "s2"]
            sti = [0]
            def store(dst, src, Bsrc, Bdst):
                fw.dma("sp", btrack(Bsrc), dst, src, reads=[Bsrc], writes=[Bdst])
            winv = wv(winp)
            fw.dma("sp", trk["x"], m.xt[:], xTv[:, :, 0:TT], writes=[m.Bxt])
            for ti in range(NT):
                t0 = ti * TT
                rmsnorm_mod(m, 0)
                ffn(m, wv(w1a), wv(w3a), wv(w2a), 0, "f1")
                pre = ti < NPRE
                if not pre:
                    store(fmv(h1T)[:, :, t0:t0 + TT], m.xt[:], m.Bxt, Bh1)
                rmsnorm_mod(m, 1)
                if ti + 1 < NT:
                    fw.dma("sp", trk["x"], m.xt[:], xTv[:, :, t0 + TT:t0 + 2 * TT], writes=[m.Bxt])
                for blk in range(WP // 256):
                    if pre and not (8 <= blk < 24 or 40 <= blk < 56 or (blk < 8 and ti == NPRE - 1)):
                        continue
                    wt, wb = wload(m.wsA, "win", WP // 256, blk, winv[:, :, blk * 256:(blk + 1) * 256])
                    if 48 <= blk < 56:
                        for tb in range(4):
                            pv, Bpv = pp.get()
                            for kc in range(KC):
                                fw.op("pe", "matmul", MM(
                                    pv[:, 0:256], lhsT=m.ub[:, kc, tb * 128:(tb + 1) * 128], rhs=wt[:, kc, :],
                                    start=(kc == 0), stop=(kc == KC - 1)),
                                    reads=[wb, m.Bub], writes=[Bpv], sig=(kc == KC - 1))
                            c0 = (blk - 48) * 256
                            fw.op("act", "activation", dict(
                                out=vst[:, tb, c0:c0 + 256], in_=pv[:, 0:256], func=AF.Copy), reads=[Bpv], writes=[Bvst])
                        continue
                    for j in range(2):
                        oc = blk * 2 + j
                        p1, Bp1 = pp.get()
                        for kc in range(KC):
                            fw.op("pe", "matmul", MM(
                                p1[:], lhsT=wt[:, kc, j * 128:(j + 1) * 128], rhs=m.ub[:, kc, :], start=(kc == 0), stop=(kc == KC - 1)),
                                reads=[wb, m.Bub], writes=[Bp1], sig=(kc == KC - 1))
                        if oc < 48:
                            cb, Bcb = cbuf.get()
                            fw.op("act", "activation", dict(out=cb[:, 0:3], in_=halo[:, oc, :], func=AF.Copy),
                                  reads=[Bhalo], writes=[Bcb])
                            fw.op("act", "activation", dict(out=cb[:, 3:TT + 3], in_=p1[:], func=AF.Copy),
                                  reads=[Bp1], writes=[Bcb])
                            fw.op("dve", "tensor_copy", dict(out=halo[:, oc, :], in_=cb[:, TT:TT + 3]),
                                  reads=[Bcb], writes=[Bhalo])
                            ac, Bac = acc.get()
                            fw.op("dve", "tensor_scalar", dict(
                                out=ac[:], in0=cb[:, 0:TT], scalar1=cw[:, oc, 0:1], scalar2=None, op0=ALU.mult),
                                reads=[Bcb, Bconst], writes=[Bac])
                            for jj in range(1, 4):
                                fw.op("dve", "scalar_tensor_tensor", dict(
                                    out=ac[:], in0=cb[:, jj:jj + TT], scalar=cw[:, oc, jj:jj + 1], in1=ac[:], op0=ALU.mult, op1=ALU.add),
                                    reads=[Bcb, Bconst, Bac], writes=[Bac])
                            o_, Bo_ = ost.get()
                            if oc >= 32:
                                fw.op("act", "activation", dict(out=o_[:], in_=ac[:], func=AF.Silu),
                                      reads=[Bac], writes=[Bo_])
                            else:
                                fw.op("act", "activation", dict(out=ac[:], in_=ac[:], func=AF.Silu),
                                      reads=[Bac], writes=[Bac])
                                fw.op("dve", "tensor_tensor", dict(out=m.sq[:], in0=ac[:], in1=ac[:], op=ALU.mult),
                                      reads=[Bac], writes=[m.Bsq])
                                p2, Bp2 = pp.get()
                                fw.op("pe", "matmul", MM(p2[:], lhsT=ones_b[:], rhs=m.sq[:], start=True, stop=True),
                                      reads=[m.Bsq, Bconst], writes=[Bp2])
                                fw.op("act", "activation", dict(out=m.rstd[:], in_=p2[:], func=AF.Sqrt, scale=1.0, bias=EPS),
                                      reads=[Bp2], writes=[m.Brstd])
                                fw.op("dve", "reciprocal", dict(out=m.rstd[:], in_=m.rstd[:]), reads=[m.Brstd], writes=[m.Brstd])
                                sc_ = (128.0 ** -0.5) if oc < 16 else 1.0
                                fw.op("dve", "scalar_tensor_tensor", dict(
                                    out=o_[:], in0=ac[:], scalar=sc_, in1=m.rstd[:], op0=ALU.mult, op1=ALU.mult),
                                    reads=[Bac, m.Brstd], writes=[Bo_])
                            store(qkvT[oc * 128:(oc + 1) * 128, t0:t0 + TT], o_[:], Bo_, Bqkv)
                        elif oc < 64:
                            o_, Bo_ = ost.get()
                            fw.op("act", "activation", dict(out=o_[:], in_=p1[:], func=AF.Silu),
                                  reads=[Bp1], writes=[Bo_])
                            r0 = (oc - 48) * 128
                            store(zT[r0:r0 + 128, t0:t0 + TT], o_[:], Bo_, Bz)
                        elif oc < 96:
                            isq = oc < 80
                            fw.op("act", "activation", dict(out=m.sq[:], in_=p1[:], func=AF.Square),
                                  reads=[Bp1], writes=[m.Bsq])
                            p2, Bp2 = pp.get()
                            fw.op("pe", "matmul", MM(p2[:], lhsT=ones_b[:], rhs=m.sq[:], start=True, stop=True),
                                  reads=[m.Bsq, Bconst], writes=[Bp2])
                            fw.op("act", "activation", dict(out=m.rstd[:], in_=p2[:], func=AF.Sqrt, scale=1.0 / 128, bias=EPS),
                                  reads=[Bp2], writes=[m.Brstd])
                            fw.op("dve", "reciprocal", dict(out=m.rstd[:], in_=m.rstd[:]), reads=[m.Brstd], writes=[m.Brstd])
                            o_, Bo_ = ost.get()
                            gcol = qgs[:, 0:1] if isq else hgt[:, 2:3]
                            fw.op("dve", "scalar_tensor_tensor", dict(
                                out=o_[:], in0=p1[:], scalar=gcol, in1=m.rstd[:], op0=ALU.mult, op1=ALU.mult),
                                reads=[Bp1, m.Brstd, Bconst], writes=[Bo_])
                            if isq:
                                r0 = (oc - 64) * 128
                                store(qmT[r0:r0 + 128, t0:t0 + TT], o_[:], Bo_, Bqm)
                            else:
                                r0 = (oc - 80) * 128
                                store(kmT[r0:r0 + 128, t0:t0 + TT], o_[:], Bo_, Bkm)
                        else:
                            o_, Bo_ = ost.get()
                            fw.op("act", "activation", dict(out=o_[:], in_=p1[:], func=AF.Sigmoid),
                                  reads=[Bp1], writes=[Bo_])
                            if oc < 128:
                                r0 = (oc - 112) * 128
                                store(gaT[r0:r0 + 128, t0:t0 + TT], o_[:], Bo_, Bga)
                            else:
                                r0 = (oc - 128) * 128
                                store(gbT[r0:r0 + 128, t0:t0 + TT], o_[:], Bo_, Bgb)
                for tb in range(4):
                    pv, Bpv = pp.get()
                    for kc in range(KC):
                        fw.op("pe", "matmul", MM(
                            pv[:, 0:32], lhsT=m.ub[:, kc, tb * 128:(tb + 1) * 128], rhs=wba[:, kc, :],
                            start=(kc == 0), stop=(kc == KC - 1)),
                            reads=[Bwba, m.Bub], writes=[Bpv], sig=(kc == KC - 1))
                    fw.op("act", "activation", dict(out=bast[:, tb, :], in_=pv[:, 0:32], func=AF.Copy),
                          reads=[Bpv], writes=[Bbast])
                store(vm.rearrange("(n p) d -> p n d", p=128)[:, ti * 4:(ti + 1) * 4, :], vst[:], Bvst, Bvm)
                store(ba.rearrange("(n p) d -> p n d", p=128)[:, ti * 4:(ti + 1) * 4, :], bast[:], Bbast, Bba)
                if ti == NPRE - 1:
                    fw.op("dve", "tensor_scalar", dict(out=halo[:], in0=halo[:], scalar1=flagv[:, 0:1], scalar2=None, op0=ALU.mult),
                          reads=[Bhalo, Bconst], writes=[Bhalo])
            fw.barrier()
        fw.es = es
        if stage == 2:
            fw.finish([Bh1, Bqkv, Bz, Bqm, Bkm, Bvm, Bba, Bga, Bgb])
            print("instructions recorded:", fw.ninst)
            fw.emit()
            return nc

        NG = NT
        NCHr = NG * 8
        TR = NT * TT
        with ExitStack() as esB:
            fw.es = esB
            tri = cload("tri", [64, 64], tri_in)
            mstrict = cload("mstrict", [64, 64], mstrict_in)
            alB = fw.sb("alB", [64, NH], F32); dtB = fw.sb("dtB", [64, NH], F32)
            fw.dma("sp", trk["const"], alB[:], alog[0:1, :].to_broadcast([64, NH]), writes=[Bconst])
            fw.dma("sp", trk["const"], dtB[:], dtb[0:1, :].to_broadcast([64, NH]), writes=[Bconst])
            ba3 = fw.sb("ba3", [64, NCH, 32], F32); Bba3 = Buf("ba3")
            fw.dma("sp", trk["l0"], ba3[:, 0:NCHr, :], ba.rearrange("(n p) c -> p n c", p=64)[:, 0:NCHr, :], reads=[Bba], writes=[Bba3])
            NC_ = NCHr * NH
            beta = fw.sb("beta", [64, NCH, NH], F32); gg = fw.sb("gg", [64, NCH, NH], F32)
            gc = fw.sb("gc", [64, NCH, NH], F32); egc = fw.sb("egc", [64, NCH, NH], F32)
            bege = fw.sb("bege", [64, NCH, NH], F32); edec = fw.sb("edec", [64, NCH, NH], F32)
            egs = fw.sb("egs", [128, NCH, NH], F32)
            Bg = Buf("gstuff")
            R_ = slice(0, NCHr)
            fw.op("act", "activation", dict(out=beta[:, R_, :], in_=ba3[:, R_, 0:16], func=AF.Sigmoid), reads=[Bba3], writes=[Bg])
            fw.op("dve", "tensor_scalar", dict(out=beta[:, 0:NPRE * 8, :], in0=beta[:, 0:NPRE * 8, :], scalar1=flagv[0:64, 0:1], scalar2=None, op0=ALU.mult),
                  reads=[Bg, Bconst], writes=[Bg])
            fw.op("dve", "tensor_tensor", dict(out=gg[:, R_, :], in0=ba3[:, R_, 16:32],
                                                   in1=dtB[:].unsqueeze(1).to_broadcast([64, NCHr, NH]), op=ALU.add),
                  reads=[Bba3, Bconst], writes=[Bg])
            fw.op("act", "activation", dict(out=gg[:, R_, :], in_=gg[:, R_, :], func=AF.Exp), reads=[Bg], writes=[Bg])
            fw.op("act", "activation", dict(out=gg[:, R_, :], in_=gg[:, R_, :], func=AF.Ln, bias=1.0, scale=1.0), reads=[Bg], writes=[Bg])
            fw.op("act", "activation", dict(out=alB[:], in_=alB[:], func=AF.Exp), reads=[Bconst], writes=[Bconst])
            fw.op("dve", "scalar_tensor_tensor", dict(out=gg[:, R_, :], in0=gg[:, R_, :], scalar=-1.0,
                                                          in1=alB[:].unsqueeze(1).to_broadcast([64, NCHr, NH]),
                                                          op0=ALU.mult, op1=ALU.mult), reads=[Bg, Bconst], writes=[Bg])
            ggf = gg[:].rearrange("p n h -> p (n h)"); gcf = gc[:].rearrange("p n h -> p (n h)")
            egsf = egs[:].rearrange("p n h -> p (n h)")
            for cc in range(0, NC_, 512):
                w_ = min(512, NC_ - cc)
                p1, Bp1 = pp.get()
                fw.op("pe", "matmul", MM(p1[0:64, 0:w_], lhsT=tri[:], rhs=ggf[:, cc:cc + w_], start=True, stop=True),
                      reads=[Bg, Bconst], writes=[Bp1])
                fw.op("dve", "tensor_copy", dict(out=gcf[:, cc:cc + w_], in_=p1[0:64, 0:w_]), reads=[Bp1], writes=[Bg])
                p2, Bp2 = pp.get()
                fw.op("pe", "matmul", MM(p2[:, 0:w_], lhsT=ones_f[0:64, :], rhs=ggf[:, cc:cc + w_], start=True, stop=True),
                      reads=[Bg, Bconst], writes=[Bp2])
                fw.op("dve", "tensor_copy", dict(out=egsf[:, cc:cc + w_], in_=p2[:, 0:w_]), reads=[Bp2], writes=[Bg])
            fw.op("dve", "tensor_tensor", dict(out=edec[:, R_, :], in0=egs[0:64, R_, :], in1=gc[:, R_, :], op=ALU.subtract), reads=[Bg], writes=[Bg])
            fw.op("act", "activation", dict(out=edec[:, R_, :], in_=edec[:, R_, :], func=AF.Exp), reads=[Bg], writes=[Bg])
            fw.op("act", "activation", dict(out=egs[:, R_, :], in_=egs[:, R_, :], func=AF.Exp), reads=[Bg], writes=[Bg])
            fw.op("act", "activation", dict(out=egc[:, R_, :], in_=gc[:, R_, :], func=AF.Exp), reads=[Bg], writes=[Bg])
            fw.op("dve", "tensor_tensor", dict(out=bege[:, R_, :], in0=beta[:, R_, :], in1=egc[:, R_, :], op=ALU.mult), reads=[Bg], writes=[Bg])

            kTh = fw.sb("kTh", [128, T], F32); qTh = fw.sb("qTh", [128, T], F32); vTh = fw.sb("vTh", [128, T], F32)
            zTh = fw.sb("zTh", [128, T], F32); oTh = fw.sb("oTh", [128, T], F32)
            Bk, Bq, Bv_, Bzh, Bo = Buf("kTh"), Buf("qTh"), Buf("vTh"), Buf("zTh"), Buf("oTh")
            S = fw.sb("S", [128, 128], F32); BS = Buf("S")
            kbe = fw.sb("kbe", [64, 8, 128], F32); kdec = fw.sb("kdec", [64, 8, 128], F32); vb = fw.sb("vb", [64, 8, 128], F32)
            Bkbe, Bkdec, Bvb = Buf("kbe"), Buf("kdec"), Buf("vb")
            Gbc = fw.sb("Gbc", [64, 8, 128], F32); BGbc = Buf("Gbc")
            def g64(name):
                return fw.sb(name, [64, 8, 64], F32), Buf(name)
            d1, Bd1 = g64("d1"); decA, BdecA = g64("decA"); decT, BdecT = g64("decT")
            Am, BAm = g64("Am"); ATm, BATm = g64("ATm"); QT, BQT = g64("QT"); PT, BPT = g64("PT")
            M2a, BM2a = g64("M2a"); MT2a, BMT2a = g64("MT2a"); M2b, BM2b = g64("M2b"); MT2b, BMT2b = g64("MT2b")
            egcB = fw.sb("egcB", [128, 512], F32); BegcB = Buf("egcB")
            qd = fw.sb("qd", [128, 512], F32); Bqd = Buf("qd")
            U = fw.sb("U", [64, 8, 128], F32); BU = Buf("U")
            WT = fw.sb("WT", [128, 8, 64], F32); BWT = Buf("WT")
            vnr = Rot(fw, 2, [64, 128], F32, "vn")
            U_b = fw.sb("U_b", [64, 8, 128], F32); BU_b = Buf("U_b")
            WT_b = fw.sb("WT_b", [128, 8, 64], F32); BWT_b = Buf("WT_b")
            PT_b, BPT_b = g64("PT_b")
            qd_b = fw.sb("qd_b", [128, 512], F32); Bqd_b = Buf("qd_b")
            kdec_b = fw.sb("kdec_b", [64, 8, 128], F32); Bkdec_b = Buf("kdec_b")
            sqd = fw.sb("sqd", [128, 512], F32); Bsqd = Buf("sqd")
            rsd = fw.sb("rsd", [128, 512], F32); Brsd = Buf("rsd")
            yst = Rot(fw, 2, [128, 512], BF16, "yst")
            fl = lambda t: t[:].rearrange("p c i -> p (c i)")
            for hh in range(NH):
                fw.dma("sp", trk["l1"], qTh[:, 0:TR], qkvT[hh * 128:(hh + 1) * 128, 0:TR], reads=[Bqkv], writes=[Bq])
                fw.dma("sp", trk["l2"], kTh[:, 0:TR], qkvT[(16 + hh) * 128:(17 + hh) * 128, 0:TR], reads=[Bqkv], writes=[Bk])
                fw.dma("sp", trk["l3"], vTh[:, 0:TR], qkvT[(32 + hh) * 128:(33 + hh) * 128, 0:TR], reads=[Bqkv], writes=[Bv_])
                fw.dma("sp", trk["l4"], zTh[:, 0:TR], zT[hh * 128:(hh + 1) * 128, 0:TR], reads=[Bz], writes=[Bzh])
                fw.op("dve", "memset", MS(S[:], 0.0), writes=[BS])
                def prep(gi, RB):
                    U, BU, WT, BWT, PT, BPT, qd, Bqd, kdec, Bkdec = RB
                    n0 = gi * 8
                    t0 = gi * TT
                    opx = (lambda *a_, **k_: None) if gi < NPRE else fw.op
                    bc8 = lambda arr, w: arr[:, n0:n0 + 8, hh:hh + 1].to_broadcast([64, 8, w])
                    bc4 = lambda arr, a, w: arr[:, n0 + a:n0 + a + 4, hh:hh + 1].to_broadcast([64, 4, w])
                    for half in range(2):
                        pk, Bpk = pp.get()
                        pv, Bpv = pp.get()
                        for c in range(4):
                            cc = half * 4 + c
                            ts_ = slice(t0 + cc * 64, t0 + cc * 64 + 64)
                            fw.op("pe", "transpose", TRP(pk[0:64, c * 128:(c + 1) * 128], kTh[:, ts_], ident[:]),
                                  reads=[Bk, Bconst], writes=[Bpk], sig=(c == 3))
                            fw.op("pe", "transpose", TRP(pv[0:64, c * 128:(c + 1) * 128], vTh[:, ts_], ident[:]),
                                  reads=[Bv_, Bconst], writes=[Bpv], sig=(c == 3))
                        a = half * 4
                        pk3 = pk[0:64, :].rearrange("p (c d) -> p c d", c=4)
                        pv3 = pv[0:64, :].rearrange("p (c d) -> p c d", c=4)
                        fw.op("dve", "tensor_tensor", dict(out=kbe[:, a:a + 4, :], in0=pk3, in1=bc4(bege, a, 128), op=ALU.mult),
                              reads=[Bpk, Bg], writes=[Bkbe])
                        fw.op("dve", "tensor_tensor", dict(out=kdec[:, a:a + 4, :], in0=pk3, in1=bc4(edec, a, 128), op=ALU.mult),
                              reads=[Bpk, Bg], writes=[Bkdec])
                        fw.op("dve", "tensor_tensor", dict(out=vb[:, a:a + 4, :], in0=pv3, in1=bc4(beta, a, 128), op=ALU.mult),
                              reads=[Bpv, Bg], writes=[Bvb])
                        yield
                    fw.op("dve", "tensor_copy", dict(out=Gbc[:], in_=bc8(gg, 128)), reads=[Bg], writes=[BGbc])
                    pG, BpG = pp.get(); pKQ, BpKQ = pp.get(); pgB, BpgB = pp.get()
                    for c in range(8):
                        ts_ = slice(t0 + c * 64, t0 + c * 64 + 64)
                        cs_ = slice(c * 64, c * 64 + 64)
                        fw.op("pe", "matmul", MM(pG[0:64, cs_], lhsT=kTh[:, ts_], rhs=kTh[:, ts_], start=True, stop=True),
                              reads=[Bk], writes=[BpG], sig=(c == 7))
                        opx("pe", "matmul", MM(pKQ[0:64, cs_], lhsT=kTh[:, ts_], rhs=qTh[:, ts_], start=True, stop=True),
                              reads=[Bk, Bq], writes=[BpKQ], sig=(c == 7))
                        fw.op("pe", "matmul", MM(pgB[:, cs_], lhsT=Gbc[:, c, :], rhs=tri[:], start=True, stop=True),
                              reads=[BGbc, Bconst], writes=[BpgB], sig=(c == 7))
                    yield
                    pgB3 = pgB[0:64, :].rearrange("p (c i) -> p c i", c=8)
                    pG3 = pG[0:64, :].rearrange("p (c i) -> p c i", c=8)
                    pKQ3 = pKQ[0:64, :].rearrange("p (c i) -> p c i", c=8)
                    fw.op("dve", "tensor_tensor", dict(out=d1[:], in0=pgB3, in1=bc8(gc, 64), op=ALU.subtract), reads=[BpgB, Bg], writes=[Bd1])
                    fw.op("dve", "tensor_scalar", dict(out=decA[:], in0=d1[:], scalar1=0.0, scalar2=None, op0=ALU.max), reads=[Bd1], writes=[BdecA])
                    fw.op("act", "activation", dict(out=decA[:], in_=decA[:], func=AF.Exp, scale=-1.0), reads=[BdecA], writes=[BdecA])
                    fw.op("dve", "tensor_tensor", dict(out=decA[:], in0=decA[:], in1=mstrict[:].unsqueeze(1).to_broadcast([64, 8, 64]), op=ALU.mult),
                          reads=[BdecA, Bconst], writes=[BdecA])
                    opx("dve", "tensor_scalar", dict(out=decT[:], in0=d1[:], scalar1=0.0, scalar2=None, op0=ALU.min), reads=[Bd1], writes=[BdecT])
                    opx("act", "activation", dict(out=decT[:], in_=decT[:], func=AF.Exp), reads=[BdecT], writes=[BdecT])
                    opx("dve", "tensor_tensor", dict(out=decT[:], in0=decT[:], in1=tri[:].unsqueeze(1).to_broadcast([64, 8, 64]), op=ALU.mult),
                          reads=[BdecT, Bconst], writes=[BdecT])
                    fw.op("dve", "tensor_tensor", dict(out=Am[:], in0=pG3, in1=bc8(beta, 64), op=ALU.mult), reads=[BpG, Bg], writes=[BAm])
                    fw.op("dve", "tensor_tensor", dict(out=Am[:], in0=Am[:], in1=decA[:], op=ALU.mult), reads=[BAm, BdecA], writes=[BAm])
                    opx("dve", "tensor_tensor", dict(out=PT[:], in0=pKQ3, in1=decT[:], op=ALU.mult), reads=[BpKQ, BdecT], writes=[BPT])
                    opx("act", "activation", dict(out=egcB[:], in_=pgB[:], func=AF.Exp), reads=[BpgB], writes=[BegcB])
                    opx("dve", "tensor_tensor", dict(out=qd[:], in0=qTh[:, t0:t0 + TT], in1=egcB[:], op=ALU.mult),
                          reads=[Bq, BegcB], writes=[Bqd])
                    yield
                    pT_, BpT_ = pp.get()
                    for c in range(8):
                        cs_ = slice(c * 64, c * 64 + 64)
                        fw.op("pe", "transpose", TRP(pT_[0:64, cs_], Am[:, c, :], ident[0:64, 0:64]),
                              reads=[BAm, Bconst], writes=[BpT_], sig=(c == 7))
                    fw.op("dve", "tensor_copy", dict(out=fl(ATm), in_=pT_[0:64, :]), reads=[BpT_], writes=[BATm])
                    fw.op("dve", "tensor_tensor", dict(out=QT[:], in0=ident[0:64, 0:64].unsqueeze(1).to_broadcast([64, 8, 64]), in1=ATm[:], op=ALU.subtract),
                          reads=[BATm, Bconst], writes=[BQT])
                    yield
                    Mc, BMc, MTc, BMTc = Am, BAm, ATm, BATm
                    pingpong = [(M2a, BM2a, MT2a, BMT2a), (M2b, BM2b, MT2b, BMT2b)]
                    for k in range(5):
                        Mn, BMn, MTn, BMTn = pingpong[k % 2]
                        pM, BpM = pp.get()
                        for c in range(8):
                            cs_ = slice(c * 64, c * 64 + 64)
                            fw.op("pe", "matmul", MM(pM[0:64, cs_], lhsT=MTc[:, c, :], rhs=Mc[:, c, :], start=True, stop=True),
                                  reads=[BMc, BMTc], writes=[BpM], sig=(c == 7))
                        fw.op("dve", "tensor_copy", dict(out=fl(Mn), in_=pM[0:64, :]), reads=[BpM], writes=[BMn])
                        yield
                        if k < 4:
                            pMT, BpMT = pp.get()
                            for c in range(8):
                                cs_ = slice(c * 64, c * 64 + 64)
                                fw.op("pe", "matmul", MM(pMT[0:64, cs_], lhsT=Mc[:, c, :], rhs=MTc[:, c, :], start=True, stop=True),
                                      reads=[BMc, BMTc], writes=[BpMT], sig=(c == 7))
                            fw.op("act", "activation", dict(out=fl(MTn), in_=pMT[0:64, :], func=AF.Copy), reads=[BpMT], writes=[BMTn])
                            yield
                        pQ, BpQ = pp.get()
                        for c in range(8):
                            cs_ = slice(c * 64, c * 64 + 64)
                            fw.op("pe", "matmul", MM(pQ[0:64, cs_], lhsT=Mn[:, c, :], rhs=QT[:, c, :], start=True, stop=True),
                                  reads=[BMn, BQT], writes=[BpQ], sig=(c == 7))
                        fw.op("dve", "tensor_tensor", dict(out=fl(QT), in0=fl(QT), in1=pQ[0:64, :], op=ALU.add), reads=[BpQ, BQT], writes=[BQT])
                        yield
                        Mc, BMc, MTc, BMTc = Mn, BMn, MTn, BMTn
                    for half in range(2):
                        pU, BpU = pp.get()
                        for c in range(4):
                            cc = half * 4 + c
                            fw.op("pe", "matmul", MM(pU[0:64, c * 128:(c + 1) * 128], lhsT=QT[:, cc, :], rhs=vb[:, cc, :], start=True, stop=True),
                                  reads=[BQT, Bvb], writes=[BpU], sig=(c == 3))
                        fw.op("dve", "tensor_copy", dict(out=U[:, half * 4:half * 4 + 4, :].rearrange("p c e -> p (c e)"), in_=pU[0:64, :]),
                              reads=[BpU], writes=[BU])
                        yield
                    pW, BpW = pp.get()
                    for c in range(8):
                        cs_ = slice(c * 64, c * 64 + 64)
                        fw.op("pe", "matmul", MM(pW[:, cs_], lhsT=kbe[:, c, :], rhs=QT[:, c, :], start=True, stop=True),
                              reads=[Bkbe, BQT], writes=[BpW], sig=(c == 7))
                    fw.op("act", "activation", dict(out=WT[:].rearrange("p c i -> p (c i)"), in_=pW[:], func=AF.Copy), reads=[BpW], writes=[BWT])
                    yield
                def rec(gi, RB, gen):
                    U, BU, WT, BWT, PT, BPT, qd, Bqd, kdec, Bkdec = RB
                    n0 = gi * 8
                    t0 = gi * TT
                    opx = (lambda *a_, **k_: None) if gi < NPRE else fw.op
                    def filler(k):
                        if gen is not None:
                            for _ in range(k):
                                next(gen, None)
                    for c in range(8):
                        n = n0 + c
                        cs_ = slice(c * 64, c * 64 + 64)
                        pWS, BpWS = pp.get()
                        fw.op("pe", "matmul", MM(pWS[0:64, 0:128], lhsT=WT[:, c, :], rhs=S[:], start=True, stop=True),
                              reads=[BWT, BS], writes=[BpWS])
                        vn, Bvn = vnr.get()
                        fw.op("dve", "tensor_tensor", dict(out=vn[:], in0=U[:, c, :], in1=pWS[0:64, 0:128], op=ALU.subtract),
                              reads=[BU, BpWS], writes=[Bvn])
                        filler(2)
                        pO, BpO = pp.get()
                        opx("pe", "matmul", MM(pO[:, 0:64], lhsT=S[:], rhs=qd[:, cs_], start=True, stop=False),
                              reads=[BS, Bqd], writes=[BpO], sig=False)
                        opx("pe", "matmul", MM(pO[:, 0:64], lhsT=vn[:], rhs=PT[:, c, :], start=False, stop=True),
                              reads=[Bvn, BPT], writes=[BpO])
                        pS, BpS = pp.get()
                        fw.op("pe", "matmul", MM(pS[:, 0:128], lhsT=kdec[:, c, :], rhs=vn[:], start=True, stop=True),
                              reads=[Bkdec, Bvn], writes=[BpS])
                        fw.op("dve", "scalar_tensor_tensor", dict(out=S[:], in0=S[:], scalar=egs[:, n, hh:hh + 1], in1=pS[:, 0:128],
                                                                                op0=ALU.mult, op1=ALU.add),
                              reads=[BS, BpS, Bg], writes=[BS])
                        filler(2)
                        opx("act", "activation", dict(out=oTh[:, t0 + c * 64:t0 + c * 64 + 64], in_=pO[:, 0:64], func=AF.Copy),
                              reads=[BpO], writes=[Bo])
                RBs = [(U, BU, WT, BWT, PT, BPT, qd, Bqd, kdec, Bkdec), (U_b, BU_b, WT_b, BWT_b, PT_b, BPT_b, qd_b, Bqd_b, kdec_b, Bkdec_b)]
                for _ in prep(0, RBs[0]):
                    pass
                for gi in range(NG):
                    gen = prep(gi + 1, RBs[(gi + 1) % 2]) if gi + 1 < NG else None
                    rec(gi, RBs[gi % 2], gen)
                    if gen is not None:
                        for _ in gen:
                            pass
                for gi in range(NPRE, NG):
                    t0 = gi * TT
                    fw.op("dve", "tensor_tensor", dict(out=sqd[:], in0=oTh[:, t0:t0 + TT], in1=oTh[:, t0:t0 + TT], op=ALU.mult),
                          reads=[Bo], writes=[Bsqd])
                    p2, Bp2 = pp.get()
                    fw.op("pe", "matmul", MM(p2[:], lhsT=ones_f[:], rhs=sqd[:], start=True, stop=True), reads=[Bsqd, Bconst], writes=[Bp2])
                    fw.op("act", "activation", dict(out=rsd[:], in_=p2[:], func=AF.Sqrt, scale=1.0 / 128, bias=EPS), reads=[Bp2], writes=[Brsd])
                    fw.op("dve", "reciprocal", dict(out=rsd[:], in_=rsd[:]), reads=[Brsd], writes=[Brsd])
                    fw.op("dve", "scalar_tensor_tensor", dict(out=sqd[:], in0=oTh[:, t0:t0 + TT], scalar=hgt[:, 0:1], in1=rsd[:], op0=ALU.mult, op1=ALU.mult),
                          reads=[Bo, Brsd, Bconst], writes=[Bsqd])
                    ys, Bys = yst.get()
                    fw.op("dve", "tensor_tensor", dict(out=ys[:], in0=sqd[:], in1=zTh[:, t0:t0 + TT], op=ALU.mult),
                          reads=[Bsqd, Bzh], writes=[Bys])
                    fw.dma("sp", btrack(Bys), yaT[hh * 128:(hh + 1) * 128, t0:t0 + TT], ys[:], reads=[Bys], writes=[Bya])
            fw.barrier()
        fw.es = es
        if stage == 3:
            fw.finish([Bya])
            print("instructions recorded:", fw.ninst)
            fw.emit()
            return nc

        with ExitStack() as esC:
            fw.es = esC
            rb = cload("rb", [32, NH], relb)
            ngs = Rot(fw, 2, [1, 512], F32, "ngs")
            pastneg = cload("pastneg", [128, 16, 16], pastneg_in)
            pastflag = cload("pastflag", [128, 16, 16], pastflag_in)
            extram = cload("extram", [128, 16, 16], extram_in)
            esel = cload("esel", [16, 16, 128], esel_in, BF16, "pool")
            ohs = Rot(fw, 2, [32, 512], F32, "ohs")
            est = Rot(fw, 2, [16, 512], F32, "est")
            for cc in range(0, LE, 512):
                w_ = min(512, LE - cc)
                oh_, Boh = ohs.get()
                fw.dma("sp", btrack(Boh), oh_[:, 0:w_], oh_in[:, cc:cc + w_], writes=[Boh])
                pe_, Bpe_ = pp.get()
                fw.op("pe", "matmul", MM(pe_[0:16, 0:w_], lhsT=rb[:], rhs=oh_[:, 0:w_], start=True, stop=False),
                      reads=[Boh, Bconst], writes=[Bpe_], sig=False)
                ng_, Bng_ = ngs.get()
                fw.dma("sp", btrack(Bng_), ng_[:, 0:w_], negm_in[:, cc:cc + w_], writes=[Bng_])
                fw.op("pe", "matmul", MM(pe_[0:16, 0:w_], lhsT=ones_f[0:1, 0:16], rhs=ng_[:, 0:w_], start=False, stop=True),
                      reads=[Bconst, Bng_], writes=[Bpe_])
                e_, Be_ = est.get()
                fw.op("act", "activation", dict(out=e_[:, 0:w_], in_=pe_[0:16, 0:w_], func=AF.Copy), reads=[Bpe_], writes=[Be_])
                fw.dma("sp", btrack(Be_), E_d[:, cc:cc + w_], e_[:, 0:w_], reads=[Be_], writes=[BE])
            qf = fw.sb("qf", [128, T], F32); kf = fw.sb("kf", [128, T], F32)
            qb = fw.sb("qb", [128, T], BF16); kb = fw.sb("kb", [128, T], BF16)
            vmh = fw.sb("vmh", [128, T // 128, 128], BF16)
            Bqf, Bkf, Bqb, Bkb, Bvmh = Buf("qf"), Buf("kf"), Buf("qb"), Buf("kb"), Buf("vmh")
            qf2 = fw.sb("qf2", [128, T], F32); kf2 = fw.sb("kf2", [128, T], F32)
            vmh2 = fw.sb("vmh2", [128, T // 128, 128], BF16)
            Bqf2, Bkf2, Bvmh2 = Buf("qf2"), Buf("kf2"), Buf("vmh2")
            hank2 = fw.sb("hank2", [128, WSH], F32); Bhank2 = Buf("hank2")
            tsh = fw.sb("tsh", [128, WSH], F32); Btsh = Buf("tsh")
            hank = fw.sb("hank", [128, WSH], F32); Bhank = Buf("hank")
            jrev = cload("jrev", [128, 128], jrev_in)
            kmean = fw.sb("kmean", [128, 16], F32); Bkmean = Buf("kmean")
            gm = fw.sb("gm", [128, 16], F32); top8 = fw.sb("top8", [128, 8], F32); mv = fw.sb("mv", [128, 16], F32)
            Bgm, Btop8, Bmv = Buf("gm"), Buf("top8"), Buf("mv")
            mvT = fw.sb("mvT", [16, T], BF16); BmvT = Buf("mvT")
            gmA = fw.sb("gmA", [128, 16, 16], F32); BgmA = Buf("gmA")
            top8A = fw.sb("top8A", [128, 16, 8], F32); Btop8A = [Buf(f"top8A{i}") for i in range(4)]
            mvA = fw.sb("mvA", [128, 16, 16], F32); BmvA = Buf("mvA")
            ssb = Rot(fw, 3, [128, 512], F32, "ssb")
            ptb = Rot(fw, 4, [128, 512], BF16, "ptb")
            rsm = fw.sb("rsm", [128, 512], F32); Brsm = Buf("rsm")
            ybs = Rot(fw, 2, [128, 512], BF16, "ybs")
            NQB = TR // 128
            po_t = fw.ps("po_t", [128, 512]); Bpo_t = Buf("po_t")
            psm_t = fw.ps("psm_t", [128, 512]); Bpsm_t = Buf("psm_t")
            pp.n = 4; pp.i = 0
            accs = [(po_t, Bpo_t, psm_t, Bpsm_t), (pp.t[4], pp.b[4], pp.t[5], pp.b[5])]
            hsets = [(qf, Bqf, kf, Bkf, vmh, Bvmh, hank, Bhank), (qf2, Bqf2, kf2, Bkf2, vmh2, Bvmh2, hank2, Bhank2)]
            def hloads(hx):
                qf_, Bqf_, kf_, Bkf_, vmh_, Bvmh_, hank_, Bhank_ = hsets[hx % 2]
                fw.dma("sp", btrack(Bqf_), qf_[:, 0:TR], qmT[hx * 128:(hx + 1) * 128, 0:TR], reads=[Bqm], writes=[Bqf_])
                fw.dma("sp", btrack(Bkf_), kf_[:, 0:TR], kmT[hx * 128:(hx + 1) * 128, 0:TR], reads=[Bkm], writes=[Bkf_])
                fw.dma("sp", btrack(Bvmh_), vmh_[:, 0:NQB, :], vm.rearrange("(n p) d -> p n d", p=128)[:, 0:NQB, hx * 128:(hx + 1) * 128],
                       reads=[Bvm], writes=[Bvmh_])
                tsrc = bass.AP(E_d.tensor, hx * LE, [[1, 128], [1, WSH]])
                fw.dma("sp", btrack(Bhank_), hank_[:], tsrc, reads=[BE], writes=[Bhank_])
            hloads(0)
            for hh in range(NH):
                qf, Bqf, kf, Bkf, vmh, Bvmh, hank, Bhank = hsets[hh % 2]
                if hh + 1 < NH:
                    hloads(hh + 1)
                for cc in range(0, WSH, 512):
                    w_ = min(512, WSH - cc)
                    pj, Bpj = pp.get()
                    fw.op("pe", "matmul", MM(pj[:, 0:w_], lhsT=jrev[:], rhs=hank[:, cc:cc + w_], start=True, stop=True),
                          reads=[Bhank, Bconst], writes=[Bpj])
                    fw.op("act", "activation", dict(out=tsh[:, cc:cc + w_], in_=pj[:, 0:w_], func=AF.Copy), reads=[Bpj], writes=[Btsh])
                fw.op("dve", "tensor_copy", dict(out=qb[:, 0:TR], in_=qf[:, 0:TR]), reads=[Bqf], writes=[Bqb])
                fw.op("act", "activation", dict(out=kb[:, 0:TR], in_=kf[:, 0:TR], func=AF.Copy), reads=[Bkf], writes=[Bkb])
                nblk = TR // 256
                fw.op("dve", "memset", MS(kmean[:], 0.0), writes=[Bkmean])
                fw.op("dve", "tensor_reduce", dict(out=kmean[:, 0:nblk], in_=kf[:, 0:TR].rearrange("p (n k) -> p n k", k=256),
                                                                 axis=AX.X, op=ALU.add), reads=[Bkf], writes=[Bkmean])
                fw.op("dve", "tensor_scalar", dict(out=kmean[:], in0=kmean[:], scalar1=1.0 / 256, scalar2=None, op0=ALU.mult),
                      reads=[Bkmean], writes=[Bkmean])
                QB0 = NPRE * 4
                NQ = NQB - QB0
                pg, Bpg = pp.get()
                for qi in range(NQ):
                    qbk = QB0 + qi
                    fw.op("pe", "matmul", MM(pg[:, qi * 16:(qi + 1) * 16], lhsT=qf[:, qbk * 128:(qbk + 1) * 128], rhs=kmean[:], start=True, stop=True),
                          reads=[Bqf, Bkmean], writes=[Bpg], sig=(qi == NQ - 1))
                def pairb(t):
                    return t[:, NPRE * 2:NPRE * 2 + NQ // 2, :].unsqueeze(2).to_broadcast([128, NQ // 2, 2, 16])
                v4 = lambda t: t[:, 0:NQ, :].rearrange("p (a two) j -> p a two j", two=2)
                fw.op("dve", "tensor_tensor", dict(out=v4(gmA), in0=pg[:, 0:NQ * 16].rearrange("p (a two j) -> p a two j", two=2, j=16),
                                                   in1=pairb(pastneg), op=ALU.add), reads=[Bpg, Bconst], writes=[BgmA])
                for qi in range(NQ):
                    fw.op("dve", "max", dict(out=top8A[:, qi, :], in_=gmA[:, qi, :]), reads=[BgmA], writes=[Btop8A[qi % 4]])
                fw.op("dve", "tensor_tensor", dict(out=mvA[:, 0:NQ, :], in0=gmA[:, 0:NQ, :], in1=top8A[:, 0:NQ, 2:3].to_broadcast([128, NQ, 16]),
                                                   op=ALU.is_ge), reads=[BgmA] + Btop8A, writes=[BmvA])
                fw.op("dve", "tensor_scalar", dict(out=mvA[:, 0:NQ, :], in0=mvA[:, 0:NQ, :], scalar1=-NEG, scalar2=NEG, op0=ALU.mult, op1=ALU.add),
                      reads=[BmvA], writes=[BmvA])
                fw.op("dve", "tensor_tensor", dict(out=v4(mvA), in0=v4(mvA), in1=pairb(pastflag), op=ALU.mult), reads=[BmvA, Bconst], writes=[BmvA])
                fw.op("dve", "tensor_tensor", dict(out=v4(mvA), in0=v4(mvA), in1=pairb(extram), op=ALU.add), reads=[BmvA, Bconst], writes=[BmvA])
                for g4 in range(0, NQ, 4):
                    pt_, Bpt_ = pp.get()
                    for c in range(4):
                        fw.op("pe", "transpose", TRP(pt_[0:16, c * 128:(c + 1) * 128], mvA[:, g4 + c, :], ident[:]),
                              reads=[BmvA, Bconst], writes=[Bpt_], sig=(c == 3))
                    fw.op("act", "activation", dict(out=mvT[:, (QB0 + g4) * 128:(QB0 + g4 + 4) * 128], in_=pt_[0:16, :], func=AF.Copy),
                          reads=[Bpt_], writes=[BmvT])
                for qt in range(NPRE, NT):
                    q0 = qt * TT
                    po, Bpo, psm, Bpsm = accs[qt % 2]
                    nkt = 4 * qt + 4
                    LAG = 2
                    pend = {}
                    for kk in range(nkt + LAG):
                        if kk < nkt:
                            kt = kk
                            k0 = kt * 128
                            dl = q0 - k0
                            sp_, Bsp_ = pp.get()
                            fw.op("pe", "matmul", MM(sp_[:], lhsT=kb[:, k0:k0 + 128], rhs=qb[:, q0:q0 + TT], start=True, stop=False),
                                  reads=[Bkb, Bqb], writes=[Bsp_], sig=False)
                            fw.op("pe", "matmul", MM(sp_[:], lhsT=esel[:, kt // 2, :], rhs=mvT[:, q0:q0 + TT], start=False, stop=True),
                                  reads=[BmvT, Bconst], writes=[Bsp_])
                            pb_, Bpb_ = ptb.get()
                            if dl >= 1024:
                                fw.op("act", "activation", dict(out=pb_[:], in_=sp_[:], func=AF.Exp, bias=tsh[:, WSH - 1:WSH], scale=1.0),
                                      reads=[Bsp_, Btsh], writes=[Bpb_])
                            else:
                                sb_, Bsb_ = ssb.get()
                                fw.op("dve", "tensor_tensor", dict(out=sb_[:], in0=sp_[:], in1=tsh[:, 513 + dl:513 + dl + TT], op=ALU.add),
                                      reads=[Bsp_, Btsh], writes=[Bsb_])
                                fw.op("act", "activation", dict(out=pb_[:], in_=sb_[:], func=AF.Exp), reads=[Bsb_], writes=[Bpb_])
                            pend[kt] = (pb_, Bpb_)
                        if kk >= LAG:
                            kt = kk - LAG
                            pb_, Bpb_ = pend.pop(kt)
                            fw.op("pe", "matmul", MM(po[:], lhsT=vmh[:, kt, :], rhs=pb_[:], start=(kt == 0), stop=(kt == nkt - 1)),
                                  reads=[Bvmh, Bpb_], writes=[Bpo], sig=False)
                            fw.op("pe", "matmul", MM(psm[:], lhsT=ones_b[:], rhs=pb_[:], start=(kt == 0), stop=(kt == nkt - 1)),
                                  reads=[Bconst, Bpb_], writes=[Bpsm, Bpo])
                    fw.op("dve", "reciprocal", dict(out=rsm[:], in_=psm[:]), reads=[Bpsm], writes=[Brsm])
                    yb_, Byb_ = ybs.get()
                    fw.op("dve", "tensor_tensor", dict(out=yb_[:], in0=po[:], in1=rsm[:], op=ALU.mult), reads=[Bpo, Brsm], writes=[Byb_])
                    fw.dma("sp", btrack(Byb_), ybT[hh * 128:(hh + 1) * 128, q0:q0 + TT], yb_[:], reads=[Byb_], writes=[Byb])
            pp.n = 6
            fw.barrier()
        fw.es = es
        if stage == 4:
            fw.finish([Byb, BE])
            print("instructions recorded:", fw.ninst)
            fw.emit()
            return nc

        with ExitStack() as esD:
            fw.es = esD
            m = alloc_main()
            gst = Rot(fw, 4, [128, TT], F32, "gst")
            m1r = Rot(fw, 2, [128, TT], F32, "m1r")
            for ti in range(NPRE, NT):
                t0 = ti * TT
                fw.dma("sp", trk["x"], m.xt[:], fmv(h1T)[:, :, t0:t0 + TT], reads=[Bh1], writes=[m.Bxt])
                fw.dma("sp", trk["l0"], m.actb[:, 0:16, :], fmv(yaT)[:, :, t0:t0 + TT], reads=[Bya], writes=[m.Bact])
                fw.dma("sp", trk["l1"], m.actb[:, 16:32, :], fmv(ybT)[:, :, t0:t0 + TT], reads=[Byb], writes=[m.Bact])
                for blk in range(D // 256):
                    wat, wab = wload(m.wsA, "wpa", D // 256, blk, wv(wpa)[:, :, blk * 256:(blk + 1) * 256])
                    wbt, wbb = wload(m.wsA, "wpb", D // 256, blk, wv(wpb)[:, :, blk * 256:(blk + 1) * 256])
                    for j in range(2):
                        dc = blk * 2 + j
                        pa, Bpa = pp.get(); pb, Bpb = pp.get()
                        for kc in range(KC):
                            fw.op("pe", "matmul", MM(
                                pa[:], lhsT=wat[:, kc, j * 128:(j + 1) * 128], rhs=m.actb[:, kc, :], start=(kc == 0), stop=(kc == KC - 1)),
                                reads=[wab, m.Bact], writes=[Bpa], sig=(kc == KC - 1))
                        for kc in range(KC):
                            fw.op("pe", "matmul", MM(
                                pb[:], lhsT=wbt[:, kc, j * 128:(j + 1) * 128], rhs=m.actb[:, 16 + kc, :], start=(kc == 0), stop=(kc == KC - 1)),
                                reads=[wbb, m.Bact], writes=[Bpb], sig=(kc == KC - 1))
                        ga_, Bga_ = gst.get(); gb_, Bgb_ = gst.get()
                        fw.dma("sp", btrack(Bga_), ga_[:], gaT[dc * 128:(dc + 1) * 128, t0:t0 + TT], reads=[Bga], writes=[Bga_])
                        fw.dma("sp", btrack(Bgb_), gb_[:], gbT[dc * 128:(dc + 1) * 128, t0:t0 + TT], reads=[Bgb], writes=[Bgb_])
                        m1, Bm1 = m1r.get()
                        fw.op("dve", "tensor_tensor", dict(out=m1[:], in0=pa[:], in1=ga_[:], op=ALU.mult),
                              reads=[Bpa, Bga_], writes=[Bm1])
                        fw.op("dve", "tensor_tensor", dict(out=gb_[:], in0=pb[:], in1=gb_[:], op=ALU.mult),
                              reads=[Bpb, Bgb_], writes=[Bgb_])
                        fw.op("dve", "tensor_tensor", dict(out=m.ub[:, dc, :], in0=gb_[:], in1=m1[:], op=ALU.add),
                              reads=[Bgb_, Bm1], writes=[m.Bub])
                G2 = mod[:, 5 * KC:6 * KC]
                for blk in range(D // 256):
                    wot, wob = wload(m.wsA, "wout", D // 256, blk, wv(wout)[:, :, blk * 256:(blk + 1) * 256])
                    for j in range(2):
                        dc = blk * 2 + j
                        po, Bpo = pp.get()
                        for kc in range(KC):
                            fw.op("pe", "matmul", MM(
                                po[:], lhsT=wot[:, kc, j * 128:(j + 1) * 128], rhs=m.ub[:, kc, :], start=(kc == 0), stop=(kc == KC - 1)),
                                reads=[wob, m.Bub], writes=[Bpo], sig=(kc == KC - 1))
                        fw.op("dve", "scalar_tensor_tensor", dict(
                            out=m.xt[:, dc, :], in0=po[:], scalar=G2[:, dc:dc + 1], in1=m.xt[:, dc, :], op0=ALU.mult, op1=ALU.add),
                            reads=[Bpo, Bmod, m.Bxt], writes=[m.Bxt])
                rmsnorm_mod(m, 2)
                ffn(m, wv(w1b), wv(w3b), wv(w2b), 2, "f2")
                fw.dma("sp", trk["o"], fmv(outT)[:, :, t0 - NPRE * TT:t0 - NPRE * TT + TT], m.xt[:], reads=[m.Bxt], writes=[Bout])
            fw.finish([Bout])
        fw.es = es
        print("instructions recorded:", fw.ninst)
        fw.emit()
    return nc


def t5_bucket_np(d):
    f = np.float32
    dd = np.maximum(d, 1).astype(f)
    large = 16 + (np.log(dd / f(16)) / f(math.log(64)) * f(16)).astype(np.int32)
    large = np.minimum(large, 31)
    return np.where(d < 16, d, large)


_CONST = {}


def consts():
    if _CONST:
        return _CONST
    f = np.float32
    c = _CONST
    c["ones"] = np.ones((128, 128), f)
    c["ident"] = np.eye(128, dtype=f)
    c["jrev"] = np.ascontiguousarray(np.eye(128, dtype=f)[::-1])
    idx = np.arange(64)
    c["tri"] = (idx[:, None] <= idx[None, :]).astype(f)
    c["mstrict"] = (idx[:, None] > idx[None, :]).astype(f)
    dist = np.arange(LE, dtype=np.int64) - 640
    bk = t5_bucket_np(np.maximum(dist, 0).astype(np.int32))
    oh = np.zeros((32, LE), f)
    valid = dist >= 0
    oh[bk[valid], np.nonzero(valid)[0]] = 1.0
    c["oh"] = oh
    c["negm"] = np.where(valid, 0.0, NEG).astype(f)[None, :]
    j = np.arange(16)
    qb = np.arange(16)
    past = (j[None, :] < qb[:, None])
    c["pastneg"] = np.broadcast_to(np.where(past, 0.0, -1e30).astype(f)[None], (128, 16, 16)).copy()
    c["pastflag"] = np.broadcast_to(past.astype(f)[None], (128, 16, 16)).copy()
    es_ = np.zeros((16, 16, 128), f)
    for jj in range(16):
        es_[jj, jj, :] = 1.0
    c["esel"] = es_
    return c


def prep_shared(inputs):
    f = np.float32
    def fm(v):
        return np.ascontiguousarray(np.asarray(v, f).reshape(-1, 128).T)
    m = dict(consts())
    m["ada_w"] = np.asarray(inputs["ada_w"][0], f)
    m["ada_bT"] = fm(inputs["ada_b"][0])
    m["ngT"] = np.concatenate([fm(inputs["norm1_g"][0]), fm(inputs["norm2_g"][0]), fm(inputs["norm3_g"][0])], axis=1)
    for k in ("ffn1_w1", "ffn1_w3", "ffn1_w2", "ffn2_w1", "ffn2_w3", "ffn2_w2"):
        m[k] = np.asarray(inputs[k][0], f)
    w = np.asarray(inputs["w_in"][0], f)
    m["winp"] = np.ascontiguousarray(np.concatenate([w[:, 0:8192], w[:, 8224:18464], w[:, 8192:8224]], axis=1))
    cw = np.asarray(inputs["dn_conv_w"][0], f)
    m["convw"] = np.ascontiguousarray(cw.T.reshape(48, 128, 4).transpose(1, 0, 2))
    m["alog"] = np.asarray(inputs["dn_a_log"], f).reshape(1, NH)
    m["dtb"] = np.asarray(inputs["dn_dt_bias"], f).reshape(1, NH)
    m["hg"] = np.ascontiguousarray(np.stack([np.asarray(inputs["dn_norm_g"][0], f), np.asarray(inputs["mb_q_norm_g"][0], f),
                                             np.asarray(inputs["mb_k_norm_g"][0], f)], axis=1))
    m["relb"] = np.asarray(inputs["rel_bias"], f)
    m["wpa"] = np.asarray(inputs["w_proj_a"][0], f)
    m["wpb"] = np.asarray(inputs["w_proj_b"][0], f)
    m["wout"] = np.asarray(inputs["w_out"][0], f)
    return m


def seq_masks(sidx, npre_blk):
    f = np.float32
    j = np.arange(16)
    qb = np.arange(16)
    past = (j[None, :] < qb[:, None])
    extra = np.zeros((16, 16), f)
    if sidx == 0:
        past = past & (j[None, :] >= npre_blk)
        extra[:, :npre_blk] = NEG
    rep = lambda a: np.broadcast_to(a[None], (128, 16, 16)).copy()
    return (rep(np.where(past, 0.0, -1e30).astype(f)), rep(past.astype(f)), rep(extra))


def prep_core(inputs, shared, b, sidx, NT=NTILE):
    f = np.float32
    m = dict(shared)
    npre = NT // 2
    half = npre * TT
    xb = np.asarray(inputs["x"][b], f)
    xT = np.zeros((D, T), f)
    if sidx == 1:
        xT[:, 0:2 * half] = xb[0:2 * half].T
    else:
        xT[:, half:2 * half] = xb[0:half].T
    m["xT"] = xT
    m["cT"] = np.ascontiguousarray(np.asarray(inputs["c"][b], f).reshape(-1, 128).T)
    m["flagv"] = np.full((128, 1), float(sidx), f)
    m["pastneg"], m["pastflag"], m["extram"] = seq_masks(sidx, npre * 2)
    return m


_NC = {}


def kernel(**inputs):
    if "nc" not in _NC:
        _NC["nc"] = build(99, NTILE)
    nc = _NC["nc"]
    shared = prep_shared(inputs)
    maps = [prep_core(inputs, shared, c % 4, c // 4) for c in range(8)]
    res = run_bass_kernel_spmd(nc, maps, core_ids=list(range(8)))
    out = np.empty((4, T, D), np.float32)
    h = T // 2
    for c in range(8):
        out[c % 4, (c // 4) * h:(c // 4 + 1) * h, :] = res.results[c]["outT"].T
    return out
```

```python
import math
import numpy as np
from concourse.bass_utils import run_bass_kernel_spmd
import numpy as np
import concourse.bass as bass
import concourse.mybir as mybir
from contextlib import ExitStack

F32 = mybir.dt.float32
BF16 = mybir.dt.bfloat16
AF = mybir.ActivationFunctionType
ALU = mybir.AluOpType
AX = mybir.AxisListType

ENGS = ("pe", "act", "dve", "pool", "sp")


class Buf:
    __slots__ = ("name", "w", "r", "multi")

    def __init__(self, name="", multi=False):
        self.name = name
        self.multi = multi
        self.w = None
        self.r = {}


class Track:
    def __init__(self, sem, name):
        self.sem = sem
        self.n = 0
        self.name = name


class FW:
    def __init__(self, nc, es):
        self.nc = nc
        self.es = es
        self.es_glob = es
        self.h = {"pe": nc.tensor, "act": nc.scalar, "dve": nc.vector, "pool": nc.gpsimd, "sp": nc.sync}
        self.sem = {e: es.enter_context(nc.semaphore("sem_" + e)) for e in ENGS}
        self.cnt = {e: 0 for e in ENGS}
        self.seen = {e: {} for e in ENGS}
        self.prog = {e: [] for e in ENGS}
        self.segs = []
        self.tracks = []
        self.ninst = 0

    def track(self, name):
        t = Track(self.es_glob.enter_context(self.nc.semaphore("trk_" + name)), name)
        self.tracks.append(t)
        return t

    def sb(self, name, shape, dt):
        self.uid = getattr(self, "uid", 0) + 1
        return self.es.enter_context(self.nc.sbuf_tensor(f"{name}_{self.uid}", list(shape), dt))

    def ps(self, name, shape, dt=F32):
        return self.es.enter_context(self.nc.psum_tensor(name, list(shape), dt))

    def _deps(self, eng, reads, writes):
        deps = {}
        def add(d):
            if d is None:
                return
            kind, key, val = d
            if kind == 'e' and key == eng and eng in ('pe', 'sp'):
                return
            k = (kind, key if kind == 'e' else id(key))
            if k not in deps or deps[k][2] < val:
                deps[k] = d
        for b in reads:
            add(b.w)
        for b in writes:
            if b.multi:
                continue
            add(b.w)
            for d in b.r.values():
                add(d)
        waits = []
        seen = self.seen[eng]
        for k, (kind, key, val) in deps.items():
            if seen.get(k, 0) >= val:
                continue
            seen[k] = val
            sem = self.sem[key] if kind == 'e' else key.sem
            waits.append((sem, val))
        return waits

    def op(self, eng, name, kw, reads=(), writes=(), sig=True):
        waits = self._deps(eng, reads, writes)
        sem = self.sem[eng]
        fn = lambda h, name=name, kw=kw: getattr(h, name)(**kw)
        if sig:
            self.cnt[eng] += 1
            idx = self.cnt[eng]
            def run(h, fn=fn, waits=waits, sem=sem):
                for (s, v) in waits:
                    h.wait_ge(s, v)
                fn(h).then_inc(sem, 1)
        else:
            idx = self.cnt[eng] + 1
            def run(h, fn=fn, waits=waits):
                for (s, v) in waits:
                    h.wait_ge(s, v)
                fn(h)
        self.prog[eng].append(run)
        tok = ('e', eng, idx)
        for b in reads:
            b.r[('e', eng)] = tok
        for b in writes:
            b.w = tok
            b.r = {}
        self.ninst += 1
        return tok

    def dma(self, q, track, out, in_, reads=(), writes=()):
        if callable(out) or callable(in_):
            q = "pool"
        waits = self._deps(q, reads, writes)
        track.n += 16
        val = track.n
        def run(h, waits=waits, out=out, in_=in_, sem=track.sem):
            for (s, v) in waits:
                h.wait_ge(s, v)
            o_ = out(h) if callable(out) else out
            i_ = in_(h) if callable(in_) else in_
            h.dma_start(out=o_, in_=i_).then_inc(sem, 16)
        self.prog[q].append(run)
        tok = ('d', track, val)
        for b in reads:
            b.r[('d', id(track))] = tok
        for b in writes:
            b.w = tok
            b.r = {}
        self.ninst += 1
        return tok

    def barrier(self):
        for e in ENGS:
            waits = []
            seen = self.seen[e]
            for e2 in ENGS:
                if e2 == e or self.cnt[e2] == 0:
                    continue
                k = ('e', e2)
                if seen.get(k, 0) < self.cnt[e2]:
                    seen[k] = self.cnt[e2]
                    waits.append((self.sem[e2], self.cnt[e2]))
            for t in self.tracks:
                k = ('d', id(t))
                if t.n > 0 and seen.get(k, 0) < t.n:
                    seen[k] = t.n
                    waits.append((t.sem, t.n))
            def run(h, waits=waits):
                for (s, v) in waits:
                    h.wait_ge(s, v)
            self.prog[e].append(run)

    def core_barrier(self):
        self.barrier()
        self.segs.append(self.prog)
        self.prog = {e: [] for e in ENGS}

    def finish(self, out_bufs):
        waits = self._deps("sp", out_bufs, ())
        def run(h, waits=waits):
            for (s, v) in waits:
                h.wait_ge(s, v)
        self.prog["sp"].append(run)

    def emit(self):
        nc = self.nc
        segs = self.segs + [self.prog]
        for si, prog in enumerate(segs):
            self.cur_seg = si
            if si > 0:
                nc.all_core_barrier()
            with nc.Block() as block:
                @block.tensor
                def _(h, prog=prog):
                    for f in prog["pe"]:
                        f(h)
                @block.scalar
                def _(h, prog=prog):
                    for f in prog["act"]:
                        f(h)
                @block.vector
                def _(h, prog=prog):
                    for f in prog["dve"]:
                        f(h)
                @block.gpsimd
                def _(h, prog=prog):
                    for f in prog["pool"]:
                        f(h)
                @block.sync
                def _(h, prog=prog):
                    for f in prog["sp"]:
                        f(h)


D = 2048
T = 4096
FF = 5632
KC = D // 128
FFC = FF // 128
TT = 512
NTILE = T // TT
EPS = 1e-6
NH = 16
NCH = T // 64
WP = 18432
LE = 4736
WSH = 4609
NEG = -30000.0


def MM(out, **kw):
    return dict(out=out, **kw)


def TRP(out, in_, identity):
    return dict(out=out, in_=in_, identity=identity)


def MS(ap, constant):
    return dict(ap=ap, constant=constant)


class PsumPool:
    def __init__(self, fw, n, name="ps"):
        self.t = [fw.ps(f"{name}{i}", [128, 512]) for i in range(n)]
        self.b = [Buf(f"{name}{i}") for i in range(n)]
        self.i = 0
        self.n = n

    def get(self):
        i = self.i
        self.i = (self.i + 1) % self.n
        return self.t[i], self.b[i]


class Rot:
    def __init__(self, fw, n, shape, dt, name):
        self.t = [fw.sb(f"{name}{i}", shape, dt) for i in range(n)]
        self.b = [Buf(f"{name}{i}") for i in range(n)]
        self.i = 0
        self.n = n

    def get(self):
        i = self.i
        self.i = (self.i + 1) % self.n
        return self.t[i], self.b[i]


class WSlots:
    def __init__(self, fw, trks, shape, name, dt=BF16, q="pool"):
        n = len(trks)
        self.fw = fw
        self.t = [fw.sb(f"{name}{i}", shape, dt) for i in range(n)]
        self.b = [Buf(f"{name}{i}") for i in range(n)]
        self.trk = trks
        self.i = 0
        self.n = n
        self.q = q

    def load(self, src_ap, dst_slice=None):
        i = self.i
        self.i = (self.i + 1) % self.n
        dst = self.t[i][:] if dst_slice is None else dst_slice(self.t[i])
        self.fw.dma(self.q, self.trk[i], dst, src_ap, writes=[self.b[i]])
        return self.t[i], self.b[i]


def build(stage=99, NT=NTILE):
    NPRE = NT // 2
    nc = bass.Bass("TRN2", target_bir_lowering=False)
    def din(name, shape, dt=F32):
        return nc.dram_tensor(name, list(shape), dt, kind="ExternalInput").ap()
    dbg = stage != 99
    def dscr(name, shape, dt=F32):
        return nc.dram_tensor(name, list(shape), dt, kind=("ExternalOutput" if dbg else "Internal")).ap()
    xT = din("xT", [D, T])
    cT = din("cT", [128, KC])
    ada_w = din("ada_w", [D, 9 * D])
    ada_bT = din("ada_bT", [128, 9 * KC])
    ngT = din("ngT", [128, 3 * KC])
    w1a = din("ffn1_w1", [D, FF]); w3a = din("ffn1_w3", [D, FF]); w2a = din("ffn1_w2", [FF, D])
    w1b = din("ffn2_w1", [D, FF]); w3b = din("ffn2_w3", [D, FF]); w2b = din("ffn2_w2", [FF, D])
    winp = din("winp", [D, WP + 32])
    convw = din("convw", [128, 48, 4])
    alog = din("alog", [1, NH]); dtb = din("dtb", [1, NH])
    hg = din("hg", [128, 3])
    relb = din("relb", [32, NH])
    wpa = din("wpa", [D, D]); wpb = din("wpb", [D, D]); wout = din("wout", [D, D])
    ones_in = din("ones", [128, 128]); ident_in = din("ident", [128, 128])
    tri_in = din("tri", [64, 64]); mstrict_in = din("mstrict", [64, 64])
    oh_in = din("oh", [32, LE]); negm_in = din("negm", [1, LE])
    pastneg_in = din("pastneg", [128, 16, 16]); pastflag_in = din("pastflag", [128, 16, 16])
    esel_in = din("esel", [16, 16, 128])
    jrev_in = din("jrev", [128, 128])
    flag_in = din("flagv", [128, 1])
    extram_in = din("extram", [128, 16, 16])
    outT = nc.dram_tensor("outT", [D, T // 2], F32, kind="ExternalOutput").ap()
    h1T = dscr("h1T", [D, T]); qkvT = dscr("qkvT", [3 * D, T]); zT = dscr("zT", [D, T])
    qmT = dscr("qmT", [D, T]); kmT = dscr("kmT", [D, T]); vm = dscr("vm", [T, D], BF16)
    ba = dscr("ba", [T, 32]); gaT = dscr("gaT", [D, T]); gbT = dscr("gbT", [D, T])
    yaT = dscr("yaT", [D, T], BF16); ybT = dscr("ybT", [D, T], BF16)
    E_d = dscr("E_d", [NH, LE])
    Bh1, Bqkv, Bz, Bqm, Bkm, Bvm, Bba, Bga, Bgb, Bya, Byb, BE, Bout = [Buf(n, multi=(n != "E")) for n in
        ("h1", "qkv", "z", "qm", "km", "vm", "ba", "ga", "gb", "ya", "yb", "E", "out")]

    def fmv(ap):
        return ap.rearrange("(c p) t -> p c t", p=128)

    es = ExitStack()
    with es:
        fw = FW(nc, es)
        trk = {n: fw.track(n) for n in ("const", "x", "o", "a0", "a1", "a2", "a3", "b0", "b1", "w0", "w1",
                                         "l0", "l1", "l2", "l3", "l4", "s0", "s1", "s2")}
        pp = PsumPool(fw, 6)
        trk_of = {}
        def btrack(B):
            if id(B) not in trk_of:
                trk_of[id(B)] = (fw.track(f"bt{len(trk_of)}"), B)
            return trk_of[id(B)][0]
        caches = {}
        def wload(slots, key, nblk, idx, src_ap):
            shp = slots.t[0].shape
            elems = int(shp[1]) * int(shp[2])
            if key not in caches:
                caches[key] = (nc.dram_tensor("wc_" + key, [nblk, 128, elems], BF16, kind="Internal").ap(),
                               [Buf(f"wc_{key}{i}") for i in range(nblk)], [False] * nblk)
            cd, cb, filled = caches[key]
            cview = cd[idx].rearrange("p (c n) -> p c n", c=int(shp[1]))
            if filled[idx]:
                i = slots.i
                slots.i = (slots.i + 1) % slots.n
                fw.dma(slots.q, slots.trk[i], slots.t[i][:], cview, reads=[cb[idx]], writes=[slots.b[i]])
                return slots.t[i], slots.b[i]
            t_, b_ = slots.load(src_ap)
            fw.dma("sp", btrack(b_), cview, t_[:], reads=[b_], writes=[cb[idx]])
            filled[idx] = True
            return t_, b_
        Bconst = Buf("const")
        def cload(name, shape, src, dt=F32, q="sp"):
            t = fw.sb(name, shape, dt)
            fw.dma(q, trk["const"], t[:], src, writes=[Bconst])
            return t
        ones_f = cload("ones_f", [128, 128], ones_in)
        ident = cload("ident", [128, 128], ident_in)
        ones_b = cload("ones_b", [128, 128], ones_in, BF16, "pool")
        hgt = cload("hgt", [128, 3], hg)
        flagv = cload("flagv", [128, 1], flag_in)
        adaT = fw.sb("adaT", [128, 9 * KC], F32); Bada = Buf("ada")
        mod = fw.sb("mod", [128, 9 * KC], F32); Bmod = Buf("mod")
        qgs = fw.sb("qgs", [128, 1], F32)
        fw.op("dve", "tensor_scalar", dict(out=qgs[:], in0=hgt[:, 1:2], scalar1=128.0 ** -0.5, scalar2=None, op0=ALU.mult),
              reads=[Bconst], writes=[Bconst])
        with ExitStack() as es0:
            fw.es = es0
            cs = fw.sb("cs", [128, KC], F32); Bcs = Buf("cs")
            abT = fw.sb("abT", [128, 9 * KC], F32)
            gT = fw.sb("gT", [128, 3 * KC], F32)
            fw.dma("sp", trk["const"], cs[:], cT, writes=[Bcs])
            fw.dma("sp", trk["const"], abT[:], ada_bT, writes=[Bconst])
            fw.dma("sp", trk["const"], gT[:], ngT, writes=[Bconst])
            fw.op("act", "activation", dict(out=cs[:], in_=cs[:], func=AF.Silu), reads=[Bcs], writes=[Bcs])
            aw = WSlots(fw, [trk["w0"], trk["w1"]], [128, KC, 512], "aw", dt=F32, q="sp")
            pada = fw.ps("pada", [128, 9 * KC]); Bpada = Buf("pada")
            awv = ada_w.rearrange("(c p) n -> p c n", p=128)
            arow = fw.sb("arow", [1, 9 * D], F32); Barow = Buf("arow")
            for blk in range(9 * D // 512):
                wt, wb = aw.load(awv[:, :, blk * 512:(blk + 1) * 512])
                prow, Bprow = pp.get()
                for kc in range(KC):
                    fw.op("pe", "matmul", MM(prow[0:1, :], lhsT=cs[:, kc:kc + 1], rhs=wt[:, kc, :], start=(kc == 0), stop=(kc == KC - 1)),
                          reads=[wb, Bcs], writes=[Bprow], sig=(kc == KC - 1))
                fw.op("act", "activation", dict(out=arow[0:1, blk * 512:(blk + 1) * 512], in_=prow[0:1, :], func=AF.Copy),
                      reads=[Bprow], writes=[Barow])
            for col in range(9 * KC):
                fw.op("pe", "matmul", MM(pada[:, col:col + 1], lhsT=arow[0:1, col * 128:(col + 1) * 128], rhs=ones_f[0:1, 0:1], start=True, stop=True),
                      reads=[Barow, Bconst], writes=[Bpada], sig=(col == 9 * KC - 1))
            fw.op("dve", "tensor_tensor", dict(out=adaT[:], in0=pada[:], in1=abT[:], op=ALU.add),
                  reads=[Bpada, Bconst], writes=[Bada])
            for s in range(3):
                sh = adaT[:, (3 * s) * KC:(3 * s + 1) * KC]
                sc = adaT[:, (3 * s + 1) * KC:(3 * s + 2) * KC]
                gt = adaT[:, (3 * s + 2) * KC:(3 * s + 3) * KC]
                A = mod[:, (3 * s) * KC:(3 * s + 1) * KC]
                Bv = mod[:, (3 * s + 1) * KC:(3 * s + 2) * KC]
                G = mod[:, (3 * s + 2) * KC:(3 * s + 3) * KC]
                gn = gT[:, s * KC:(s + 1) * KC]
                fw.op("dve", "scalar_tensor_tensor", dict(
                    out=A, in0=sc, scalar=1.0, in1=gn, op0=ALU.add, op1=ALU.mult), reads=[Bada, Bconst], writes=[Bmod])
                fw.op("dve", "tensor_copy", dict(out=Bv, in_=sh), reads=[Bada], writes=[Bmod])
                fw.op("dve", "tensor_scalar", dict(
                    out=G, in0=gt, scalar1=(1.0 if s == 1 else 0.5), scalar2=None, op0=ALU.mult), reads=[Bada], writes=[Bmod])
            fw.barrier()
        fw.es = es

        class Main:
            pass

        def alloc_main():
            m = Main()
            m.xt = fw.sb("xt", [128, KC, TT], F32); m.Bxt = Buf("xt")
            m.ub = fw.sb("ub", [128, KC, TT], BF16); m.Bub = Buf("ub")
            m.actb = fw.sb("actb", [128, FFC, TT], BF16); m.Bact = Buf("act")
            m.sq = fw.sb("sq", [128, TT], BF16); m.Bsq = Buf("sq")
            m.rstd = fw.sb("rstd", [128, TT], F32); m.Brstd = Buf("rstd")
            m.tmp = Rot(fw, 3, [128, TT], F32, "tmp")
            m.wsA = WSlots(fw, [trk["a0"], trk["a1"], trk["a2"], trk["a3"]], [128, KC, 256], "wsA")
            m.wsB = WSlots(fw, [trk["b0"], trk["b1"]], [128, FFC, 128], "wsB")
            return m

        def rmsnorm_mod(m, s):
            src, Bsrc = m.xt, m.Bxt
            pss, Bpss = pp.get()
            for c in range(KC):
                fw.op("act", "activation", dict(out=m.sq[:], in_=src[:, c, :], func=AF.Square),
                      reads=[Bsrc], writes=[m.Bsq])
                fw.op("pe", "matmul", MM(pss[:], lhsT=ones_b[:], rhs=m.sq[:], start=(c == 0), stop=(c == KC - 1)),
                      reads=[m.Bsq, Bconst], writes=[Bpss])
            fw.op("act", "activation", dict(out=m.rstd[:], in_=pss[:], func=AF.Sqrt, scale=1.0 / D, bias=EPS),
                  reads=[Bpss], writes=[m.Brstd])
            fw.op("dve", "reciprocal", dict(out=m.rstd[:], in_=m.rstd[:]), reads=[m.Brstd], writes=[m.Brstd])
            A = mod[:, (3 * s) * KC:(3 * s + 1) * KC]
            Bv = mod[:, (3 * s + 1) * KC:(3 * s + 2) * KC]
            for c in range(KC):
                tb, Btb = m.tmp.get()
                fw.op("dve", "scalar_tensor_tensor", dict(
                    out=tb[:], in0=src[:, c, :], scalar=A[:, c:c + 1], in1=m.rstd[:], op0=ALU.mult, op1=ALU.mult),
                    reads=[Bsrc, m.Brstd, Bmod], writes=[Btb])
                fw.op("act", "activation", dict(out=m.ub[:, c, :], in_=tb[:], func=AF.Identity,
                                                             bias=Bv[:, c:c + 1], scale=1.0),
                      reads=[Btb, Bmod], writes=[m.Bub])

        def ffn(m, wv1, wv3, wv2, s, ck=""):
            G = mod[:, (3 * s + 2) * KC:(3 * s + 3) * KC]
            for blk in range(FF // 256):
                w1t, w1bb = wload(m.wsA, ck + "w1", FF // 256, blk, wv1[:, :, blk * 256:(blk + 1) * 256])
                w3t, w3bb = wload(m.wsA, ck + "w3", FF // 256, blk, wv3[:, :, blk * 256:(blk + 1) * 256])
                for j in range(2):
                    ffc = blk * 2 + j
                    p1, Bp1 = pp.get()
                    p3, Bp3 = pp.get()
                    for kc in range(KC):
                        fw.op("pe", "matmul", MM(
                            p1[:], lhsT=w1t[:, kc, j * 128:(j + 1) * 128], rhs=m.ub[:, kc, :], start=(kc == 0), stop=(kc == KC - 1)),
                            reads=[w1bb, m.Bub], writes=[Bp1], sig=(kc == KC - 1))
                    for kc in range(KC):
                        fw.op("pe", "matmul", MM(
                            p3[:], lhsT=w3t[:, kc, j * 128:(j + 1) * 128], rhs=m.ub[:, kc, :], start=(kc == 0), stop=(kc == KC - 1)),
                            reads=[w3bb, m.Bub], writes=[Bp3], sig=(kc == KC - 1))
                    tb, Btb = m.tmp.get()
                    fw.op("act", "activation", dict(out=tb[:], in_=p1[:], func=AF.Silu),
                          reads=[Bp1], writes=[Btb])
                    fw.op("dve", "tensor_tensor", dict(
                        out=m.actb[:, ffc, :], in0=tb[:], in1=p3[:], op=ALU.mult), reads=[Btb, Bp3], writes=[m.Bact])
            for dc in range(KC):
                w2t, w2bb = wload(m.wsB, ck + "w2", KC, dc, wv2[:, :, dc * 128:(dc + 1) * 128])
                po, Bpo = pp.get()
                for fc in range(FFC):
                    fw.op("pe", "matmul", MM(
                        po[:], lhsT=w2t[:, fc, :], rhs=m.actb[:, fc, :], start=(fc == 0), stop=(fc == FFC - 1)),
                        reads=[w2bb, m.Bact], writes=[Bpo], sig=(fc == FFC - 1))
                fw.op("dve", "scalar_tensor_tensor", dict(
                    out=m.xt[:, dc, :], in0=po[:], scalar=G[:, dc:dc + 1], in1=m.xt[:, dc, :], op0=ALU.mult, op1=ALU.add),
                    reads=[Bpo, Bmod, m.Bxt], writes=[m.Bxt])

        xTv = fmv(xT)
        wv = lambda w: w.rearrange("(c p) n -> p c n", p=128)

        with ExitStack() as esA:
            fw.es = esA
            m = alloc_main()
            cw = fw.sb("cw", [128, 48, 4], F32)
            fw.dma("sp", trk["const"], cw[:], convw, writes=[Bconst])
            halo = fw.sb("halo", [128, 48, 3], F32); Bhalo = Buf("halo")
            fw.op("dve", "memset", MS(halo[:], 0.0), writes=[Bhalo])
            cbuf = Rot(fw, 2, [128, TT + 3], F32, "cbuf")
            acc = Rot(fw, 2, [128, TT], F32, "acc")
            ost = Rot(fw, 4, [128, TT], F32, "ost")
            vst = fw.sb("vst", [128, 4, D], BF16); Bvst = Buf("vst")
            bast = fw.sb("bast", [128, 4, 32], F32); Bbast = Buf("bast")
            wba = fw.sb("wba", [128, KC, 32], BF16); Bwba = Buf("wba")
            fw.dma("pool", trk["const"], wba[:], wv(winp)[:, :, WP:WP + 32], writes=[Bwba])
            stq = ["s0", "s1", "s2"]
            sti = [0]
            def store(dst, src, Bsrc, Bdst):
                fw.dma("sp", btrack(Bsrc), dst, src, reads=[Bsrc], writes=[Bdst])
            winv = wv(winp)
            fw.dma("sp", trk["x"], m.xt[:], xTv[:, :, 0:TT], writes=[m.Bxt])
            for ti in range(NT):
                t0 = ti * TT
                rmsnorm_mod(m, 0)
                ffn(m, wv(w1a), wv(w3a), wv(w2a), 0, "f1")
                pre = ti < NPRE
                if not pre:
                    store(fmv(h1T)[:, :, t0:t0 + TT], m.xt[:], m.Bxt, Bh1)
                rmsnorm_mod(m, 1)
                if ti + 1 < NT:
                    fw.dma("sp", trk["x"], m.xt[:], xTv[:, :, t0 + TT:t0 + 2 * TT], writes=[m.Bxt])
                for blk in range(WP // 256):
                    if pre and not (8 <= blk < 24 or 40 <= blk < 56 or (blk < 8 and ti == NPRE - 1)):
                        continue
                    wt, wb = wload(m.wsA, "win", WP // 256, blk, winv[:, :, blk * 256:(blk + 1) * 256])
                    if 48 <= blk < 56:
                        for tb in range(4):
                            pv, Bpv = pp.get()
                            for kc in range(KC):
                                fw.op("pe", "matmul", MM(
                                    pv[:, 0:256], lhsT=m.ub[:, kc, tb * 128:(tb + 1) * 128], rhs=wt[:, kc, :],
                                    start=(kc == 0), stop=(kc == KC - 1)),
                                    reads=[wb, m.Bub], writes=[Bpv], sig=(kc == KC - 1))
                            c0 = (blk - 48) * 256
                            fw.op("act", "activation", dict(
                                out=vst[:, tb, c0:c0 + 256], in_=pv[:, 0:256], func=AF.Copy), reads=[Bpv], writes=[Bvst])
                        continue
                    for j in range(2):
                        oc = blk * 2 + j
                        p1, Bp1 = pp.get()
                        for kc in range(KC):
                            fw.op("pe", "matmul", MM(
                                p1[:], lhsT=wt[:, kc, j * 128:(j + 1) * 128], rhs=m.ub[:, kc, :], start=(kc == 0), stop=(kc == KC - 1)),
                                reads=[wb, m.Bub], writes=[Bp1], sig=(kc == KC - 1))
                        if oc < 48:
                            cb, Bcb = cbuf.get()
                            fw.op("act", "activation", dict(out=cb[:, 0:3], in_=halo[:, oc, :], func=AF.Copy),
                                  reads=[Bhalo], writes=[Bcb])
                            fw.op("act", "activation", dict(out=cb[:, 3:TT + 3], in_=p1[:], func=AF.Copy),
                                  reads=[Bp1], writes=[Bcb])
                            fw.op("dve", "tensor_copy", dict(out=halo[:, oc, :], in_=cb[:, TT:TT + 3]),
                                  reads=[Bcb], writes=[Bhalo])
                            ac, Bac = acc.get()
                            fw.op("act", "activation", dict(out=ac[:], in_=cb[:, 0:TT], func=AF.Copy, scale=cw[:, oc, 0:1]),
                                  reads=[Bcb, Bconst], writes=[Bac])
                            for jj in range(1, 4):
                                fw.op("dve", "scalar_tensor_tensor", dict(
                                    out=ac[:], in0=cb[:, jj:jj + TT], scalar=cw[:, oc, jj:jj + 1], in1=ac[:], op0=ALU.mult, op1=ALU.add),
                                    reads=[Bcb, Bconst, Bac], writes=[Bac])
                            o_, Bo_ = ost.get()
                            if oc >= 32:
                                fw.op("act", "activation", dict(out=o_[:], in_=ac[:], func=AF.Silu),
                                      reads=[Bac], writes=[Bo_])
                            else:
                                fw.op("act", "activation", dict(out=ac[:], in_=ac[:], func=AF.Silu),
                                      reads=[Bac], writes=[Bac])
                                fw.op("act", "activation", dict(out=m.sq[:], in_=ac[:], func=AF.Square),
                                      reads=[Bac], writes=[m.Bsq])
                                p2, Bp2 = pp.get()
                                fw.op("pe", "matmul", MM(p2[:], lhsT=ones_b[:], rhs=m.sq[:], start=True, stop=True),
                                      reads=[m.Bsq, Bconst], writes=[Bp2])
                                fw.op("act", "activation", dict(out=m.rstd[:], in_=p2[:], func=AF.Sqrt, scale=1.0, bias=EPS),
                                      reads=[Bp2], writes=[m.Brstd])
                                fw.op("dve", "reciprocal", dict(out=m.rstd[:], in_=m.rstd[:]), reads=[m.Brstd], writes=[m.Brstd])
                                sc_ = (128.0 ** -0.5) if oc < 16 else 1.0
                                fw.op("dve", "scalar_tensor_tensor", dict(
                                    out=o_[:], in0=ac[:], scalar=sc_, in1=m.rstd[:], op0=ALU.mult, op1=ALU.mult),
                                    reads=[Bac, m.Brstd], writes=[Bo_])
                            store(qkvT[oc * 128:(oc + 1) * 128, t0:t0 + TT], o_[:], Bo_, Bqkv)
                        elif oc < 64:
                            o_, Bo_ = ost.get()
                            fw.op("act", "activation", dict(out=o_[:], in_=p1[:], func=AF.Silu),
                                  reads=[Bp1], writes=[Bo_])
                            r0 = (oc - 48) * 128
                            store(zT[r0:r0 + 128, t0:t0 + TT], o_[:], Bo_, Bz)
                        elif oc < 96:
                            isq = oc < 80
                            fw.op("act", "activation", dict(out=m.sq[:], in_=p1[:], func=AF.Square),
                                  reads=[Bp1], writes=[m.Bsq])
                            p2, Bp2 = pp.get()
                            fw.op("pe", "matmul", MM(p2[:], lhsT=ones_b[:], rhs=m.sq[:], start=True, stop=True),
                                  reads=[m.Bsq, Bconst], writes=[Bp2])
                            fw.op("act", "activation", dict(out=m.rstd[:], in_=p2[:], func=AF.Sqrt, scale=1.0 / 128, bias=EPS),
                                  reads=[Bp2], writes=[m.Brstd])
                            fw.op("dve", "reciprocal", dict(out=m.rstd[:], in_=m.rstd[:]), reads=[m.Brstd], writes=[m.Brstd])
                            o_, Bo_ = ost.get()
                            gcol = qgs[:, 0:1] if isq else hgt[:, 2:3]
                            fw.op("dve", "scalar_tensor_tensor", dict(
                                out=o_[:], in0=p1[:], scalar=gcol, in1=m.rstd[:], op0=ALU.mult, op1=ALU.mult),
                                reads=[Bp1, m.Brstd, Bconst], writes=[Bo_])
                            if isq:
                                r0 = (oc - 64) * 128
                                store(qmT[r0:r0 + 128, t0:t0 + TT], o_[:], Bo_, Bqm)
                            else:
                                r0 = (oc - 80) * 128
                                store(kmT[r0:r0 + 128, t0:t0 + TT], o_[:], Bo_, Bkm)
                        else:
                            o_, Bo_ = ost.get()
                            fw.op("act", "activation", dict(out=o_[:], in_=p1[:], func=AF.Sigmoid),
                                  reads=[Bp1], writes=[Bo_])
                            if oc < 128:
                                r0 = (oc - 112) * 128
                                store(gaT[r0:r0 + 128, t0:t0 + TT], o_[:], Bo_, Bga)
                            else:
                                r0 = (oc - 128) * 128
                                store(gbT[r0:r0 + 128, t0:t0 + TT], o_[:], Bo_, Bgb)
                for tb in range(4):
                    pv, Bpv = pp.get()
                    for kc in range(KC):
                        fw.op("pe", "matmul", MM(
                            pv[:, 0:32], lhsT=m.ub[:, kc, tb * 128:(tb + 1) * 128], rhs=wba[:, kc, :],
                            start=(kc == 0), stop=(kc == KC - 1)),
                            reads=[Bwba, m.Bub], writes=[Bpv], sig=(kc == KC - 1))
                    fw.op("act", "activation", dict(out=bast[:, tb, :], in_=pv[:, 0:32], func=AF.Copy),
                          reads=[Bpv], writes=[Bbast])
                store(vm.rearrange("(n p) d -> p n d", p=128)[:, ti * 4:(ti + 1) * 4, :], vst[:], Bvst, Bvm)
                store(ba.rearrange("(n p) d -> p n d", p=128)[:, ti * 4:(ti + 1) * 4, :], bast[:], Bbast, Bba)
                if ti == NPRE - 1:
                    fw.op("dve", "tensor_scalar", dict(out=halo[:], in0=halo[:], scalar1=flagv[:, 0:1], scalar2=None, op0=ALU.mult),
                          reads=[Bhalo, Bconst], writes=[Bhalo])
            fw.barrier()
        fw.es = es
        if stage == 2:
            fw.finish([Bh1, Bqkv, Bz, Bqm, Bkm, Bvm, Bba, Bga, Bgb])
            print("instructions recorded:", fw.ninst)
            fw.emit()
            return nc

        NG = NT
        NCHr = NG * 8
        TR = NT * TT
        with ExitStack() as esB:
            fw.es = esB
            tri = cload("tri", [64, 64], tri_in)
            mstrict = cload("mstrict", [64, 64], mstrict_in)
            alB = fw.sb("alB", [64, NH], F32); dtB = fw.sb("dtB", [64, NH], F32)
            fw.dma("sp", trk["const"], alB[:], alog[0:1, :].to_broadcast([64, NH]), writes=[Bconst])
            fw.dma("sp", trk["const"], dtB[:], dtb[0:1, :].to_broadcast([64, NH]), writes=[Bconst])
            ba3 = fw.sb("ba3", [64, NCH, 32], F32); Bba3 = Buf("ba3")
            fw.dma("sp", trk["l0"], ba3[:, 0:NCHr, :], ba.rearrange("(n p) c -> p n c", p=64)[:, 0:NCHr, :], reads=[Bba], writes=[Bba3])
            NC_ = NCHr * NH
            beta = fw.sb("beta", [64, NCH, NH], F32); gg = fw.sb("gg", [64, NCH, NH], F32)
            gc = fw.sb("gc", [64, NCH, NH], F32); egc = fw.sb("egc", [64, NCH, NH], F32)
            bege = fw.sb("bege", [64, NCH, NH], F32); edec = fw.sb("edec", [64, NCH, NH], F32)
            egs = fw.sb("egs", [128, NCH, NH], F32)
            Bg = Buf("gstuff")
            R_ = slice(0, NCHr)
            fw.op("act", "activation", dict(out=beta[:, R_, :], in_=ba3[:, R_, 0:16], func=AF.Sigmoid), reads=[Bba3], writes=[Bg])
            fw.op("dve", "tensor_scalar", dict(out=beta[:, 0:NPRE * 8, :], in0=beta[:, 0:NPRE * 8, :], scalar1=flagv[0:64, 0:1], scalar2=None, op0=ALU.mult),
                  reads=[Bg, Bconst], writes=[Bg])
            fw.op("dve", "tensor_tensor", dict(out=gg[:, R_, :], in0=ba3[:, R_, 16:32],
                                                   in1=dtB[:].unsqueeze(1).to_broadcast([64, NCHr, NH]), op=ALU.add),
                  reads=[Bba3, Bconst], writes=[Bg])
            fw.op("act", "activation", dict(out=gg[:, R_, :], in_=gg[:, R_, :], func=AF.Exp), reads=[Bg], writes=[Bg])
            fw.op("act", "activation", dict(out=gg[:, R_, :], in_=gg[:, R_, :], func=AF.Ln, bias=1.0, scale=1.0), reads=[Bg], writes=[Bg])
            fw.op("act", "activation", dict(out=alB[:], in_=alB[:], func=AF.Exp), reads=[Bconst], writes=[Bconst])
            fw.op("dve", "scalar_tensor_tensor", dict(out=gg[:, R_, :], in0=gg[:, R_, :], scalar=-1.0,
                                                          in1=alB[:].unsqueeze(1).to_broadcast([64, NCHr, NH]),
                                                          op0=ALU.mult, op1=ALU.mult), reads=[Bg, Bconst], writes=[Bg])
            ggf = gg[:].rearrange("p n h -> p (n h)"); gcf = gc[:].rearrange("p n h -> p (n h)")
            egsf = egs[:].rearrange("p n h -> p (n h)")
            for cc in range(0, NC_, 512):
                w_ = min(512, NC_ - cc)
                p1, Bp1 = pp.get()
                fw.op("pe", "matmul", MM(p1[0:64, 0:w_], lhsT=tri[:], rhs=ggf[:, cc:cc + w_], start=True, stop=True),
                      reads=[Bg, Bconst], writes=[Bp1])
                fw.op("dve", "tensor_copy", dict(out=gcf[:, cc:cc + w_], in_=p1[0:64, 0:w_]), reads=[Bp1], writes=[Bg])
                p2, Bp2 = pp.get()
                fw.op("pe", "matmul", MM(p2[:, 0:w_], lhsT=ones_f[0:64, :], rhs=ggf[:, cc:cc + w_], start=True, stop=True),
                      reads=[Bg, Bconst], writes=[Bp2])
                fw.op("dve", "tensor_copy", dict(out=egsf[:, cc:cc + w_], in_=p2[:, 0:w_]), reads=[Bp2], writes=[Bg])
            fw.op("dve", "tensor_tensor", dict(out=edec[:, R_, :], in0=egs[0:64, R_, :], in1=gc[:, R_, :], op=ALU.subtract), reads=[Bg], writes=[Bg])
            fw.op("act", "activation", dict(out=edec[:, R_, :], in_=edec[:, R_, :], func=AF.Exp), reads=[Bg], writes=[Bg])
            fw.op("act", "activation", dict(out=egs[:, R_, :], in_=egs[:, R_, :], func=AF.Exp), reads=[Bg], writes=[Bg])
            fw.op("act", "activation", dict(out=egc[:, R_, :], in_=gc[:, R_, :], func=AF.Exp), reads=[Bg], writes=[Bg])
            fw.op("dve", "tensor_tensor", dict(out=bege[:, R_, :], in0=beta[:, R_, :], in1=egc[:, R_, :], op=ALU.mult), reads=[Bg], writes=[Bg])

            kTh = fw.sb("kTh", [128, T], F32); qTh = fw.sb("qTh", [128, T], F32); vTh = fw.sb("vTh", [128, T], F32)
            zTh = fw.sb("zTh", [128, T], F32); oTh = fw.sb("oTh", [128, T], F32)
            Bk, Bq, Bv_, Bzh, Bo = Buf("kTh"), Buf("qTh"), Buf("vTh"), Buf("zTh"), Buf("oTh")
            S = fw.sb("S", [128, 128], F32); BS = Buf("S")
            kbe = fw.sb("kbe", [64, 8, 128], F32); kdec = fw.sb("kdec", [64, 8, 128], F32); vb = fw.sb("vb", [64, 8, 128], F32)
            Bkbe, Bkdec, Bvb = Buf("kbe"), Buf("kdec"), Buf("vb")
            Gbc = fw.sb("Gbc", [64, 8, 128], F32); BGbc = Buf("Gbc")
            def g64(name):
                return fw.sb(name, [64, 8, 64], F32), Buf(name)
            d1, Bd1 = g64("d1"); decA, BdecA = g64("decA"); decT, BdecT = g64("decT")
            Am, BAm = g64("Am"); ATm, BATm = g64("ATm"); QT, BQT = g64("QT"); PT, BPT = g64("PT")
            M2a, BM2a = g64("M2a"); MT2a, BMT2a = g64("MT2a"); M2b, BM2b = g64("M2b"); MT2b, BMT2b = g64("MT2b")
            egcB = fw.sb("egcB", [128, 512], F32); BegcB = Buf("egcB")
            qd = fw.sb("qd", [128, 512], F32); Bqd = Buf("qd")
            U = fw.sb("U", [64, 8, 128], F32); BU = Buf("U")
            WT = fw.sb("WT", [128, 8, 64], F32); BWT = Buf("WT")
            vnr = Rot(fw, 2, [64, 128], F32, "vn")
            U_b = fw.sb("U_b", [64, 8, 128], F32); BU_b = Buf("U_b")
            WT_b = fw.sb("WT_b", [128, 8, 64], F32); BWT_b = Buf("WT_b")
            PT_b, BPT_b = g64("PT_b")
            qd_b = fw.sb("qd_b", [128, 512], F32); Bqd_b = Buf("qd_b")
            kdec_b = fw.sb("kdec_b", [64, 8, 128], F32); Bkdec_b = Buf("kdec_b")
            sqd = fw.sb("sqd", [128, 512], F32); Bsqd = Buf("sqd")
            rsd = fw.sb("rsd", [128, 512], F32); Brsd = Buf("rsd")
            yst = Rot(fw, 2, [128, 512], BF16, "yst")
            fl = lambda t: t[:].rearrange("p c i -> p (c i)")
            for hh in range(NH):
                fw.dma("sp", trk["l1"], qTh[:, 0:TR], qkvT[hh * 128:(hh + 1) * 128, 0:TR], reads=[Bqkv], writes=[Bq])
                fw.dma("sp", trk["l2"], kTh[:, 0:TR], qkvT[(16 + hh) * 128:(17 + hh) * 128, 0:TR], reads=[Bqkv], writes=[Bk])
                fw.dma("sp", trk["l3"], vTh[:, 0:TR], qkvT[(32 + hh) * 128:(33 + hh) * 128, 0:TR], reads=[Bqkv], writes=[Bv_])
                fw.dma("sp", trk["l4"], zTh[:, 0:TR], zT[hh * 128:(hh + 1) * 128, 0:TR], reads=[Bz], writes=[Bzh])
                fw.op("dve", "memset", MS(S[:], 0.0), writes=[BS])
                def prep(gi, RB):
                    U, BU, WT, BWT, PT, BPT, qd, Bqd, kdec, Bkdec = RB
                    n0 = gi * 8
                    t0 = gi * TT
                    opx = (lambda *a_, **k_: None) if gi < NPRE else fw.op
                    bc8 = lambda arr, w: arr[:, n0:n0 + 8, hh:hh + 1].to_broadcast([64, 8, w])
                    bc4 = lambda arr, a, w: arr[:, n0 + a:n0 + a + 4, hh:hh + 1].to_broadcast([64, 4, w])
                    for half in range(2):
                        pk, Bpk = pp.get()
                        pv, Bpv = pp.get()
                        for c in range(4):
                            cc = half * 4 + c
                            ts_ = slice(t0 + cc * 64, t0 + cc * 64 + 64)
                            fw.op("pe", "transpose", TRP(pk[0:64, c * 128:(c + 1) * 128], kTh[:, ts_], ident[:]),
                                  reads=[Bk, Bconst], writes=[Bpk], sig=(c == 3))
                            fw.op("pe", "transpose", TRP(pv[0:64, c * 128:(c + 1) * 128], vTh[:, ts_], ident[:]),
                                  reads=[Bv_, Bconst], writes=[Bpv], sig=(c == 3))
                        a = half * 4
                        pk3 = pk[0:64, :].rearrange("p (c d) -> p c d", c=4)
                        pv3 = pv[0:64, :].rearrange("p (c d) -> p c d", c=4)
                        fw.op("dve", "tensor_tensor", dict(out=kbe[:, a:a + 4, :], in0=pk3, in1=bc4(bege, a, 128), op=ALU.mult),
                              reads=[Bpk, Bg], writes=[Bkbe])
                        fw.op("dve", "tensor_tensor", dict(out=kdec[:, a:a + 4, :], in0=pk3, in1=bc4(edec, a, 128), op=ALU.mult),
                              reads=[Bpk, Bg], writes=[Bkdec])
                        fw.op("dve", "tensor_tensor", dict(out=vb[:, a:a + 4, :], in0=pv3, in1=bc4(beta, a, 128), op=ALU.mult),
                              reads=[Bpv, Bg], writes=[Bvb])
                        yield
                    fw.op("dve", "tensor_copy", dict(out=Gbc[:], in_=bc8(gg, 128)), reads=[Bg], writes=[BGbc])
                    pG, BpG = pp.get(); pKQ, BpKQ = pp.get(); pgB, BpgB = pp.get()
                    for c in range(8):
                        ts_ = slice(t0 + c * 64, t0 + c * 64 + 64)
                        cs_ = slice(c * 64, c * 64 + 64)
                        fw.op("pe", "matmul", MM(pG[0:64, cs_], lhsT=kTh[:, ts_], rhs=kTh[:, ts_], start=True, stop=True),
                              reads=[Bk], writes=[BpG], sig=(c == 7))
                        opx("pe", "matmul", MM(pKQ[0:64, cs_], lhsT=kTh[:, ts_], rhs=qTh[:, ts_], start=True, stop=True),
                              reads=[Bk, Bq], writes=[BpKQ], sig=(c == 7))
                        fw.op("pe", "matmul", MM(pgB[:, cs_], lhsT=Gbc[:, c, :], rhs=tri[:], start=True, stop=True),
                              reads=[BGbc, Bconst], writes=[BpgB], sig=(c == 7))
                    yield
                    pgB3 = pgB[0:64, :].rearrange("p (c i) -> p c i", c=8)
                    pG3 = pG[0:64, :].rearrange("p (c i) -> p c i", c=8)
                    pKQ3 = pKQ[0:64, :].rearrange("p (c i) -> p c i", c=8)
                    fw.op("dve", "tensor_tensor", dict(out=d1[:], in0=pgB3, in1=bc8(gc, 64), op=ALU.subtract), reads=[BpgB, Bg], writes=[Bd1])
                    fw.op("dve", "tensor_scalar", dict(out=decA[:], in0=d1[:], scalar1=0.0, scalar2=None, op0=ALU.max), reads=[Bd1], writes=[BdecA])
                    fw.op("act", "activation", dict(out=decA[:], in_=decA[:], func=AF.Exp, scale=-1.0), reads=[BdecA], writes=[BdecA])
                    fw.op("dve", "tensor_tensor", dict(out=decA[:], in0=decA[:], in1=mstrict[:].unsqueeze(1).to_broadcast([64, 8, 64]), op=ALU.mult),
                          reads=[BdecA, Bconst], writes=[BdecA])
                    opx("dve", "tensor_scalar", dict(out=decT[:], in0=d1[:], scalar1=0.0, scalar2=None, op0=ALU.min), reads=[Bd1], writes=[BdecT])
                    opx("act", "activation", dict(out=decT[:], in_=decT[:], func=AF.Exp), reads=[BdecT], writes=[BdecT])
                    opx("dve", "tensor_tensor", dict(out=decT[:], in0=decT[:], in1=tri[:].unsqueeze(1).to_broadcast([64, 8, 64]), op=ALU.mult),
                          reads=[BdecT, Bconst], writes=[BdecT])
                    fw.op("dve", "tensor_tensor", dict(out=Am[:], in0=pG3, in1=bc8(beta, 64), op=ALU.mult), reads=[BpG, Bg], writes=[BAm])
                    fw.op("dve", "tensor_tensor", dict(out=Am[:], in0=Am[:], in1=decA[:], op=ALU.mult), reads=[BAm, BdecA], writes=[BAm])
                    opx("dve", "tensor_tensor", dict(out=PT[:], in0=pKQ3, in1=decT[:], op=ALU.mult), reads=[BpKQ, BdecT], writes=[BPT])
                    opx("act", "activation", dict(out=egcB[:], in_=pgB[:], func=AF.Exp), reads=[BpgB], writes=[BegcB])
                    opx("dve", "tensor_tensor", dict(out=qd[:], in0=qTh[:, t0:t0 + TT], in1=egcB[:], op=ALU.mult),
                          reads=[Bq, BegcB], writes=[Bqd])
                    yield
                    pT_, BpT_ = pp.get()
                    for c in range(8):
                        cs_ = slice(c * 64, c * 64 + 64)
                        fw.op("pe", "transpose", TRP(pT_[0:64, cs_], Am[:, c, :], ident[0:64, 0:64]),
                              reads=[BAm, Bconst], writes=[BpT_], sig=(c == 7))
                    fw.op("dve", "tensor_copy", dict(out=fl(ATm), in_=pT_[0:64, :]), reads=[BpT_], writes=[BATm])
                    fw.op("dve", "tensor_tensor", dict(out=QT[:], in0=ident[0:64, 0:64].unsqueeze(1).to_broadcast([64, 8, 64]), in1=ATm[:], op=ALU.subtract),
                          reads=[BATm, Bconst], writes=[BQT])
                    yield
                    Mc, BMc, MTc, BMTc = Am, BAm, ATm, BATm
                    pingpong = [(M2a, BM2a, MT2a, BMT2a), (M2b, BM2b, MT2b, BMT2b)]
                    for k in range(5):
                        Mn, BMn, MTn, BMTn = pingpong[k % 2]
                        pM, BpM = pp.get()
                        for c in range(8):
                            cs_ = slice(c * 64, c * 64 + 64)
                            fw.op("pe", "matmul", MM(pM[0:64, cs_], lhsT=MTc[:, c, :], rhs=Mc[:, c, :], start=True, stop=True),
                                  reads=[BMc, BMTc], writes=[BpM], sig=(c == 7))
                        fw.op("dve", "tensor_copy", dict(out=fl(Mn), in_=pM[0:64, :]), reads=[BpM], writes=[BMn])
                        yield
                        if k < 4:
                            pMT, BpMT = pp.get()
                            for c in range(8):
                                cs_ = slice(c * 64, c * 64 + 64)
                                fw.op("pe", "matmul", MM(pMT[0:64, cs_], lhsT=Mc[:, c, :], rhs=MTc[:, c, :], start=True, stop=True),
                                      reads=[BMc, BMTc], writes=[BpMT], sig=(c == 7))
                            fw.op("act", "activation", dict(out=fl(MTn), in_=pMT[0:64, :], func=AF.Copy), reads=[BpMT], writes=[BMTn])
                            yield
                        pQ, BpQ = pp.get()
                        for c in range(8):
                            cs_ = slice(c * 64, c * 64 + 64)
                            fw.op("pe", "matmul", MM(pQ[0:64, cs_], lhsT=Mn[:, c, :], rhs=QT[:, c, :], start=True, stop=True),
                                  reads=[BMn, BQT], writes=[BpQ], sig=(c == 7))
                        fw.op("dve", "tensor_tensor", dict(out=fl(QT), in0=fl(QT), in1=pQ[0:64, :], op=ALU.add), reads=[BpQ, BQT], writes=[BQT])
                        yield
                        Mc, BMc, MTc, BMTc = Mn, BMn, MTn, BMTn
                    for half in range(2):
                        pU, BpU = pp.get()
                        for c in range(4):
                            cc = half * 4 + c
                            fw.op("pe", "matmul", MM(pU[0:64, c * 128:(c + 1) * 128], lhsT=QT[:, cc, :], rhs=vb[:, cc, :], start=True, stop=True),
                                  reads=[BQT, Bvb], writes=[BpU], sig=(c == 3))
                        fw.op("dve", "tensor_copy", dict(out=U[:, half * 4:half * 4 + 4, :].rearrange("p c e -> p (c e)"), in_=pU[0:64, :]),
                              reads=[BpU], writes=[BU])
                        yield
                    pW, BpW = pp.get()
                    for c in range(8):
                        cs_ = slice(c * 64, c * 64 + 64)
                        fw.op("pe", "matmul", MM(pW[:, cs_], lhsT=kbe[:, c, :], rhs=QT[:, c, :], start=True, stop=True),
                              reads=[Bkbe, BQT], writes=[BpW], sig=(c == 7))
                    fw.op("act", "activation", dict(out=WT[:].rearrange("p c i -> p (c i)"), in_=pW[:], func=AF.Copy), reads=[BpW], writes=[BWT])
                    yield
                def rec(gi, RB, gen):
                    U, BU, WT, BWT, PT, BPT, qd, Bqd, kdec, Bkdec = RB
                    n0 = gi * 8
                    t0 = gi * TT
                    opx = (lambda *a_, **k_: None) if gi < NPRE else fw.op
                    def filler(k):
                        if gen is not None:
                            for _ in range(k):
                                next(gen, None)
                    for c in range(8):
                        n = n0 + c
                        cs_ = slice(c * 64, c * 64 + 64)
                        pWS, BpWS = pp.get()
                        fw.op("pe", "matmul", MM(pWS[0:64, 0:128], lhsT=WT[:, c, :], rhs=S[:], start=True, stop=True),
                              reads=[BWT, BS], writes=[BpWS])
                        vn, Bvn = vnr.get()
                        fw.op("dve", "tensor_tensor", dict(out=vn[:], in0=U[:, c, :], in1=pWS[0:64, 0:128], op=ALU.subtract),
                              reads=[BU, BpWS], writes=[Bvn])
                        filler(2)
                        pO, BpO = pp.get()
                        opx("pe", "matmul", MM(pO[:, 0:64], lhsT=S[:], rhs=qd[:, cs_], start=True, stop=False),
                              reads=[BS, Bqd], writes=[BpO], sig=False)
                        opx("pe", "matmul", MM(pO[:, 0:64], lhsT=vn[:], rhs=PT[:, c, :], start=False, stop=True),
                              reads=[Bvn, BPT], writes=[BpO])
                        pS, BpS = pp.get()
                        fw.op("pe", "matmul", MM(pS[:, 0:128], lhsT=kdec[:, c, :], rhs=vn[:], start=True, stop=True),
                              reads=[Bkdec, Bvn], writes=[BpS])
                        fw.op("dve", "scalar_tensor_tensor", dict(out=S[:], in0=S[:], scalar=egs[:, n, hh:hh + 1], in1=pS[:, 0:128],
                                                                                op0=ALU.mult, op1=ALU.add),
                              reads=[BS, BpS, Bg], writes=[BS])
                        filler(2)
                        opx("act", "activation", dict(out=oTh[:, t0 + c * 64:t0 + c * 64 + 64], in_=pO[:, 0:64], func=AF.Copy),
                              reads=[BpO], writes=[Bo])
                RBs = [(U, BU, WT, BWT, PT, BPT, qd, Bqd, kdec, Bkdec), (U_b, BU_b, WT_b, BWT_b, PT_b, BPT_b, qd_b, Bqd_b, kdec_b, Bkdec_b)]
                for _ in prep(0, RBs[0]):
                    pass
                for gi in range(NG):
                    gen = prep(gi + 1, RBs[(gi + 1) % 2]) if gi + 1 < NG else None
                    rec(gi, RBs[gi % 2], gen)
                    if gen is not None:
                        for _ in gen:
                            pass
                for gi in range(NPRE, NG):
                    t0 = gi * TT
                    fw.op("dve", "tensor_tensor", dict(out=sqd[:], in0=oTh[:, t0:t0 + TT], in1=oTh[:, t0:t0 + TT], op=ALU.mult),
                          reads=[Bo], writes=[Bsqd])
                    p2, Bp2 = pp.get()
                    fw.op("pe", "matmul", MM(p2[:], lhsT=ones_f[:], rhs=sqd[:], start=True, stop=True), reads=[Bsqd, Bconst], writes=[Bp2])
                    fw.op("act", "activation", dict(out=rsd[:], in_=p2[:], func=AF.Sqrt, scale=1.0 / 128, bias=EPS), reads=[Bp2], writes=[Brsd])
                    fw.op("dve", "reciprocal", dict(out=rsd[:], in_=rsd[:]), reads=[Brsd], writes=[Brsd])
                    fw.op("dve", "scalar_tensor_tensor", dict(out=sqd[:], in0=oTh[:, t0:t0 + TT], scalar=hgt[:, 0:1], in1=rsd[:], op0=ALU.mult, op1=ALU.mult),
                          reads=[Bo, Brsd, Bconst], writes=[Bsqd])
                    ys, Bys = yst.get()
                    fw.op("dve", "tensor_tensor", dict(out=ys[:], in0=sqd[:], in1=zTh[:, t0:t0 + TT], op=ALU.mult),
                          reads=[Bsqd, Bzh], writes=[Bys])
                    fw.dma("sp", btrack(Bys), yaT[hh * 128:(hh + 1) * 128, t0:t0 + TT], ys[:], reads=[Bys], writes=[Bya])
            fw.barrier()
        fw.es = es
        if stage == 3:
            fw.finish([Bya])
            print("instructions recorded:", fw.ninst)
            fw.emit()
            return nc

        with ExitStack() as esC:
            fw.es = esC
            rb = cload("rb", [32, NH], relb)
            ngs = Rot(fw, 2, [1, 512], F32, "ngs")
            pastneg = cload("pastneg", [128, 16, 16], pastneg_in)
            pastflag = cload("pastflag", [128, 16, 16], pastflag_in)
            extram = cload("extram", [128, 16, 16], extram_in)
            esel = cload("esel", [16, 16, 128], esel_in, BF16, "pool")
            ohs = Rot(fw, 2, [32, 512], F32, "ohs")
            est = Rot(fw, 2, [16, 512], F32, "est")
            for cc in range(0, LE, 512):
                w_ = min(512, LE - cc)
                oh_, Boh = ohs.get()
                fw.dma("sp", btrack(Boh), oh_[:, 0:w_], oh_in[:, cc:cc + w_], writes=[Boh])
                pe_, Bpe_ = pp.get()
                fw.op("pe", "matmul", MM(pe_[0:16, 0:w_], lhsT=rb[:], rhs=oh_[:, 0:w_], start=True, stop=False),
                      reads=[Boh, Bconst], writes=[Bpe_], sig=False)
                ng_, Bng_ = ngs.get()
                fw.dma("sp", btrack(Bng_), ng_[:, 0:w_], negm_in[:, cc:cc + w_], writes=[Bng_])
                fw.op("pe", "matmul", MM(pe_[0:16, 0:w_], lhsT=ones_f[0:1, 0:16], rhs=ng_[:, 0:w_], start=False, stop=True),
                      reads=[Bconst, Bng_], writes=[Bpe_])
                e_, Be_ = est.get()
                fw.op("act", "activation", dict(out=e_[:, 0:w_], in_=pe_[0:16, 0:w_], func=AF.Copy), reads=[Bpe_], writes=[Be_])
                fw.dma("sp", btrack(Be_), E_d[:, cc:cc + w_], e_[:, 0:w_], reads=[Be_], writes=[BE])
            qf = fw.sb("qf", [128, T], F32); kf = fw.sb("kf", [128, T], F32)
            qb = fw.sb("qb", [128, T], BF16); kb = fw.sb("kb", [128, T], BF16)
            vmh = fw.sb("vmh", [128, T // 128, 128], BF16)
            Bqf, Bkf, Bqb, Bkb, Bvmh = Buf("qf"), Buf("kf"), Buf("qb"), Buf("kb"), Buf("vmh")
            qf2 = fw.sb("qf2", [128, T], F32); kf2 = fw.sb("kf2", [128, T], F32)
            vmh2 = fw.sb("vmh2", [128, T // 128, 128], BF16)
            Bqf2, Bkf2, Bvmh2 = Buf("qf2"), Buf("kf2"), Buf("vmh2")
            hank2 = fw.sb("hank2", [128, WSH], F32); Bhank2 = Buf("hank2")
            tsh = fw.sb("tsh", [128, WSH], F32); Btsh = Buf("tsh")
            hank = fw.sb("hank", [128, WSH], F32); Bhank = Buf("hank")
            jrev = cload("jrev", [128, 128], jrev_in)
            kmean = fw.sb("kmean", [128, 16], F32); Bkmean = Buf("kmean")
            gm = fw.sb("gm", [128, 16], F32); top8 = fw.sb("top8", [128, 8], F32); mv = fw.sb("mv", [128, 16], F32)
            Bgm, Btop8, Bmv = Buf("gm"), Buf("top8"), Buf("mv")
            mvT = fw.sb("mvT", [16, T], BF16); BmvT = Buf("mvT")
            gmA = fw.sb("gmA", [128, 16, 16], F32); BgmA = Buf("gmA")
            top8A = fw.sb("top8A", [128, 16, 8], F32); Btop8A = [Buf(f"top8A{i}") for i in range(4)]
            mvA = fw.sb("mvA", [128, 16, 16], F32); BmvA = Buf("mvA")
            ssb = Rot(fw, 3, [128, 512], F32, "ssb")
            ptb = Rot(fw, 4, [128, 512], BF16, "ptb")
            rsm = fw.sb("rsm", [128, 512], F32); Brsm = Buf("rsm")
            ybs = Rot(fw, 2, [128, 512], BF16, "ybs")
            NQB = TR // 128
            po_t = fw.ps("po_t", [128, 512]); Bpo_t = Buf("po_t")
            psm_t = fw.ps("psm_t", [128, 512]); Bpsm_t = Buf("psm_t")
            pp.n = 4; pp.i = 0
            accs = [(po_t, Bpo_t, psm_t, Bpsm_t), (pp.t[4], pp.b[4], pp.t[5], pp.b[5])]
            hsets = [(qf, Bqf, kf, Bkf, vmh, Bvmh, hank, Bhank), (qf2, Bqf2, kf2, Bkf2, vmh2, Bvmh2, hank2, Bhank2)]
            def hloads(hx):
                qf_, Bqf_, kf_, Bkf_, vmh_, Bvmh_, hank_, Bhank_ = hsets[hx % 2]
                fw.dma("sp", btrack(Bqf_), qf_[:, 0:TR], qmT[hx * 128:(hx + 1) * 128, 0:TR], reads=[Bqm], writes=[Bqf_])
                fw.dma("sp", btrack(Bkf_), kf_[:, 0:TR], kmT[hx * 128:(hx + 1) * 128, 0:TR], reads=[Bkm], writes=[Bkf_])
                fw.dma("sp", btrack(Bvmh_), vmh_[:, 0:NQB, :], vm.rearrange("(n p) d -> p n d", p=128)[:, 0:NQB, hx * 128:(hx + 1) * 128],
                       reads=[Bvm], writes=[Bvmh_])
                tsrc = bass.AP(E_d.tensor, hx * LE, [[1, 128], [1, WSH]])
                fw.dma("sp", btrack(Bhank_), hank_[:], tsrc, reads=[BE], writes=[Bhank_])
            hloads(0)
            for hh in range(NH):
                qf, Bqf, kf, Bkf, vmh, Bvmh, hank, Bhank = hsets[hh % 2]
                if hh + 1 < NH:
                    hloads(hh + 1)
                for cc in range(0, WSH, 512):
                    w_ = min(512, WSH - cc)
                    pj, Bpj = pp.get()
                    fw.op("pe", "matmul", MM(pj[:, 0:w_], lhsT=jrev[:], rhs=hank[:, cc:cc + w_], start=True, stop=True),
                          reads=[Bhank, Bconst], writes=[Bpj])
                    fw.op("act", "activation", dict(out=tsh[:, cc:cc + w_], in_=pj[:, 0:w_], func=AF.Copy), reads=[Bpj], writes=[Btsh])
                fw.op("dve", "tensor_copy", dict(out=qb[:, 0:TR], in_=qf[:, 0:TR]), reads=[Bqf], writes=[Bqb])
                fw.op("act", "activation", dict(out=kb[:, 0:TR], in_=kf[:, 0:TR], func=AF.Copy), reads=[Bkf], writes=[Bkb])
                nblk = TR // 256
                fw.op("dve", "memset", MS(kmean[:], 0.0), writes=[Bkmean])
                fw.op("dve", "tensor_reduce", dict(out=kmean[:, 0:nblk], in_=kf[:, 0:TR].rearrange("p (n k) -> p n k", k=256),
                                                                 axis=AX.X, op=ALU.add), reads=[Bkf], writes=[Bkmean])
                fw.op("dve", "tensor_scalar", dict(out=kmean[:], in0=kmean[:], scalar1=1.0 / 256, scalar2=None, op0=ALU.mult),
                      reads=[Bkmean], writes=[Bkmean])
                QB0 = NPRE * 4
                NQ = NQB - QB0
                pg, Bpg = pp.get()
                for qi in range(NQ):
                    qbk = QB0 + qi
                    fw.op("pe", "matmul", MM(pg[:, qi * 16:(qi + 1) * 16], lhsT=qf[:, qbk * 128:(qbk + 1) * 128], rhs=kmean[:], start=True, stop=True),
                          reads=[Bqf, Bkmean], writes=[Bpg], sig=(qi == NQ - 1))
                def pairb(t):
                    return t[:, NPRE * 2:NPRE * 2 + NQ // 2, :].unsqueeze(2).to_broadcast([128, NQ // 2, 2, 16])
                v4 = lambda t: t[:, 0:NQ, :].rearrange("p (a two) j -> p a two j", two=2)
                fw.op("dve", "tensor_tensor", dict(out=v4(gmA), in0=pg[:, 0:NQ * 16].rearrange("p (a two j) -> p a two j", two=2, j=16),
                                                   in1=pairb(pastneg), op=ALU.add), reads=[Bpg, Bconst], writes=[BgmA])
                for qi in range(NQ):
                    fw.op("dve", "max", dict(out=top8A[:, qi, :], in_=gmA[:, qi, :]), reads=[BgmA], writes=[Btop8A[qi % 4]])
                fw.op("dve", "tensor_tensor", dict(out=mvA[:, 0:NQ, :], in0=gmA[:, 0:NQ, :], in1=top8A[:, 0:NQ, 2:3].to_broadcast([128, NQ, 16]),
                                                   op=ALU.is_ge), reads=[BgmA] + Btop8A, writes=[BmvA])
                fw.op("dve", "tensor_scalar", dict(out=mvA[:, 0:NQ, :], in0=mvA[:, 0:NQ, :], scalar1=-NEG, scalar2=NEG, op0=ALU.mult, op1=ALU.add),
                      reads=[BmvA], writes=[BmvA])
                fw.op("dve", "tensor_tensor", dict(out=v4(mvA), in0=v4(mvA), in1=pairb(pastflag), op=ALU.mult), reads=[BmvA, Bconst], writes=[BmvA])
                fw.op("dve", "tensor_tensor", dict(out=v4(mvA), in0=v4(mvA), in1=pairb(extram), op=ALU.add), reads=[BmvA, Bconst], writes=[BmvA])
                for g4 in range(0, NQ, 4):
                    pt_, Bpt_ = pp.get()
                    for c in range(4):
                        fw.op("pe", "transpose", TRP(pt_[0:16, c * 128:(c + 1) * 128], mvA[:, g4 + c, :], ident[:]),
                              reads=[BmvA, Bconst], writes=[Bpt_], sig=(c == 3))
                    fw.op("act", "activation", dict(out=mvT[:, (QB0 + g4) * 128:(QB0 + g4 + 4) * 128], in_=pt_[0:16, :], func=AF.Copy),
                          reads=[Bpt_], writes=[BmvT])
                for qt in range(NPRE, NT):
                    q0 = qt * TT
                    po, Bpo, psm, Bpsm = accs[qt % 2]
                    nkt = 4 * qt + 4
                    LAG = 2
                    pend = {}
                    for kk in range(nkt + LAG):
                        if kk < nkt:
                            kt = kk
                            k0 = kt * 128
                            dl = q0 - k0
                            sp_, Bsp_ = pp.get()
                            fw.op("pe", "matmul", MM(sp_[:], lhsT=kb[:, k0:k0 + 128], rhs=qb[:, q0:q0 + TT], start=True, stop=False),
                                  reads=[Bkb, Bqb], writes=[Bsp_], sig=False)
                            fw.op("pe", "matmul", MM(sp_[:], lhsT=esel[:, kt // 2, :], rhs=mvT[:, q0:q0 + TT], start=False, stop=True),
                                  reads=[BmvT, Bconst], writes=[Bsp_])
                            pb_, Bpb_ = ptb.get()
                            if dl >= 1024:
                                fw.op("act", "activation", dict(out=pb_[:], in_=sp_[:], func=AF.Exp, bias=tsh[:, WSH - 1:WSH], scale=1.0),
                                      reads=[Bsp_, Btsh], writes=[Bpb_])
                            else:
                                sb_, Bsb_ = ssb.get()
                                fw.op("dve", "tensor_tensor", dict(out=sb_[:], in0=sp_[:], in1=tsh[:, 513 + dl:513 + dl + TT], op=ALU.add),
                                      reads=[Bsp_, Btsh], writes=[Bsb_])
                                fw.op("act", "activation", dict(out=pb_[:], in_=sb_[:], func=AF.Exp), reads=[Bsb_], writes=[Bpb_])
                            pend[kt] = (pb_, Bpb_)
                        if kk >= LAG:
                            kt = kk - LAG
                            pb_, Bpb_ = pend.pop(kt)
                            fw.op("pe", "matmul", MM(po[:], lhsT=vmh[:, kt, :], rhs=pb_[:], start=(kt == 0), stop=(kt == nkt - 1)),
                                  reads=[Bvmh, Bpb_], writes=[Bpo], sig=False)
                            fw.op("pe", "matmul", MM(psm[:], lhsT=ones_b[:], rhs=pb_[:], start=(kt == 0), stop=(kt == nkt - 1)),
                                  reads=[Bconst, Bpb_], writes=[Bpsm, Bpo])
                    fw.op("dve", "reciprocal", dict(out=rsm[:], in_=psm[:]), reads=[Bpsm], writes=[Brsm])
                    yb_, Byb_ = ybs.get()
                    fw.op("dve", "tensor_tensor", dict(out=yb_[:], in0=po[:], in1=rsm[:], op=ALU.mult), reads=[Bpo, Brsm], writes=[Byb_])
                    fw.dma("sp", btrack(Byb_), ybT[hh * 128:(hh + 1) * 128, q0:q0 + TT], yb_[:], reads=[Byb_], writes=[Byb])
            pp.n = 6
            fw.barrier()
        fw.es = es
        if stage == 4:
            fw.finish([Byb, BE])
            print("instructions recorded:", fw.ninst)
            fw.emit()
            return nc

        with ExitStack() as esD:
            fw.es = esD
            m = alloc_main()
            gst = Rot(fw, 4, [128, TT], F32, "gst")
            m1r = Rot(fw, 2, [128, TT], F32, "m1r")
            for ti in range(NPRE, NT):
                t0 = ti * TT
                fw.dma("sp", trk["x"], m.xt[:], fmv(h1T)[:, :, t0:t0 + TT], reads=[Bh1], writes=[m.Bxt])
                fw.dma("sp", trk["l0"], m.actb[:, 0:16, :], fmv(yaT)[:, :, t0:t0 + TT], reads=[Bya], writes=[m.Bact])
                fw.dma("sp", trk["l1"], m.actb[:, 16:32, :], fmv(ybT)[:, :, t0:t0 + TT], reads=[Byb], writes=[m.Bact])
                for blk in range(D // 256):
                    wat, wab = wload(m.wsA, "wpa", D // 256, blk, wv(wpa)[:, :, blk * 256:(blk + 1) * 256])
                    wbt, wbb = wload(m.wsA, "wpb", D // 256, blk, wv(wpb)[:, :, blk * 256:(blk + 1) * 256])
                    for j in range(2):
                        dc = blk * 2 + j
                        pa, Bpa = pp.get(); pb, Bpb = pp.get()
                        for kc in range(KC):
                            fw.op("pe", "matmul", MM(
                                pa[:], lhsT=wat[:, kc, j * 128:(j + 1) * 128], rhs=m.actb[:, kc, :], start=(kc == 0), stop=(kc == KC - 1)),
                                reads=[wab, m.Bact], writes=[Bpa], sig=(kc == KC - 1))
                        for kc in range(KC):
                            fw.op("pe", "matmul", MM(
                                pb[:], lhsT=wbt[:, kc, j * 128:(j + 1) * 128], rhs=m.actb[:, 16 + kc, :], start=(kc == 0), stop=(kc == KC - 1)),
                                reads=[wbb, m.Bact], writes=[Bpb], sig=(kc == KC - 1))
                        ga_, Bga_ = gst.get(); gb_, Bgb_ = gst.get()
                        fw.dma("sp", btrack(Bga_), ga_[:], gaT[dc * 128:(dc + 1) * 128, t0:t0 + TT], reads=[Bga], writes=[Bga_])
                        fw.dma("sp", btrack(Bgb_), gb_[:], gbT[dc * 128:(dc + 1) * 128, t0:t0 + TT], reads=[Bgb], writes=[Bgb_])
                        m1, Bm1 = m1r.get()
                        fw.op("dve", "tensor_tensor", dict(out=m1[:], in0=pa[:], in1=ga_[:], op=ALU.mult),
                              reads=[Bpa, Bga_], writes=[Bm1])
                        fw.op("dve", "tensor_tensor", dict(out=gb_[:], in0=pb[:], in1=gb_[:], op=ALU.mult),
                              reads=[Bpb, Bgb_], writes=[Bgb_])
                        fw.op("dve", "tensor_tensor", dict(out=m.ub[:, dc, :], in0=gb_[:], in1=m1[:], op=ALU.add),
                              reads=[Bgb_, Bm1], writes=[m.Bub])
                G2 = mod[:, 5 * KC:6 * KC]
                for blk in range(D // 256):
                    wot, wob = wload(m.wsA, "wout", D // 256, blk, wv(wout)[:, :, blk * 256:(blk + 1) * 256])
                    for j in range(2):
                        dc = blk * 2 + j
                        po, Bpo = pp.get()
                        for kc in range(KC):
                            fw.op("pe", "matmul", MM(
                                po[:], lhsT=wot[:, kc, j * 128:(j + 1) * 128], rhs=m.ub[:, kc, :], start=(kc == 0), stop=(kc == KC - 1)),
                                reads=[wob, m.Bub], writes=[Bpo], sig=(kc == KC - 1))
                        fw.op("dve", "scalar_tensor_tensor", dict(
                            out=m.xt[:, dc, :], in0=po[:], scalar=G2[:, dc:dc + 1], in1=m.xt[:, dc, :], op0=ALU.mult, op1=ALU.add),
                            reads=[Bpo, Bmod, m.Bxt], writes=[m.Bxt])
                rmsnorm_mod(m, 2)
                ffn(m, wv(w1b), wv(w3b), wv(w2b), 2, "f2")
                fw.dma("sp", trk["o"], fmv(outT)[:, :, t0 - NPRE * TT:t0 - NPRE * TT + TT], m.xt[:], reads=[m.Bxt], writes=[Bout])
            fw.finish([Bout])
        fw.es = es
        print("instructions recorded:", fw.ninst)
        fw.emit()
    return nc


def t5_bucket_np(d):
    f = np.float32
    dd = np.maximum(d, 1).astype(f)
    large = 16 + (np.log(dd / f(16)) / f(math.log(64)) * f(16)).astype(np.int32)
    large = np.minimum(large, 31)
    return np.where(d < 16, d, large)


_CONST = {}


def consts():
    if _CONST:
        return _CONST
    f = np.float32
    c = _CONST
    c["ones"] = np.ones((128, 128), f)
    c["ident"] = np.eye(128, dtype=f)
    c["jrev"] = np.ascontiguousarray(np.eye(128, dtype=f)[::-1])
    idx = np.arange(64)
    c["tri"] = (idx[:, None] <= idx[None, :]).astype(f)
    c["mstrict"] = (idx[:, None] > idx[None, :]).astype(f)
    dist = np.arange(LE, dtype=np.int64) - 640
    bk = t5_bucket_np(np.maximum(dist, 0).astype(np.int32))
    oh = np.zeros((32, LE), f)
    valid = dist >= 0
    oh[bk[valid], np.nonzero(valid)[0]] = 1.0
    c["oh"] = oh
    c["negm"] = np.where(valid, 0.0, NEG).astype(f)[None, :]
    j = np.arange(16)
    qb = np.arange(16)
    past = (j[None, :] < qb[:, None])
    c["pastneg"] = np.broadcast_to(np.where(past, 0.0, -1e30).astype(f)[None], (128, 16, 16)).copy()
    c["pastflag"] = np.broadcast_to(past.astype(f)[None], (128, 16, 16)).copy()
    es_ = np.zeros((16, 16, 128), f)
    for jj in range(16):
        es_[jj, jj, :] = 1.0
    c["esel"] = es_
    return c


def prep_shared(inputs):
    f = np.float32
    def fm(v):
        return np.ascontiguousarray(np.asarray(v, f).reshape(-1, 128).T)
    m = dict(consts())
    m["ada_w"] = np.asarray(inputs["ada_w"][0], f)
    m["ada_bT"] = fm(inputs["ada_b"][0])
    m["ngT"] = np.concatenate([fm(inputs["norm1_g"][0]), fm(inputs["norm2_g"][0]), fm(inputs["norm3_g"][0])], axis=1)
    for k in ("ffn1_w1", "ffn1_w3", "ffn1_w2", "ffn2_w1", "ffn2_w3", "ffn2_w2"):
        m[k] = np.asarray(inputs[k][0], f)
    w = np.asarray(inputs["w_in"][0], f)
    m["winp"] = np.ascontiguousarray(np.concatenate([w[:, 0:8192], w[:, 8224:18464], w[:, 8192:8224]], axis=1))
    cw = np.asarray(inputs["dn_conv_w"][0], f)
    m["convw"] = np.ascontiguousarray(cw.T.reshape(48, 128, 4).transpose(1, 0, 2))
    m["alog"] = np.asarray(inputs["dn_a_log"], f).reshape(1, NH)
    m["dtb"] = np.asarray(inputs["dn_dt_bias"], f).reshape(1, NH)
    m["hg"] = np.ascontiguousarray(np.stack([np.asarray(inputs["dn_norm_g"][0], f), np.asarray(inputs["mb_q_norm_g"][0], f),
                                             np.asarray(inputs["mb_k_norm_g"][0], f)], axis=1))
    m["relb"] = np.asarray(inputs["rel_bias"], f)
    m["wpa"] = np.asarray(inputs["w_proj_a"][0], f)
    m["wpb"] = np.asarray(inputs["w_proj_b"][0], f)
    m["wout"] = np.asarray(inputs["w_out"][0], f)
    return m


def seq_masks(sidx, npre_blk):
    f = np.float32
    j = np.arange(16)
    qb = np.arange(16)
    past = (j[None, :] < qb[:, None])
    extra = np.zeros((16, 16), f)
    if sidx == 0:
        past = past & (j[None, :] >= npre_blk)
        extra[:, :npre_blk] = NEG
    rep = lambda a: np.broadcast_to(a[None], (128, 16, 16)).copy()
    return (rep(np.where(past, 0.0, -1e30).astype(f)), rep(past.astype(f)), rep(extra))


def prep_core(inputs, shared, b, sidx, NT=NTILE):
    f = np.float32
    m = dict(shared)
    npre = NT // 2
    half = npre * TT
    xb = np.asarray(inputs["x"][b], f)
    xT = np.zeros((D, T), f)
    if sidx == 1:
        xT[:, 0:2 * half] = xb[0:2 * half].T
    else:
        xT[:, half:2 * half] = xb[0:half].T
    m["xT"] = xT
    m["cT"] = np.ascontiguousarray(np.asarray(inputs["c"][b], f).reshape(-1, 128).T)
    m["flagv"] = np.full((128, 1), float(sidx), f)
    m["pastneg"], m["pastflag"], m["extram"] = seq_masks(sidx, npre * 2)
    return m


_NC = {}


def kernel(**inputs):
    if "nc" not in _NC:
        _NC["nc"] = build(99, NTILE)
    nc = _NC["nc"]
    shared = prep_shared(inputs)
    maps = [prep_core(inputs, shared, c % 4, c // 4) for c in range(8)]
    res = run_bass_kernel_spmd(nc, maps, core_ids=list(range(8)))
    out = np.empty((4, T, D), np.float32)
    h = T // 2
    for c in range(8):
        out[c % 4, (c // 4) * h:(c // 4 + 1) * h, :] = res.results[c]["outT"].T
    return out
```

```python
import math
import numpy as np
from concourse.bass_utils import run_bass_kernel_spmd
import numpy as np
import concourse.bass as bass
import concourse.mybir as mybir
from contextlib import ExitStack

F32 = mybir.dt.float32
BF16 = mybir.dt.bfloat16
AF = mybir.ActivationFunctionType
ALU = mybir.AluOpType
AX = mybir.AxisListType

ENGS = ("pe", "act", "dve", "pool", "sp")


class Buf:
    __slots__ = ("name", "w", "r", "multi")

    def __init__(self, name="", multi=False):
        self.name = name
        self.multi = multi
        self.w = None
        self.r = {}


class Track:
    def __init__(self, sem, name):
        self.sem = sem
        self.n = 0
        self.name = name


class FW:
    def __init__(self, nc, es):
        self.nc = nc
        self.es = es
        self.es_glob = es
        self.h = {"pe": nc.tensor, "act": nc.scalar, "dve": nc.vector, "pool": nc.gpsimd, "sp": nc.sync}
        self.sem = {e: es.enter_context(nc.semaphore("sem_" + e)) for e in ENGS}
        self.cnt = {e: 0 for e in ENGS}
        self.seen = {e: {} for e in ENGS}
        self.prog = {e: [] for e in ENGS}
        self.segs = []
        self.tracks = []
        self.ninst = 0

    def track(self, name):
        t = Track(self.es_glob.enter_context(self.nc.semaphore("trk_" + name)), name)
        self.tracks.append(t)
        return t

    def sb(self, name, shape, dt):
        self.uid = getattr(self, "uid", 0) + 1
        return self.es.enter_context(self.nc.sbuf_tensor(f"{name}_{self.uid}", list(shape), dt))

    def ps(self, name, shape, dt=F32):
        return self.es.enter_context(self.nc.psum_tensor(name, list(shape), dt))

    def _deps(self, eng, reads, writes):
        deps = {}
        def add(d):
            if d is None:
                return
            kind, key, val = d
            if kind == 'e' and key == eng and eng in ('pe', 'sp'):
                return
            k = (kind, key if kind == 'e' else id(key))
            if k not in deps or deps[k][2] < val:
                deps[k] = d
        for b in reads:
            add(b.w)
        for b in writes:
            if b.multi:
                continue
            add(b.w)
            for d in b.r.values():
                add(d)
        waits = []
        seen = self.seen[eng]
        for k, (kind, key, val) in deps.items():
            if seen.get(k, 0) >= val:
                continue
            seen[k] = val
            sem = self.sem[key] if kind == 'e' else key.sem
            waits.append((sem, val))
        return waits

    def op(self, eng, name, kw, reads=(), writes=(), sig=True):
        waits = self._deps(eng, reads, writes)
        sem = self.sem[eng]
        fn = lambda h, name=name, kw=kw: getattr(h, name)(**kw)
        if sig:
            self.cnt[eng] += 1
            idx = self.cnt[eng]
            def run(h, fn=fn, waits=waits, sem=sem):
                for (s, v) in waits:
                    h.wait_ge(s, v)
                fn(h).then_inc(sem, 1)
        else:
            idx = self.cnt[eng] + 1
            def run(h, fn=fn, waits=waits):
                for (s, v) in waits:
                    h.wait_ge(s, v)
                fn(h)
        self.prog[eng].append(run)
        tok = ('e', eng, idx)
        for b in reads:
            b.r[('e', eng)] = tok
        for b in writes:
            b.w = tok
            b.r = {}
        self.ninst += 1
        return tok

    def dma(self, q, track, out, in_, reads=(), writes=()):
        if callable(out) or callable(in_):
            q = "pool"
        waits = self._deps(q, reads, writes)
        track.n += 16
        val = track.n
        def run(h, waits=waits, out=out, in_=in_, sem=track.sem):
            for (s, v) in waits:
                h.wait_ge(s, v)
            o_ = out(h) if callable(out) else out
            i_ = in_(h) if callable(in_) else in_
            h.dma_start(out=o_, in_=i_).then_inc(sem, 16)
        self.prog[q].append(run)
        tok = ('d', track, val)
        for b in reads:
            b.r[('d', id(track))] = tok
        for b in writes:
            b.w = tok
            b.r = {}
        self.ninst += 1
        return tok

    def barrier(self):
        for e in ENGS:
            waits = []
            seen = self.seen[e]
            for e2 in ENGS:
                if e2 == e or self.cnt[e2] == 0:
                    continue
                k = ('e', e2)
                if seen.get(k, 0) < self.cnt[e2]:
                    seen[k] = self.cnt[e2]
                    waits.append((self.sem[e2], self.cnt[e2]))
            for t in self.tracks:
                k = ('d', id(t))
                if t.n > 0 and seen.get(k, 0) < t.n:
                    seen[k] = t.n
                    waits.append((t.sem, t.n))
            def run(h, waits=waits):
                for (s, v) in waits:
                    h.wait_ge(s, v)
            self.prog[e].append(run)

    def core_barrier(self):
        self.barrier()
        self.segs.append(self.prog)
        self.prog = {e: [] for e in ENGS}

    def finish(self, out_bufs):
        waits = self._deps("sp", out_bufs, ())
        def run(h, waits=waits):
            for (s, v) in waits:
                h.wait_ge(s, v)
        self.prog["sp"].append(run)

    def emit(self):
        nc = self.nc
        segs = self.segs + [self.prog]
        for si, prog in enumerate(segs):
            self.cur_seg = si
            if si > 0:
                nc.all_core_barrier()
            with nc.Block() as block:
                @block.tensor
                def _(h, prog=prog):
                    for f in prog["pe"]:
                        f(h)
                @block.scalar
                def _(h, prog=prog):
                    for f in prog["act"]:
                        f(h)
                @block.vector
                def _(h, prog=prog):
                    for f in prog["dve"]:
                        f(h)
                @block.gpsimd
                def _(h, prog=prog):
                    for f in prog["pool"]:
                        f(h)
                @block.sync
                def _(h, prog=prog):
                    for f in prog["sp"]:
                        f(h)


D = 2048
T = 4096
FF = 5632
KC = D // 128
FFC = FF // 128
TT = 512
NTILE = T // TT
EPS = 1e-6
NH = 16
NCH = T // 64
WP = 18432
LE = 4736
WSH = 4609
NEG = -30000.0


def MM(out, **kw):
    return dict(out=out, **kw)


def TRP(out, in_, identity):
    return dict(out=out, in_=in_, identity=identity)


def MS(ap, constant):
    return dict(ap=ap, constant=constant)


class PsumPool:
    def __init__(self, fw, n, name="ps"):
        self.t = [fw.ps(f"{name}{i}", [128, 512]) for i in range(n)]
        self.b = [Buf(f"{name}{i}") for i in range(n)]
        self.i = 0
        self.n = n

    def get(self):
        i = self.i
        self.i = (self.i + 1) % self.n
        return self.t[i], self.b[i]


class Rot:
    def __init__(self, fw, n, shape, dt, name):
        self.t = [fw.sb(f"{name}{i}", shape, dt) for i in range(n)]
        self.b = [Buf(f"{name}{i}") for i in range(n)]
        self.i = 0
        self.n = n

    def get(self):
        i = self.i
        self.i = (self.i + 1) % self.n
        return self.t[i], self.b[i]


class WSlots:
    def __init__(self, fw, trks, shape, name, dt=BF16, q="pool"):
        n = len(trks)
        self.fw = fw
        self.t = [fw.sb(f"{name}{i}", shape, dt) for i in range(n)]
        self.b = [Buf(f"{name}{i}") for i in range(n)]
        self.trk = trks
        self.i = 0
        self.n = n
        self.q = q

    def load(self, src_ap, dst_slice=None):
        i = self.i
        self.i = (self.i + 1) % self.n
        dst = self.t[i][:] if dst_slice is None else dst_slice(self.t[i])
        self.fw.dma(self.q, self.trk[i], dst, src_ap, writes=[self.b[i]])
        return self.t[i], self.b[i]


def build(stage=99, NT=NTILE):
    NPRE = NT // 2
    nc = bass.Bass("TRN2", target_bir_lowering=False)
    def din(name, shape, dt=F32):
        return nc.dram_tensor(name, list(shape), dt, kind="ExternalInput").ap()
    dbg = stage != 99
    def dscr(name, shape, dt=F32):
        return nc.dram_tensor(name, list(shape), dt, kind=("ExternalOutput" if dbg else "Internal")).ap()
    xT = din("xT", [D, T])
    cT = din("cT", [128, KC])
    ada_w = din("ada_w", [D, 9 * D])
    ada_bT = din("ada_bT", [128, 9 * KC])
    ngT = din("ngT", [128, 3 * KC])
    w1a = din("ffn1_w1", [D, FF]); w3a = din("ffn1_w3", [D, FF]); w2a = din("ffn1_w2", [FF, D])
    w1b = din("ffn2_w1", [D, FF]); w3b = din("ffn2_w3", [D, FF]); w2b = din("ffn2_w2", [FF, D])
    winp = din("winp", [D, WP + 32])
    convw = din("convw", [128, 48, 4])
    alog = din("alog", [1, NH]); dtb = din("dtb", [1, NH])
    hg = din("hg", [128, 3])
    relb = din("relb", [32, NH])
    wpa = din("wpa", [D, D]); wpb = din("wpb", [D, D]); wout = din("wout", [D, D])
    ones_in = din("ones", [128, 128]); ident_in = din("ident", [128, 128])
    tri_in = din("tri", [64, 64]); mstrict_in = din("mstrict", [64, 64])
    oh_in = din("oh", [32, LE]); negm_in = din("negm", [1, LE])
    pastneg_in = din("pastneg", [128, 16, 16]); pastflag_in = din("pastflag", [128, 16, 16])
    esel_in = din("esel", [16, 16, 128])
    jrev_in = din("jrev", [128, 128])
    flag_in = din("flagv", [128, 1])
    extram_in = din("extram", [128, 16, 16])
    outT = nc.dram_tensor("outT", [D, T // 2], F32, kind="ExternalOutput").ap()
    h1T = dscr("h1T", [D, T]); qkvT = dscr("qkvT", [3 * D, T]); zT = dscr("zT", [D, T])
    qmT = dscr("qmT", [D, T]); kmT = dscr("kmT", [D, T]); vm = dscr("vm", [T, D], BF16)
    ba = dscr("ba", [T, 32]); gaT = dscr("gaT", [D, T]); gbT = dscr("gbT", [D, T])
    yaT = dscr("yaT", [D, T], BF16); ybT = dscr("ybT", [D, T], BF16)
    E_d = dscr("E_d", [NH, LE])
    Bh1, Bqkv, Bz, Bqm, Bkm, Bvm, Bba, Bga, Bgb, Bya, Byb, BE, Bout = [Buf(n, multi=(n != "E")) for n in
        ("h1", "qkv", "z", "qm", "km", "vm", "ba", "ga", "gb", "ya", "yb", "E", "out")]

    def fmv(ap):
        return ap.rearrange("(c p) t -> p c t", p=128)

    es = ExitStack()
    with es:
        fw = FW(nc, es)
        trk = {n: fw.track(n) for n in ("const", "x", "o", "a0", "a1", "a2", "a3", "b0", "b1", "w0", "w1",
                                         "l0", "l1", "l2", "l3", "l4", "s0", "s1", "s2")}
        pp = PsumPool(fw, 6)
        trk_of = {}
        def btrack(B):
            if id(B) not in trk_of:
                trk_of[id(B)] = (fw.track(f"bt{len(trk_of)}"), B)
            return trk_of[id(B)][0]
        caches = {}
        def wload(slots, key, nblk, idx, src_ap):
            shp = slots.t[0].shape
            elems = int(shp[1]) * int(shp[2])
            if key not in caches:
                caches[key] = (nc.dram_tensor("wc_" + key, [nblk, 128, elems], BF16, kind="Internal").ap(),
                               [Buf(f"wc_{key}{i}") for i in range(nblk)], [False] * nblk)
            cd, cb, filled = caches[key]
            cview = cd[idx].rearrange("p (c n) -> p c n", c=int(shp[1]))
            if filled[idx]:
                i = slots.i
                slots.i = (slots.i + 1) % slots.n
                fw.dma(slots.q, slots.trk[i], slots.t[i][:], cview, reads=[cb[idx]], writes=[slots.b[i]])
                return slots.t[i], slots.b[i]
            t_, b_ = slots.load(src_ap)
            fw.dma("sp", btrack(b_), cview, t_[:], reads=[b_], writes=[cb[idx]])
            filled[idx] = True
            return t_, b_
        Bconst = Buf("const")
        cjoin = fw.sb("cjoin", [1, 8], F32); Bcjoin = Buf("cjoin")
        trk["constp"] = fw.track("constp")
        Bconstp = Buf("constp")
        def cload(name, shape, src, dt=F32, q="sp"):
            t = fw.sb(name, shape, dt)
            if q == "pool":
                fw.dma(q, trk["constp"], t[:], src, writes=[Bconstp])
                fw.op("dve", "memset", MS(cjoin[:], 0.0), reads=[Bconstp], writes=[Bconst, Bcjoin])
            else:
                fw.dma(q, trk["const"], t[:], src, writes=[Bconst])
            return t
        ones_f = cload("ones_f", [128, 128], ones_in)
        ident = cload("ident", [128, 128], ident_in)
        ones_b = cload("ones_b", [128, 128], ones_in, BF16, "pool")
        hgt = cload("hgt", [128, 3], hg)
        flagv = cload("flagv", [128, 1], flag_in)
        adaT = fw.sb("adaT", [128, 9 * KC], F32); Bada = Buf("ada")
        mod = fw.sb("mod", [128, 9 * KC], F32); Bmod = Buf("mod")
        qgs = fw.sb("qgs", [128, 1], F32)
        fw.op("dve", "tensor_scalar", dict(out=qgs[:], in0=hgt[:, 1:2], scalar1=128.0 ** -0.5, scalar2=None, op0=ALU.mult),
              reads=[Bconst], writes=[Bconst])
        with ExitStack() as es0:
            fw.es = es0
            cs = fw.sb("cs", [128, KC], F32); Bcs = Buf("cs")
            abT = fw.sb("abT", [128, 9 * KC], F32)
            gT = fw.sb("gT", [128, 3 * KC], F32)
            fw.dma("sp", trk["const"], cs[:], cT, writes=[Bcs])
            fw.dma("sp", trk["const"], abT[:], ada_bT, writes=[Bconst])
            fw.dma("sp", trk["const"], gT[:], ngT, writes=[Bconst])
            fw.op("act", "activation", dict(out=cs[:], in_=cs[:], func=AF.Silu), reads=[Bcs], writes=[Bcs])
            aw = WSlots(fw, [trk["w0"], trk["w1"]], [128, KC, 512], "aw", dt=F32, q="sp")
            pada = fw.ps("pada", [128, 9 * KC]); Bpada = Buf("pada")
            awv = ada_w.rearrange("(c p) n -> p c n", p=128)
            arow = fw.sb("arow", [1, 9 * D], F32); Barow = Buf("arow")
            for blk in range(9 * D // 512):
                wt, wb = aw.load(awv[:, :, blk * 512:(blk + 1) * 512])
                prow, Bprow = pp.get()
                for kc in range(KC):
                    fw.op("pe", "matmul", MM(prow[0:1, :], lhsT=cs[:, kc:kc + 1], rhs=wt[:, kc, :], start=(kc == 0), stop=(kc == KC - 1)),
                          reads=[wb, Bcs], writes=[Bprow], sig=(kc == KC - 1))
                fw.op("act", "activation", dict(out=arow[0:1, blk * 512:(blk + 1) * 512], in_=prow[0:1, :], func=AF.Copy),
                      reads=[Bprow], writes=[Barow])
            for col in range(9 * KC):
                fw.op("pe", "matmul", MM(pada[:, col:col + 1], lhsT=arow[0:1, col * 128:(col + 1) * 128], rhs=ones_f[0:1, 0:1], start=True, stop=True),
                      reads=[Barow, Bconst], writes=[Bpada], sig=(col == 9 * KC - 1))
            fw.op("dve", "tensor_tensor", dict(out=adaT[:], in0=pada[:], in1=abT[:], op=ALU.add),
                  reads=[Bpada, Bconst], writes=[Bada])
            for s in range(3):
                sh = adaT[:, (3 * s) * KC:(3 * s + 1) * KC]
                sc = adaT[:, (3 * s + 1) * KC:(3 * s + 2) * KC]
                gt = adaT[:, (3 * s + 2) * KC:(3 * s + 3) * KC]
                A = mod[:, (3 * s) * KC:(3 * s + 1) * KC]
                Bv = mod[:, (3 * s + 1) * KC:(3 * s + 2) * KC]
                G = mod[:, (3 * s + 2) * KC:(3 * s + 3) * KC]
                gn = gT[:, s * KC:(s + 1) * KC]
                fw.op("dve", "scalar_tensor_tensor", dict(
                    out=A, in0=sc, scalar=1.0, in1=gn, op0=ALU.add, op1=ALU.mult), reads=[Bada, Bconst], writes=[Bmod])
                fw.op("dve", "tensor_copy", dict(out=Bv, in_=sh), reads=[Bada], writes=[Bmod])
                fw.op("dve", "tensor_scalar", dict(
                    out=G, in0=gt, scalar1=(1.0 if s == 1 else 0.5), scalar2=None, op0=ALU.mult), reads=[Bada], writes=[Bmod])
            fw.barrier()
        fw.es = es

        class Main:
            pass

        def alloc_main():
            m = Main()
            m.xt = fw.sb("xt", [128, KC, TT], F32); m.Bxt = Buf("xt")
            m.ub = fw.sb("ub", [128, KC, TT], BF16); m.Bub = Buf("ub")
            m.actb = fw.sb("actb", [128, FFC, TT], BF16); m.Bact = Buf("act")
            m.sq = fw.sb("sq", [128, TT], BF16); m.Bsq = Buf("sq")
            m.rstd = fw.sb("rstd", [128, TT], F32); m.Brstd = Buf("rstd")
            m.tmp = Rot(fw, 3, [128, TT], F32, "tmp")
            m.wsA = WSlots(fw, [trk["a0"], trk["a1"], trk["a2"], trk["a3"]], [128, KC, 256], "wsA")
            m.wsB = WSlots(fw, [trk["b0"], trk["b1"]], [128, FFC, 128], "wsB")
            return m

        def rmsnorm_mod(m, s):
            src, Bsrc = m.xt, m.Bxt
            pss, Bpss = pp.get()
            for c in range(KC):
                fw.op("act", "activation", dict(out=m.sq[:], in_=src[:, c, :], func=AF.Square),
                      reads=[Bsrc], writes=[m.Bsq])
                fw.op("pe", "matmul", MM(pss[:], lhsT=ones_b[:], rhs=m.sq[:], start=(c == 0), stop=(c == KC - 1)),
                      reads=[m.Bsq, Bconst], writes=[Bpss])
            fw.op("act", "activation", dict(out=m.rstd[:], in_=pss[:], func=AF.Sqrt, scale=1.0 / D, bias=EPS),
                  reads=[Bpss], writes=[m.Brstd])
            fw.op("dve", "reciprocal", dict(out=m.rstd[:], in_=m.rstd[:]), reads=[m.Brstd], writes=[m.Brstd])
            A = mod[:, (3 * s) * KC:(3 * s + 1) * KC]
            Bv = mod[:, (3 * s + 1) * KC:(3 * s + 2) * KC]
            for c in range(KC):
                tb, Btb = m.tmp.get()
                fw.op("dve", "scalar_tensor_tensor", dict(
                    out=tb[:], in0=src[:, c, :], scalar=A[:, c:c + 1], in1=m.rstd[:], op0=ALU.mult, op1=ALU.mult),
                    reads=[Bsrc, m.Brstd, Bmod], writes=[Btb])
                fw.op("act", "activation", dict(out=m.ub[:, c, :], in_=tb[:], func=AF.Identity,
                                                             bias=Bv[:, c:c + 1], scale=1.0),
                      reads=[Btb, Bmod], writes=[m.Bub])

        def ffn(m, wv1, wv3, wv2, s, ck=""):
            G = mod[:, (3 * s + 2) * KC:(3 * s + 3) * KC]
            for blk in range(FF // 256):
                w1t, w1bb = wload(m.wsA, ck + "w1", FF // 256, blk, wv1[:, :, blk * 256:(blk + 1) * 256])
                w3t, w3bb = wload(m.wsA, ck + "w3", FF // 256, blk, wv3[:, :, blk * 256:(blk + 1) * 256])
                for j in range(2):
                    ffc = blk * 2 + j
                    p1, Bp1 = pp.get()
                    p3, Bp3 = pp.get()
                    for kc in range(KC):
                        fw.op("pe", "matmul", MM(
                            p1[:], lhsT=w1t[:, kc, j * 128:(j + 1) * 128], rhs=m.ub[:, kc, :], start=(kc == 0), stop=(kc == KC - 1)),
                            reads=[w1bb, m.Bub], writes=[Bp1], sig=(kc == KC - 1))
                    for kc in range(KC):
                        fw.op("pe", "matmul", MM(
                            p3[:], lhsT=w3t[:, kc, j * 128:(j + 1) * 128], rhs=m.ub[:, kc, :], start=(kc == 0), stop=(kc == KC - 1)),
                            reads=[w3bb, m.Bub], writes=[Bp3], sig=(kc == KC - 1))
                    tb, Btb = m.tmp.get()
                    fw.op("act", "activation", dict(out=tb[:], in_=p1[:], func=AF.Silu),
                          reads=[Bp1], writes=[Btb])
                    fw.op("dve", "tensor_tensor", dict(
                        out=m.actb[:, ffc, :], in0=tb[:], in1=p3[:], op=ALU.mult), reads=[Btb, Bp3], writes=[m.Bact])
            for dc in range(KC):
                w2t, w2bb = wload(m.wsB, ck + "w2", KC, dc, wv2[:, :, dc * 128:(dc + 1) * 128])
                po, Bpo = pp.get()
                for fc in range(FFC):
                    fw.op("pe", "matmul", MM(
                        po[:], lhsT=w2t[:, fc, :], rhs=m.actb[:, fc, :], start=(fc == 0), stop=(fc == FFC - 1)),
                        reads=[w2bb, m.Bact], writes=[Bpo], sig=(fc == FFC - 1))
                fw.op("dve", "scalar_tensor_tensor", dict(
                    out=m.xt[:, dc, :], in0=po[:], scalar=G[:, dc:dc + 1], in1=m.xt[:, dc, :], op0=ALU.mult, op1=ALU.add),
                    reads=[Bpo, Bmod, m.Bxt], writes=[m.Bxt])

        xTv = fmv(xT)
        wv = lambda w: w.rearrange("(c p) n -> p c n", p=128)

        with ExitStack() as esA:
            fw.es = esA
            m = alloc_main()
            cw = fw.sb("cw", [128, 48, 4], F32)
            fw.dma("sp", trk["const"], cw[:], convw, writes=[Bconst])
            halo = fw.sb("halo", [128, 48, 3], F32); Bhalo = Buf("halo")
            fw.op("dve", "memset", MS(halo[:], 0.0), writes=[Bhalo])
            cbuf = Rot(fw, 2, [128, TT + 3], F32, "cbuf")
            acc = Rot(fw, 2, [128, TT], F32, "acc")
            ost = Rot(fw, 4, [128, TT], F32, "ost")
            vst = fw.sb("vst", [128, 4, D], BF16); Bvst = Buf("vst")
            bast = fw.sb("bast", [128, 4, 32], F32); Bbast = Buf("bast")
            wba = fw.sb("wba", [128, KC, 32], BF16); Bwba = Buf("wba")
            fw.dma("pool", trk["constp"], wba[:], wv(winp)[:, :, WP:WP + 32], writes=[Bwba])
            stq = ["s0", "s1", "s2"]
            sti = [0]
            def store(dst, src, Bsrc, Bdst):
                fw.dma("sp", btrack(Bsrc), dst, src, reads=[Bsrc], writes=[Bdst])
            winv = wv(winp)
            fw.dma("sp", trk["x"], m.xt[:], xTv[:, :, 0:TT], writes=[m.Bxt])
            for ti in range(NT):
                t0 = ti * TT
                rmsnorm_mod(m, 0)
                ffn(m, wv(w1a), wv(w3a), wv(w2a), 0, "f1")
                pre = ti < NPRE
                if not pre:
                    store(fmv(h1T)[:, :, t0:t0 + TT], m.xt[:], m.Bxt, Bh1)
                rmsnorm_mod(m, 1)
                if ti + 1 < NT:
                    fw.dma("sp", trk["x"], m.xt[:], xTv[:, :, t0 + TT:t0 + 2 * TT], writes=[m.Bxt])
                for blk in range(WP // 256):
                    if pre and not (8 <= blk < 24 or 40 <= blk < 56 or (blk < 8 and ti == NPRE - 1)):
                        continue
                    wt, wb = wload(m.wsA, "win", WP // 256, blk, winv[:, :, blk * 256:(blk + 1) * 256])
                    if 48 <= blk < 56:
                        for tb in range(4):
                            pv, Bpv = pp.get()
                            for kc in range(KC):
                                fw.op("pe", "matmul", MM(
                                    pv[:, 0:256], lhsT=m.ub[:, kc, tb * 128:(tb + 1) * 128], rhs=wt[:, kc, :],
                                    start=(kc == 0), stop=(kc == KC - 1)),
                                    reads=[wb, m.Bub], writes=[Bpv], sig=(kc == KC - 1))
                            c0 = (blk - 48) * 256
                            fw.op("act", "activation", dict(
                                out=vst[:, tb, c0:c0 + 256], in_=pv[:, 0:256], func=AF.Copy), reads=[Bpv], writes=[Bvst])
                        continue
                    for j in range(2):
                        oc = blk * 2 + j
                        p1, Bp1 = pp.get()
                        for kc in range(KC):
                            fw.op("pe", "matmul", MM(
                                p1[:], lhsT=wt[:, kc, j * 128:(j + 1) * 128], rhs=m.ub[:, kc, :], start=(kc == 0), stop=(kc == KC - 1)),
                                reads=[wb, m.Bub], writes=[Bp1], sig=(kc == KC - 1))
                        if oc < 48:
                            cb, Bcb = cbuf.get()
                            fw.op("act", "activation", dict(out=cb[:, 0:3], in_=halo[:, oc, :], func=AF.Copy),
                                  reads=[Bhalo], writes=[Bcb])
                            fw.op("act", "activation", dict(out=cb[:, 3:TT + 3], in_=p1[:], func=AF.Copy),
                                  reads=[Bp1], writes=[Bcb])
                            fw.op("dve", "tensor_copy", dict(out=halo[:, oc, :], in_=cb[:, TT:TT + 3]),
                                  reads=[Bcb], writes=[Bhalo])
                            ac, Bac = acc.get()
                            fw.op("dve", "tensor_scalar", dict(
                                out=ac[:], in0=cb[:, 0:TT], scalar1=cw[:, oc, 0:1], scalar2=None, op0=ALU.mult),
                                reads=[Bcb, Bconst], writes=[Bac])
                            for jj in range(1, 4):
                                fw.op("dve", "scalar_tensor_tensor", dict(
                                    out=ac[:], in0=cb[:, jj:jj + TT], scalar=cw[:, oc, jj:jj + 1], in1=ac[:], op0=ALU.mult, op1=ALU.add),
                                    reads=[Bcb, Bconst, Bac], writes=[Bac])
                            o_, Bo_ = ost.get()
                            if oc >= 32:
                                fw.op("act", "activation", dict(out=o_[:], in_=ac[:], func=AF.Silu),
                                      reads=[Bac], writes=[Bo_])
                            else:
                                fw.op("act", "activation", dict(out=ac[:], in_=ac[:], func=AF.Silu),
                                      reads=[Bac], writes=[Bac])
                                fw.op("dve", "tensor_tensor", dict(out=m.sq[:], in0=ac[:], in1=ac[:], op=ALU.mult),
                                      reads=[Bac], writes=[m.Bsq])
                                p2, Bp2 = pp.get()
                                fw.op("pe", "matmul", MM(p2[:], lhsT=ones_b[:], rhs=m.sq[:], start=True, stop=True),
                                      reads=[m.Bsq, Bconst], writes=[Bp2])
                                fw.op("act", "activation", dict(out=m.rstd[:], in_=p2[:], func=AF.Sqrt, scale=1.0, bias=EPS),
                                      reads=[Bp2], writes=[m.Brstd])
                                fw.op("dve", "reciprocal", dict(out=m.rstd[:], in_=m.rstd[:]), reads=[m.Brstd], writes=[m.Brstd])
                                sc_ = (128.0 ** -0.5) if oc < 16 else 1.0
                                fw.op("dve", "scalar_tensor_tensor", dict(
                                    out=o_[:], in0=ac[:], scalar=sc_, in1=m.rstd[:], op0=ALU.mult, op1=ALU.mult),
                                    reads=[Bac, m.Brstd], writes=[Bo_])
                            store(qkvT[oc * 128:(oc + 1) * 128, t0:t0 + TT], o_[:], Bo_, Bqkv)
                        elif oc < 64:
                            o_, Bo_ = ost.get()
                            fw.op("act", "activation", dict(out=o_[:], in_=p1[:], func=AF.Silu),
                                  reads=[Bp1], writes=[Bo_])
                            r0 = (oc - 48) * 128
                            store(zT[r0:r0 + 128, t0:t0 + TT], o_[:], Bo_, Bz)
                        elif oc < 96:
                            isq = oc < 80
                            fw.op("act", "activation", dict(out=m.sq[:], in_=p1[:], func=AF.Square),
                                  reads=[Bp1], writes=[m.Bsq])
                            p2, Bp2 = pp.get()
                            fw.op("pe", "matmul", MM(p2[:], lhsT=ones_b[:], rhs=m.sq[:], start=True, stop=True),
                                  reads=[m.Bsq, Bconst], writes=[Bp2])
                            fw.op("act", "activation", dict(out=m.rstd[:], in_=p2[:], func=AF.Sqrt, scale=1.0 / 128, bias=EPS),
                                  reads=[Bp2], writes=[m.Brstd])
                            fw.op("dve", "reciprocal", dict(out=m.rstd[:], in_=m.rstd[:]), reads=[m.Brstd], writes=[m.Brstd])
                            o_, Bo_ = ost.get()
                            gcol = qgs[:, 0:1] if isq else hgt[:, 2:3]
                            fw.op("dve", "scalar_tensor_tensor", dict(
                                out=o_[:], in0=p1[:], scalar=gcol, in1=m.rstd[:], op0=ALU.mult, op1=ALU.mult),
                                reads=[Bp1, m.Brstd, Bconst], writes=[Bo_])
                            if isq:
                                r0 = (oc - 64) * 128
                                store(qmT[r0:r0 + 128, t0:t0 + TT], o_[:], Bo_, Bqm)
                            else:
                                r0 = (oc - 80) * 128
                                store(kmT[r0:r0 + 128, t0:t0 + TT], o_[:], Bo_, Bkm)
                        else:
                            o_, Bo_ = ost.get()
                            fw.op("act", "activation", dict(out=o_[:], in_=p1[:], func=AF.Sigmoid),
                                  reads=[Bp1], writes=[Bo_])
                            if oc < 128:
                                r0 = (oc - 112) * 128
                                store(gaT[r0:r0 + 128, t0:t0 + TT], o_[:], Bo_, Bga)
                            else:
                                r0 = (oc - 128) * 128
                                store(gbT[r0:r0 + 128, t0:t0 + TT], o_[:], Bo_, Bgb)
                for tb in range(4):
                    pv, Bpv = pp.get()
                    for kc in range(KC):
                        fw.op("pe", "matmul", MM(
                            pv[:, 0:32], lhsT=m.ub[:, kc, tb * 128:(tb + 1) * 128], rhs=wba[:, kc, :],
                            start=(kc == 0), stop=(kc == KC - 1)),
                            reads=[Bwba, m.Bub], writes=[Bpv], sig=(kc == KC - 1))
                    fw.op("act", "activation", dict(out=bast[:, tb, :], in_=pv[:, 0:32], func=AF.Copy),
                          reads=[Bpv], writes=[Bbast])
                store(vm.rearrange("(n p) d -> p n d", p=128)[:, ti * 4:(ti + 1) * 4, :], vst[:], Bvst, Bvm)
                store(ba.rearrange("(n p) d -> p n d", p=128)[:, ti * 4:(ti + 1) * 4, :], bast[:], Bbast, Bba)
                if ti == NPRE - 1:
                    fw.op("dve", "tensor_scalar", dict(out=halo[:], in0=halo[:], scalar1=flagv[:, 0:1], scalar2=None, op0=ALU.mult),
                          reads=[Bhalo, Bconst], writes=[Bhalo])
            fw.barrier()
        fw.es = es
        if stage == 2:
            fw.finish([Bh1, Bqkv, Bz, Bqm, Bkm, Bvm, Bba, Bga, Bgb])
            print("instructions recorded:", fw.ninst)
            fw.emit()
            return nc

        NG = NT
        NCHr = NG * 8
        TR = NT * TT
        with ExitStack() as esB:
            fw.es = esB
            tri = cload("tri", [64, 64], tri_in)
            mstrict = cload("mstrict", [64, 64], mstrict_in)
            alB = fw.sb("alB", [64, NH], F32); dtB = fw.sb("dtB", [64, NH], F32)
            fw.dma("sp", trk["const"], alB[:], alog[0:1, :].to_broadcast([64, NH]), writes=[Bconst])
            fw.dma("sp", trk["const"], dtB[:], dtb[0:1, :].to_broadcast([64, NH]), writes=[Bconst])
            ba3 = fw.sb("ba3", [64, NCH, 32], F32); Bba3 = Buf("ba3")
            fw.dma("sp", trk["l0"], ba3[:, 0:NCHr, :], ba.rearrange("(n p) c -> p n c", p=64)[:, 0:NCHr, :], reads=[Bba], writes=[Bba3])
            NC_ = NCHr * NH
            beta = fw.sb("beta", [64, NCH, NH], F32); gg = fw.sb("gg", [64, NCH, NH], F32)
            gc = fw.sb("gc", [64, NCH, NH], F32); egc = fw.sb("egc", [64, NCH, NH], F32)
            bege = fw.sb("bege", [64, NCH, NH], F32); edec = fw.sb("edec", [64, NCH, NH], F32)
            egs = fw.sb("egs", [128, NCH, NH], F32)
            Bg = Buf("gstuff")
            R_ = slice(0, NCHr)
            fw.op("act", "activation", dict(out=beta[:, R_, :], in_=ba3[:, R_, 0:16], func=AF.Sigmoid), reads=[Bba3], writes=[Bg])
            fw.op("dve", "tensor_scalar", dict(out=beta[:, 0:NPRE * 8, :], in0=beta[:, 0:NPRE * 8, :], scalar1=flagv[0:64, 0:1], scalar2=None, op0=ALU.mult),
                  reads=[Bg, Bconst], writes=[Bg])
            fw.op("dve", "tensor_tensor", dict(out=gg[:, R_, :], in0=ba3[:, R_, 16:32],
                                                   in1=dtB[:].unsqueeze(1).to_broadcast([64, NCHr, NH]), op=ALU.add),
                  reads=[Bba3, Bconst], writes=[Bg])
            fw.op("act", "activation", dict(out=gg[:, R_, :], in_=gg[:, R_, :], func=AF.Exp), reads=[Bg], writes=[Bg])
            fw.op("act", "activation", dict(out=gg[:, R_, :], in_=gg[:, R_, :], func=AF.Ln, bias=1.0, scale=1.0), reads=[Bg], writes=[Bg])
            fw.op("act", "activation", dict(out=alB[:], in_=alB[:], func=AF.Exp), reads=[Bconst], writes=[Bconst])
            fw.op("dve", "scalar_tensor_tensor", dict(out=gg[:, R_, :], in0=gg[:, R_, :], scalar=-1.0,
                                                          in1=alB[:].unsqueeze(1).to_broadcast([64, NCHr, NH]),
                                                          op0=ALU.mult, op1=ALU.mult), reads=[Bg, Bconst], writes=[Bg])
            ggf = gg[:].rearrange("p n h -> p (n h)"); gcf = gc[:].rearrange("p n h -> p (n h)")
            egsf = egs[:].rearrange("p n h -> p (n h)")
            for cc in range(0, NC_, 512):
                w_ = min(512, NC_ - cc)
                p1, Bp1 = pp.get()
                fw.op("pe", "matmul", MM(p1[0:64, 0:w_], lhsT=tri[:], rhs=ggf[:, cc:cc + w_], start=True, stop=True),
                      reads=[Bg, Bconst], writes=[Bp1])
                fw.op("dve", "tensor_copy", dict(out=gcf[:, cc:cc + w_], in_=p1[0:64, 0:w_]), reads=[Bp1], writes=[Bg])
                p2, Bp2 = pp.get()
                fw.op("pe", "matmul", MM(p2[:, 0:w_], lhsT=ones_f[0:64, :], rhs=ggf[:, cc:cc + w_], start=True, stop=True),
                      reads=[Bg, Bconst], writes=[Bp2])
                fw.op("dve", "tensor_copy", dict(out=egsf[:, cc:cc + w_], in_=p2[:, 0:w_]), reads=[Bp2], writes=[Bg])
            fw.op("dve", "tensor_tensor", dict(out=edec[:, R_, :], in0=egs[0:64, R_, :], in1=gc[:, R_, :], op=ALU.subtract), reads=[Bg], writes=[Bg])
            fw.op("act", "activation", dict(out=edec[:, R_, :], in_=edec[:, R_, :], func=AF.Exp), reads=[Bg], writes=[Bg])
            fw.op("act", "activation", dict(out=egs[:, R_, :], in_=egs[:, R_, :], func=AF.Exp), reads=[Bg], writes=[Bg])
            fw.op("act", "activation", dict(out=egc[:, R_, :], in_=gc[:, R_, :], func=AF.Exp), reads=[Bg], writes=[Bg])
            fw.op("dve", "tensor_tensor", dict(out=bege[:, R_, :], in0=beta[:, R_, :], in1=egc[:, R_, :], op=ALU.mult), reads=[Bg], writes=[Bg])

            kTh = fw.sb("kTh", [128, T], F32); qTh = fw.sb("qTh", [128, T], F32); vTh = fw.sb("vTh", [128, T], F32)
            zTh = fw.sb("zTh", [128, T], F32); oTh = fw.sb("oTh", [128, T], F32)
            Bk, Bq, Bv_, Bzh, Bo = Buf("kTh"), Buf("qTh"), Buf("vTh"), Buf("zTh"), Buf("oTh")
            S = fw.sb("S", [128, 128], F32); BS = Buf("S")
            kbe = fw.sb("kbe", [64, 8, 128], F32); kdec = fw.sb("kdec", [64, 8, 128], F32); vb = fw.sb("vb", [64, 8, 128], F32)
            Bkbe, Bkdec, Bvb = Buf("kbe"), Buf("kdec"), Buf("vb")
            Gbc = fw.sb("Gbc", [64, 8, 128], F32); BGbc = Buf("Gbc")
            def g64(name):
                return fw.sb(name, [64, 8, 64], F32), Buf(name)
            d1, Bd1 = g64("d1"); decA, BdecA = g64("decA"); decT, BdecT = g64("decT")
            Am, BAm = g64("Am"); ATm, BATm = g64("ATm"); QT, BQT = g64("QT"); PT, BPT = g64("PT")
            M2a, BM2a = g64("M2a"); MT2a, BMT2a = g64("MT2a"); M2b, BM2b = g64("M2b"); MT2b, BMT2b = g64("MT2b")
            egcB = fw.sb("egcB", [128, 512], F32); BegcB = Buf("egcB")
            qd = fw.sb("qd", [128, 512], F32); Bqd = Buf("qd")
            U = fw.sb("U", [64, 8, 128], F32); BU = Buf("U")
            WT = fw.sb("WT", [128, 8, 64], F32); BWT = Buf("WT")
            vnr = Rot(fw, 2, [64, 128], F32, "vn")
            U_b = fw.sb("U_b", [64, 8, 128], F32); BU_b = Buf("U_b")
            WT_b = fw.sb("WT_b", [128, 8, 64], F32); BWT_b = Buf("WT_b")
            PT_b, BPT_b = g64("PT_b")
            qd_b = fw.sb("qd_b", [128, 512], F32); Bqd_b = Buf("qd_b")
            kdec_b = fw.sb("kdec_b", [64, 8, 128], F32); Bkdec_b = Buf("kdec_b")
            sqd = fw.sb("sqd", [128, 512], F32); Bsqd = Buf("sqd")
            rsd = fw.sb("rsd", [128, 512], F32); Brsd = Buf("rsd")
            yst = Rot(fw, 2, [128, 512], BF16, "yst")
            fl = lambda t: t[:].rearrange("p c i -> p (c i)")
            for hh in range(NH):
                fw.dma("sp", trk["l1"], qTh[:, 0:TR], qkvT[hh * 128:(hh + 1) * 128, 0:TR], reads=[Bqkv], writes=[Bq])
                fw.dma("sp", trk["l2"], kTh[:, 0:TR], qkvT[(16 + hh) * 128:(17 + hh) * 128, 0:TR], reads=[Bqkv], writes=[Bk])
                fw.dma("sp", trk["l3"], vTh[:, 0:TR], qkvT[(32 + hh) * 128:(33 + hh) * 128, 0:TR], reads=[Bqkv], writes=[Bv_])
                fw.dma("sp", trk["l4"], zTh[:, 0:TR], zT[hh * 128:(hh + 1) * 128, 0:TR], reads=[Bz], writes=[Bzh])
                fw.op("dve", "memset", MS(S[:], 0.0), writes=[BS])
                def prep(gi, RB):
                    U, BU, WT, BWT, PT, BPT, qd, Bqd, kdec, Bkdec = RB
                    n0 = gi * 8
                    t0 = gi * TT
                    opx = (lambda *a_, **k_: None) if gi < NPRE else fw.op
                    bc8 = lambda arr, w: arr[:, n0:n0 + 8, hh:hh + 1].to_broadcast([64, 8, w])
                    bc4 = lambda arr, a, w: arr[:, n0 + a:n0 + a + 4, hh:hh + 1].to_broadcast([64, 4, w])
                    for half in range(2):
                        pk, Bpk = pp.get()
                        pv, Bpv = pp.get()
                        for c in range(4):
                            cc = half * 4 + c
                            ts_ = slice(t0 + cc * 64, t0 + cc * 64 + 64)
                            fw.op("pe", "transpose", TRP(pk[0:64, c * 128:(c + 1) * 128], kTh[:, ts_], ident[:]),
                                  reads=[Bk, Bconst], writes=[Bpk], sig=(c == 3))
                            fw.op("pe", "transpose", TRP(pv[0:64, c * 128:(c + 1) * 128], vTh[:, ts_], ident[:]),
                                  reads=[Bv_, Bconst], writes=[Bpv], sig=(c == 3))
                        a = half * 4
                        pk3 = pk[0:64, :].rearrange("p (c d) -> p c d", c=4)
                        pv3 = pv[0:64, :].rearrange("p (c d) -> p c d", c=4)
                        fw.op("dve", "tensor_tensor", dict(out=kbe[:, a:a + 4, :], in0=pk3, in1=bc4(bege, a, 128), op=ALU.mult),
                              reads=[Bpk, Bg], writes=[Bkbe])
                        fw.op("dve", "tensor_tensor", dict(out=kdec[:, a:a + 4, :], in0=pk3, in1=bc4(edec, a, 128), op=ALU.mult),
                              reads=[Bpk, Bg], writes=[Bkdec])
                        fw.op("dve", "tensor_tensor", dict(out=vb[:, a:a + 4, :], in0=pv3, in1=bc4(beta, a, 128), op=ALU.mult),
                              reads=[Bpv, Bg], writes=[Bvb])
                        yield
                    fw.op("dve", "tensor_copy", dict(out=Gbc[:], in_=bc8(gg, 128)), reads=[Bg], writes=[BGbc])
                    pG, BpG = pp.get(); pKQ, BpKQ = pp.get(); pgB, BpgB = pp.get()
                    for c in range(8):
                        ts_ = slice(t0 + c * 64, t0 + c * 64 + 64)
                        cs_ = slice(c * 64, c * 64 + 64)
                        fw.op("pe", "matmul", MM(pG[0:64, cs_], lhsT=kTh[:, ts_], rhs=kTh[:, ts_], start=True, stop=True),
                              reads=[Bk], writes=[BpG], sig=(c == 7))
                        opx("pe", "matmul", MM(pKQ[0:64, cs_], lhsT=kTh[:, ts_], rhs=qTh[:, ts_], start=True, stop=True),
                              reads=[Bk, Bq], writes=[BpKQ], sig=(c == 7))
                        fw.op("pe", "matmul", MM(pgB[:, cs_], lhsT=Gbc[:, c, :], rhs=tri[:], start=True, stop=True),
                              reads=[BGbc, Bconst], writes=[BpgB], sig=(c == 7))
                    yield
                    pgB3 = pgB[0:64, :].rearrange("p (c i) -> p c i", c=8)
                    pG3 = pG[0:64, :].rearrange("p (c i) -> p c i", c=8)
                    pKQ3 = pKQ[0:64, :].rearrange("p (c i) -> p c i", c=8)
                    fw.op("dve", "tensor_tensor", dict(out=d1[:], in0=pgB3, in1=bc8(gc, 64), op=ALU.subtract), reads=[BpgB, Bg], writes=[Bd1])
                    fw.op("dve", "tensor_scalar", dict(out=decA[:], in0=d1[:], scalar1=0.0, scalar2=None, op0=ALU.max), reads=[Bd1], writes=[BdecA])
                    fw.op("act", "activation", dict(out=decA[:], in_=decA[:], func=AF.Exp, scale=-1.0), reads=[BdecA], writes=[BdecA])
                    fw.op("dve", "tensor_tensor", dict(out=decA[:], in0=decA[:], in1=mstrict[:].unsqueeze(1).to_broadcast([64, 8, 64]), op=ALU.mult),
                          reads=[BdecA, Bconst], writes=[BdecA])
                    opx("dve", "tensor_scalar", dict(out=decT[:], in0=d1[:], scalar1=0.0, scalar2=None, op0=ALU.min), reads=[Bd1], writes=[BdecT])
                    opx("act", "activation", dict(out=decT[:], in_=decT[:], func=AF.Exp), reads=[BdecT], writes=[BdecT])
                    opx("dve", "tensor_tensor", dict(out=decT[:], in0=decT[:], in1=tri[:].unsqueeze(1).to_broadcast([64, 8, 64]), op=ALU.mult),
                          reads=[BdecT, Bconst], writes=[BdecT])
                    fw.op("dve", "tensor_tensor", dict(out=Am[:], in0=pG3, in1=bc8(beta, 64), op=ALU.mult), reads=[BpG, Bg], writes=[BAm])
                    fw.op("dve", "tensor_tensor", dict(out=Am[:], in0=Am[:], in1=decA[:], op=ALU.mult), reads=[BAm, BdecA], writes=[BAm])
                    opx("dve", "tensor_tensor", dict(out=PT[:], in0=pKQ3, in1=decT[:], op=ALU.mult), reads=[BpKQ, BdecT], writes=[BPT])
                    opx("act", "activation", dict(out=egcB[:], in_=pgB[:], func=AF.Exp), reads=[BpgB], writes=[BegcB])
                    opx("dve", "tensor_tensor", dict(out=qd[:], in0=qTh[:, t0:t0 + TT], in1=egcB[:], op=ALU.mult),
                          reads=[Bq, BegcB], writes=[Bqd])
                    yield
                    pT_, BpT_ = pp.get()
                    for c in range(8):
                        cs_ = slice(c * 64, c * 64 + 64)
                        fw.op("pe", "transpose", TRP(pT_[0:64, cs_], Am[:, c, :], ident[0:64, 0:64]),
                              reads=[BAm, Bconst], writes=[BpT_], sig=(c == 7))
                    fw.op("dve", "tensor_copy", dict(out=fl(ATm), in_=pT_[0:64, :]), reads=[BpT_], writes=[BATm])
                    fw.op("dve", "tensor_tensor", dict(out=QT[:], in0=ident[0:64, 0:64].unsqueeze(1).to_broadcast([64, 8, 64]), in1=ATm[:], op=ALU.subtract),
                          reads=[BATm, Bconst], writes=[BQT])
                    yield
                    Mc, BMc, MTc, BMTc = Am, BAm, ATm, BATm
                    pingpong = [(M2a, BM2a, MT2a, BMT2a), (M2b, BM2b, MT2b, BMT2b)]
                    for k in range(5):
                        Mn, BMn, MTn, BMTn = pingpong[k % 2]
                        pM, BpM = pp.get()
                        for c in range(8):
                            cs_ = slice(c * 64, c * 64 + 64)
                            fw.op("pe", "matmul", MM(pM[0:64, cs_], lhsT=MTc[:, c, :], rhs=Mc[:, c, :], start=True, stop=True),
                                  reads=[BMc, BMTc], writes=[BpM], sig=(c == 7))
                        fw.op("dve", "tensor_copy", dict(out=fl(Mn), in_=pM[0:64, :]), reads=[BpM], writes=[BMn])
                        yield
                        if k < 4:
                            pMT, BpMT = pp.get()
                            for c in range(8):
                                cs_ = slice(c * 64, c * 64 + 64)
                                fw.op("pe", "matmul", MM(pMT[0:64, cs_], lhsT=Mc[:, c, :], rhs=MTc[:, c, :], start=True, stop=True),
                                      reads=[BMc, BMTc], writes=[BpMT], sig=(c == 7))
                            fw.op("act", "activation", dict(out=fl(MTn), in_=pMT[0:64, :], func=AF.Copy), reads=[BpMT], writes=[BMTn])
                            yield
                        pQ, BpQ = pp.get()
                        for c in range(8):
                            cs_ = slice(c * 64, c * 64 + 64)
                            fw.op("pe", "matmul", MM(pQ[0:64, cs_], lhsT=Mn[:, c, :], rhs=QT[:, c, :], start=True, stop=True),
                                  reads=[BMn, BQT], writes=[BpQ], sig=(c == 7))
                        fw.op("dve", "tensor_tensor", dict(out=fl(QT), in0=fl(QT), in1=pQ[0:64, :], op=ALU.add), reads=[BpQ, BQT], writes=[BQT])
                        yield
                        Mc, BMc, MTc, BMTc = Mn, BMn, MTn, BMTn
                    for half in range(2):
                        pU, BpU = pp.get()
                        for c in range(4):
                            cc = half * 4 + c
                            fw.op("pe", "matmul", MM(pU[0:64, c * 128:(c + 1) * 128], lhsT=QT[:, cc, :], rhs=vb[:, cc, :], start=True, stop=True),
                                  reads=[BQT, Bvb], writes=[BpU], sig=(c == 3))
                        fw.op("dve", "tensor_copy", dict(out=U[:, half * 4:half * 4 + 4, :].rearrange("p c e -> p (c e)"), in_=pU[0:64, :]),
                              reads=[BpU], writes=[BU])
                        yield
                    pW, BpW = pp.get()
                    for c in range(8):
                        cs_ = slice(c * 64, c * 64 + 64)
                        fw.op("pe", "matmul", MM(pW[:, cs_], lhsT=kbe[:, c, :], rhs=QT[:, c, :], start=True, stop=True),
                              reads=[Bkbe, BQT], writes=[BpW], sig=(c == 7))
                    fw.op("act", "activation", dict(out=WT[:].rearrange("p c i -> p (c i)"), in_=pW[:], func=AF.Copy), reads=[BpW], writes=[BWT])
                    yield
                def rec(gi, RB, gen):
                    U, BU, WT, BWT, PT, BPT, qd, Bqd, kdec, Bkdec = RB
                    n0 = gi * 8
                    t0 = gi * TT
                    opx = (lambda *a_, **k_: None) if gi < NPRE else fw.op
                    def filler(k):
                        if gen is not None:
                            for _ in range(k):
                                next(gen, None)
                    for c in range(8):
                        n = n0 + c
                        cs_ = slice(c * 64, c * 64 + 64)
                        pWS, BpWS = pp.get()
                        fw.op("pe", "matmul", MM(pWS[0:64, 0:128], lhsT=WT[:, c, :], rhs=S[:], start=True, stop=True),
                              reads=[BWT, BS], writes=[BpWS])
                        vn, Bvn = vnr.get()
                        fw.op("dve", "tensor_tensor", dict(out=vn[:], in0=U[:, c, :], in1=pWS[0:64, 0:128], op=ALU.subtract),
                              reads=[BU, BpWS], writes=[Bvn])
                        filler(2)
                        pO, BpO = pp.get()
                        opx("pe", "matmul", MM(pO[:, 0:64], lhsT=S[:], rhs=qd[:, cs_], start=True, stop=False),
                              reads=[BS, Bqd], writes=[BpO], sig=False)
                        opx("pe", "matmul", MM(pO[:, 0:64], lhsT=vn[:], rhs=PT[:, c, :], start=False, stop=True),
                              reads=[Bvn, BPT], writes=[BpO])
                        pS, BpS = pp.get()
                        fw.op("pe", "matmul", MM(pS[:, 0:128], lhsT=kdec[:, c, :], rhs=vn[:], start=True, stop=True),
                              reads=[Bkdec, Bvn], writes=[BpS])
                        fw.op("dve", "scalar_tensor_tensor", dict(out=S[:], in0=S[:], scalar=egs[:, n, hh:hh + 1], in1=pS[:, 0:128],
                                                                                op0=ALU.mult, op1=ALU.add),
                              reads=[BS, BpS, Bg], writes=[BS])
                        filler(2)
                        opx("act", "activation", dict(out=oTh[:, t0 + c * 64:t0 + c * 64 + 64], in_=pO[:, 0:64], func=AF.Copy),
                              reads=[BpO], writes=[Bo])
                RBs = [(U, BU, WT, BWT, PT, BPT, qd, Bqd, kdec, Bkdec), (U_b, BU_b, WT_b, BWT_b, PT_b, BPT_b, qd_b, Bqd_b, kdec_b, Bkdec_b)]
                for _ in prep(0, RBs[0]):
                    pass
                for gi in range(NG):
                    gen = prep(gi + 1, RBs[(gi + 1) % 2]) if gi + 1 < NG else None
                    rec(gi, RBs[gi % 2], gen)
                    if gen is not None:
                        for _ in gen:
                            pass
                for gi in range(NPRE, NG):
                    t0 = gi * TT
                    fw.op("dve", "tensor_tensor", dict(out=sqd[:], in0=oTh[:, t0:t0 + TT], in1=oTh[:, t0:t0 + TT], op=ALU.mult),
                          reads=[Bo], writes=[Bsqd])
                    p2, Bp2 = pp.get()
                    fw.op("pe", "matmul", MM(p2[:], lhsT=ones_f[:], rhs=sqd[:], start=True, stop=True), reads=[Bsqd, Bconst], writes=[Bp2])
                    fw.op("act", "activation", dict(out=rsd[:], in_=p2[:], func=AF.Sqrt, scale=1.0 / 128, bias=EPS), reads=[Bp2], writes=[Brsd])
                    fw.op("dve", "reciprocal", dict(out=rsd[:], in_=rsd[:]), reads=[Brsd], writes=[Brsd])
                    fw.op("dve", "scalar_tensor_tensor", dict(out=sqd[:], in0=oTh[:, t0:t0 + TT], scalar=hgt[:, 0:1], in1=rsd[:], op0=ALU.mult, op1=ALU.mult),
                          reads=[Bo, Brsd, Bconst], writes=[Bsqd])
                    ys, Bys = yst.get()
                    fw.op("dve", "tensor_tensor", dict(out=ys[:], in0=sqd[:], in1=zTh[:, t0:t0 + TT], op=ALU.mult),
                          reads=[Bsqd, Bzh], writes=[Bys])
                    fw.dma("sp", btrack(Bys), yaT[hh * 128:(hh + 1) * 128, t0:t0 + TT], ys[:], reads=[Bys], writes=[Bya])
            fw.barrier()
        fw.es = es
        if stage == 3:
            fw.finish([Bya])
            print("instructions recorded:", fw.ninst)
            fw.emit()
            return nc

        with ExitStack() as esC:
            fw.es = esC
            rb = cload("rb", [32, NH], relb)
            ngs = Rot(fw, 2, [1, 512], F32, "ngs")
            pastneg = cload("pastneg", [128, 16, 16], pastneg_in)
            pastflag = cload("pastflag", [128, 16, 16], pastflag_in)
            extram = cload("extram", [128, 16, 16], extram_in)
            esel = cload("esel", [16, 16, 128], esel_in, BF16, "pool")
            ohs = Rot(fw, 2, [32, 512], F32, "ohs")
            est = Rot(fw, 2, [16, 512], F32, "est")
            for cc in range(0, LE, 512):
                w_ = min(512, LE - cc)
                oh_, Boh = ohs.get()
                fw.dma("sp", btrack(Boh), oh_[:, 0:w_], oh_in[:, cc:cc + w_], writes=[Boh])
                pe_, Bpe_ = pp.get()
                fw.op("pe", "matmul", MM(pe_[0:16, 0:w_], lhsT=rb[:], rhs=oh_[:, 0:w_], start=True, stop=False),
                      reads=[Boh, Bconst], writes=[Bpe_], sig=False)
                ng_, Bng_ = ngs.get()
                fw.dma("sp", btrack(Bng_), ng_[:, 0:w_], negm_in[:, cc:cc + w_], writes=[Bng_])
                fw.op("pe", "matmul", MM(pe_[0:16, 0:w_], lhsT=ones_f[0:1, 0:16], rhs=ng_[:, 0:w_], start=False, stop=True),
                      reads=[Bconst, Bng_], writes=[Bpe_])
                e_, Be_ = est.get()
                fw.op("act", "activation", dict(out=e_[:, 0:w_], in_=pe_[0:16, 0:w_], func=AF.Copy), reads=[Bpe_], writes=[Be_])
                fw.dma("sp", btrack(Be_), E_d[:, cc:cc + w_], e_[:, 0:w_], reads=[Be_], writes=[BE])
            qf = fw.sb("qf", [128, T], F32); kf = fw.sb("kf", [128, T], F32)
            qb = fw.sb("qb", [128, T], BF16); kb = fw.sb("kb", [128, T], BF16)
            vmh = fw.sb("vmh", [128, T // 128, 128], BF16)
            Bqf, Bkf, Bqb, Bkb, Bvmh = Buf("qf"), Buf("kf"), Buf("qb"), Buf("kb"), Buf("vmh")
            qf2 = fw.sb("qf2", [128, T], F32); kf2 = fw.sb("kf2", [128, T], F32)
            vmh2 = fw.sb("vmh2", [128, T // 128, 128], BF16)
            Bqf2, Bkf2, Bvmh2 = Buf("qf2"), Buf("kf2"), Buf("vmh2")
            hank2 = fw.sb("hank2", [128, WSH], F32); Bhank2 = Buf("hank2")
            tsh = fw.sb("tsh", [128, WSH], F32); Btsh = Buf("tsh")
            hank = fw.sb("hank", [128, WSH], F32); Bhank = Buf("hank")
            jrev = cload("jrev", [128, 128], jrev_in)
            kmean = fw.sb("kmean", [128, 16], F32); Bkmean = Buf("kmean")
            gm = fw.sb("gm", [128, 16], F32); top8 = fw.sb("top8", [128, 8], F32); mv = fw.sb("mv", [128, 16], F32)
            Bgm, Btop8, Bmv = Buf("gm"), Buf("top8"), Buf("mv")
            mvT = fw.sb("mvT", [16, T], BF16); BmvT = Buf("mvT")
            gmA = fw.sb("gmA", [128, 16, 16], F32); BgmA = Buf("gmA")
            top8A = fw.sb("top8A", [128, 16, 8], F32); Btop8A = [Buf(f"top8A{i}") for i in range(4)]
            mvA = fw.sb("mvA", [128, 16, 16], F32); BmvA = Buf("mvA")
            ssb = Rot(fw, 3, [128, 512], F32, "ssb")
            ptb = Rot(fw, 4, [128, 512], BF16, "ptb")
            rsm = fw.sb("rsm", [128, 512], F32); Brsm = Buf("rsm")
            ybs = Rot(fw, 2, [128, 512], BF16, "ybs")
            NQB = TR // 128
            po_t = fw.ps("po_t", [128, 512]); Bpo_t = Buf("po_t")
            psm_t = fw.ps("psm_t", [128, 512]); Bpsm_t = Buf("psm_t")
            pp.n = 4; pp.i = 0
            accs = [(po_t, Bpo_t, psm_t, Bpsm_t), (pp.t[4], pp.b[4], pp.t[5], pp.b[5])]
            hsets = [(qf, Bqf, kf, Bkf, vmh, Bvmh, hank, Bhank), (qf2, Bqf2, kf2, Bkf2, vmh2, Bvmh2, hank2, Bhank2)]
            def hloads(hx):
                qf_, Bqf_, kf_, Bkf_, vmh_, Bvmh_, hank_, Bhank_ = hsets[hx % 2]
                fw.dma("sp", btrack(Bqf_), qf_[:, 0:TR], qmT[hx * 128:(hx + 1) * 128, 0:TR], reads=[Bqm], writes=[Bqf_])
                fw.dma("sp", btrack(Bkf_), kf_[:, 0:TR], kmT[hx * 128:(hx + 1) * 128, 0:TR], reads=[Bkm], writes=[Bkf_])
                fw.dma("sp", btrack(Bvmh_), vmh_[:, 0:NQB, :], vm.rearrange("(n p) d -> p n d", p=128)[:, 0:NQB, hx * 128:(hx + 1) * 128],
                       reads=[Bvm], writes=[Bvmh_])
                tsrc = bass.AP(E_d.tensor, hx * LE, [[1, 128], [1, WSH]])
                fw.dma("sp", btrack(Bhank_), hank_[:], tsrc, reads=[BE], writes=[Bhank_])
            hloads(0)
            for hh in range(NH):
                qf, Bqf, kf, Bkf, vmh, Bvmh, hank, Bhank = hsets[hh % 2]
                if hh + 1 < NH:
                    hloads(hh + 1)
                for cc in range(0, WSH, 512):
                    w_ = min(512, WSH - cc)
                    pj, Bpj = pp.get()
                    fw.op("pe", "matmul", MM(pj[:, 0:w_], lhsT=jrev[:], rhs=hank[:, cc:cc + w_], start=True, stop=True),
                          reads=[Bhank, Bconst], writes=[Bpj])
                    fw.op("act", "activation", dict(out=tsh[:, cc:cc + w_], in_=pj[:, 0:w_], func=AF.Copy), reads=[Bpj], writes=[Btsh])
                fw.op("dve", "tensor_copy", dict(out=qb[:, 0:TR], in_=qf[:, 0:TR]), reads=[Bqf], writes=[Bqb])
                fw.op("act", "activation", dict(out=kb[:, 0:TR], in_=kf[:, 0:TR], func=AF.Copy), reads=[Bkf], writes=[Bkb])
                nblk = TR // 256
                fw.op("dve", "memset", MS(kmean[:], 0.0), writes=[Bkmean])
                fw.op("dve", "tensor_reduce", dict(out=kmean[:, 0:nblk], in_=kf[:, 0:TR].rearrange("p (n k) -> p n k", k=256),
                                                                 axis=AX.X, op=ALU.add), reads=[Bkf], writes=[Bkmean])
                fw.op("dve", "tensor_scalar", dict(out=kmean[:], in0=kmean[:], scalar1=1.0 / 256, scalar2=None, op0=ALU.mult),
                      reads=[Bkmean], writes=[Bkmean])
                QB0 = NPRE * 4
                NQ = NQB - QB0
                pg, Bpg = pp.get()
                for qi in range(NQ):
                    qbk = QB0 + qi
                    fw.op("pe", "matmul", MM(pg[:, qi * 16:(qi + 1) * 16], lhsT=qf[:, qbk * 128:(qbk + 1) * 128], rhs=kmean[:], start=True, stop=True),
                          reads=[Bqf, Bkmean], writes=[Bpg], sig=(qi == NQ - 1))
                def pairb(t):
                    return t[:, NPRE * 2:NPRE * 2 + NQ // 2, :].unsqueeze(2).to_broadcast([128, NQ // 2, 2, 16])
                v4 = lambda t: t[:, 0:NQ, :].rearrange("p (a two) j -> p a two j", two=2)
                fw.op("dve", "tensor_tensor", dict(out=v4(gmA), in0=pg[:, 0:NQ * 16].rearrange("p (a two j) -> p a two j", two=2, j=16),
                                                   in1=pairb(pastneg), op=ALU.add), reads=[Bpg, Bconst], writes=[BgmA])
                for qi in range(NQ):
                    fw.op("dve", "max", dict(out=top8A[:, qi, :], in_=gmA[:, qi, :]), reads=[BgmA], writes=[Btop8A[qi % 4]])
                fw.op("dve", "tensor_tensor", dict(out=mvA[:, 0:NQ, :], in0=gmA[:, 0:NQ, :], in1=top8A[:, 0:NQ, 2:3].to_broadcast([128, NQ, 16]),
                                                   op=ALU.is_ge), reads=[BgmA] + Btop8A, writes=[BmvA])
                fw.op("dve", "tensor_scalar", dict(out=mvA[:, 0:NQ, :], in0=mvA[:, 0:NQ, :], scalar1=-NEG, scalar2=NEG, op0=ALU.mult, op1=ALU.add),
                      reads=[BmvA], writes=[BmvA])
                fw.op("dve", "tensor_tensor", dict(out=v4(mvA), in0=v4(mvA), in1=pairb(pastflag), op=ALU.mult), reads=[BmvA, Bconst], writes=[BmvA])
                fw.op("dve", "tensor_tensor", dict(out=v4(mvA), in0=v4(mvA), in1=pairb(extram), op=ALU.add), reads=[BmvA, Bconst], writes=[BmvA])
                for g4 in range(0, NQ, 4):
                    pt_, Bpt_ = pp.get()
                    for c in range(4):
                        fw.op("pe", "transpose", TRP(pt_[0:16, c * 128:(c + 1) * 128], mvA[:, g4 + c, :], ident[:]),
                              reads=[BmvA, Bconst], writes=[Bpt_], sig=(c == 3))
                    fw.op("act", "activation", dict(out=mvT[:, (QB0 + g4) * 128:(QB0 + g4 + 4) * 128], in_=pt_[0:16, :], func=AF.Copy),
                          reads=[Bpt_], writes=[BmvT])
                for qt in range(NPRE, NT):
                    q0 = qt * TT
                    po, Bpo, psm, Bpsm = accs[qt % 2]
                    nkt = 4 * qt + 4
                    LAG = 2
                    pend = {}
                    for kk in range(nkt + LAG):
                        if kk < nkt:
                            kt = kk
                            k0 = kt * 128
                            dl = q0 - k0
                            sp_, Bsp_ = pp.get()
                            fw.op("pe", "matmul", MM(sp_[:], lhsT=kb[:, k0:k0 + 128], rhs=qb[:, q0:q0 + TT], start=True, stop=False),
                                  reads=[Bkb, Bqb], writes=[Bsp_], sig=False)
                            fw.op("pe", "matmul", MM(sp_[:], lhsT=esel[:, kt // 2, :], rhs=mvT[:, q0:q0 + TT], start=False, stop=True),
                                  reads=[BmvT, Bconst], writes=[Bsp_])
                            pb_, Bpb_ = ptb.get()
                            if dl >= 1024:
                                fw.op("act", "activation", dict(out=pb_[:], in_=sp_[:], func=AF.Exp, bias=tsh[:, WSH - 1:WSH], scale=1.0),
                                      reads=[Bsp_, Btsh], writes=[Bpb_])
                            else:
                                sb_, Bsb_ = ssb.get()
                                fw.op("dve", "tensor_tensor", dict(out=sb_[:], in0=sp_[:], in1=tsh[:, 513 + dl:513 + dl + TT], op=ALU.add),
                                      reads=[Bsp_, Btsh], writes=[Bsb_])
                                fw.op("act", "activation", dict(out=pb_[:], in_=sb_[:], func=AF.Exp), reads=[Bsb_], writes=[Bpb_])
                            pend[kt] = (pb_, Bpb_)
                        if kk >= LAG:
                            kt = kk - LAG
                            pb_, Bpb_ = pend.pop(kt)
                            fw.op("pe", "matmul", MM(po[:], lhsT=vmh[:, kt, :], rhs=pb_[:], start=(kt == 0), stop=(kt == nkt - 1)),
                                  reads=[Bvmh, Bpb_], writes=[Bpo], sig=False)
                            fw.op("pe", "matmul", MM(psm[:], lhsT=ones_b[:], rhs=pb_[:], start=(kt == 0), stop=(kt == nkt - 1)),
                                  reads=[Bconst, Bpb_], writes=[Bpsm, Bpo])
                    fw.op("dve", "reciprocal", dict(out=rsm[:], in_=psm[:]), reads=[Bpsm], writes=[Brsm])
                    yb_, Byb_ = ybs.get()
                    fw.op("dve", "tensor_tensor", dict(out=yb_[:], in0=po[:], in1=rsm[:], op=ALU.mult), reads=[Bpo, Brsm], writes=[Byb_])
                    fw.dma("sp", btrack(Byb_), ybT[hh * 128:(hh + 1) * 128, q0:q0 + TT], yb_[:], reads=[Byb_], writes=[Byb])
            pp.n = 6
            fw.barrier()
        fw.es = es
        if stage == 4:
            fw.finish([Byb, BE])
            print("instructions recorded:", fw.ninst)
            fw.emit()
            return nc

        with ExitStack() as esD:
            fw.es = esD
            m = alloc_main()
            gst = Rot(fw, 4, [128, TT], F32, "gst")
            m1r = Rot(fw, 2, [128, TT], F32, "m1r")
            for ti in range(NPRE, NT):
                t0 = ti * TT
                fw.dma("sp", trk["x"], m.xt[:], fmv(h1T)[:, :, t0:t0 + TT], reads=[Bh1], writes=[m.Bxt])
                fw.dma("sp", trk["l0"], m.actb[:, 0:16, :], fmv(yaT)[:, :, t0:t0 + TT], reads=[Bya], writes=[m.Bact])
                fw.dma("sp", trk["l1"], m.actb[:, 16:32, :], fmv(ybT)[:, :, t0:t0 + TT], reads=[Byb], writes=[m.Bact])
                for blk in range(D // 256):
                    wat, wab = wload(m.wsA, "wpa", D // 256, blk, wv(wpa)[:, :, blk * 256:(blk + 1) * 256])
                    wbt, wbb = wload(m.wsA, "wpb", D // 256, blk, wv(wpb)[:, :, blk * 256:(blk + 1) * 256])
                    for j in range(2):
                        dc = blk * 2 + j
                        pa, Bpa = pp.get(); pb, Bpb = pp.get()
                        for kc in range(KC):
                            fw.op("pe", "matmul", MM(
                                pa[:], lhsT=wat[:, kc, j * 128:(j + 1) * 128], rhs=m.actb[:, kc, :], start=(kc == 0), stop=(kc == KC - 1)),
                                reads=[wab, m.Bact], writes=[Bpa], sig=(kc == KC - 1))
                        for kc in range(KC):
                            fw.op("pe", "matmul", MM(
                                pb[:], lhsT=wbt[:, kc, j * 128:(j + 1) * 128], rhs=m.actb[:, 16 + kc, :], start=(kc == 0), stop=(kc == KC - 1)),
                                reads=[wbb, m.Bact], writes=[Bpb], sig=(kc == KC - 1))
                        ga_, Bga_ = gst.get(); gb_, Bgb_ = gst.get()
                        fw.dma("sp", btrack(Bga_), ga_[:], gaT[dc * 128:(dc + 1) * 128, t0:t0 + TT], reads=[Bga], writes=[Bga_])
                        fw.dma("sp", btrack(Bgb_), gb_[:], gbT[dc * 128:(dc + 1) * 128, t0:t0 + TT], reads=[Bgb], writes=[Bgb_])
                        m1, Bm1 = m1r.get()
                        fw.op("dve", "tensor_tensor", dict(out=m1[:], in0=pa[:], in1=ga_[:], op=ALU.mult),
                              reads=[Bpa, Bga_], writes=[Bm1])
                        fw.op("dve", "tensor_tensor", dict(out=gb_[:], in0=pb[:], in1=gb_[:], op=ALU.mult),
                              reads=[Bpb, Bgb_], writes=[Bgb_])
                        fw.op("dve", "tensor_tensor", dict(out=m.ub[:, dc, :], in0=gb_[:], in1=m1[:], op=ALU.add),
                              reads=[Bgb_, Bm1], writes=[m.Bub])
                G2 = mod[:, 5 * KC:6 * KC]
                for blk in range(D // 256):
                    wot, wob = wload(m.wsA, "wout", D // 256, blk, wv(wout)[:, :, blk * 256:(blk + 1) * 256])
                    for j in range(2):
                        dc = blk * 2 + j
                        po, Bpo = pp.get()
                        for kc in range(KC):
                            fw.op("pe", "matmul", MM(
                                po[:], lhsT=wot[:, kc, j * 128:(j + 1) * 128], rhs=m.ub[:, kc, :], start=(kc == 0), stop=(kc == KC - 1)),
                                reads=[wob, m.Bub], writes=[Bpo], sig=(kc == KC - 1))
                        fw.op("dve", "scalar_tensor_tensor", dict(
                            out=m.xt[:, dc, :], in0=po[:], scalar=G2[:, dc:dc + 1], in1=m.xt[:, dc, :], op0=ALU.mult, op1=ALU.add),
                            reads=[Bpo, Bmod, m.Bxt], writes=[m.Bxt])
                rmsnorm_mod(m, 2)
                ffn(m, wv(w1b), wv(w3b), wv(w2b), 2, "f2")
                fw.dma("sp", trk["o"], fmv(outT)[:, :, t0 - NPRE * TT:t0 - NPRE * TT + TT], m.xt[:], reads=[m.Bxt], writes=[Bout])
            fw.finish([Bout])
        fw.es = es
        print("instructions recorded:", fw.ninst)
        fw.emit()
    return nc


def t5_bucket_np(d):
    f = np.float32
    dd = np.maximum(d, 1).astype(f)
    large = 16 + (np.log(dd / f(16)) / f(math.log(64)) * f(16)).astype(np.int32)
    large = np.minimum(large, 31)
    return np.where(d < 16, d, large)


_CONST = {}


def consts():
    if _CONST:
        return _CONST
    f = np.float32
    c = _CONST
    c["ones"] = np.ones((128, 128), f)
    c["ident"] = np.eye(128, dtype=f)
    c["jrev"] = np.ascontiguousarray(np.eye(128, dtype=f)[::-1])
    idx = np.arange(64)
    c["tri"] = (idx[:, None] <= idx[None, :]).astype(f)
    c["mstrict"] = (idx[:, None] > idx[None, :]).astype(f)
    dist = np.arange(LE, dtype=np.int64) - 640
    bk = t5_bucket_np(np.maximum(dist, 0).astype(np.int32))
    oh = np.zeros((32, LE), f)
    valid = dist >= 0
    oh[bk[valid], np.nonzero(valid)[0]] = 1.0
    c["oh"] = oh
    c["negm"] = np.where(valid, 0.0, NEG).astype(f)[None, :]
    j = np.arange(16)
    qb = np.arange(16)
    past = (j[None, :] < qb[:, None])
    c["pastneg"] = np.broadcast_to(np.where(past, 0.0, -1e30).astype(f)[None], (128, 16, 16)).copy()
    c["pastflag"] = np.broadcast_to(past.astype(f)[None], (128, 16, 16)).copy()
    es_ = np.zeros((16, 16, 128), f)
    for jj in range(16):
        es_[jj, jj, :] = 1.0
    c["esel"] = es_
    return c


def prep_shared(inputs):
    f = np.float32
    def fm(v):
        return np.ascontiguousarray(np.asarray(v, f).reshape(-1, 128).T)
    m = dict(consts())
    m["ada_w"] = np.asarray(inputs["ada_w"][0], f)
    m["ada_bT"] = fm(inputs["ada_b"][0])
    m["ngT"] = np.concatenate([fm(inputs["norm1_g"][0]), fm(inputs["norm2_g"][0]), fm(inputs["norm3_g"][0])], axis=1)
    for k in ("ffn1_w1", "ffn1_w3", "ffn1_w2", "ffn2_w1", "ffn2_w3", "ffn2_w2"):
        m[k] = np.asarray(inputs[k][0], f)
    w = np.asarray(inputs["w_in"][0], f)
    m["winp"] = np.ascontiguousarray(np.concatenate([w[:, 0:8192], w[:, 8224:18464], w[:, 8192:8224]], axis=1))
    cw = np.asarray(inputs["dn_conv_w"][0], f)
    m["convw"] = np.ascontiguousarray(cw.T.reshape(48, 128, 4).transpose(1, 0, 2))
    m["alog"] = np.asarray(inputs["dn_a_log"], f).reshape(1, NH)
    m["dtb"] = np.asarray(inputs["dn_dt_bias"], f).reshape(1, NH)
    m["hg"] = np.ascontiguousarray(np.stack([np.asarray(inputs["dn_norm_g"][0], f), np.asarray(inputs["mb_q_norm_g"][0], f),
                                             np.asarray(inputs["mb_k_norm_g"][0], f)], axis=1))
    m["relb"] = np.asarray(inputs["rel_bias"], f)
    m["wpa"] = np.asarray(inputs["w_proj_a"][0], f)
    m["wpb"] = np.asarray(inputs["w_proj_b"][0], f)
    m["wout"] = np.asarray(inputs["w_out"][0], f)
    return m


def seq_masks(sidx, npre_blk):
    f = np.float32
    j = np.arange(16)
    qb = np.arange(16)
    past = (j[None, :] < qb[:, None])
    extra = np.zeros((16, 16), f)
    if sidx == 0:
        past = past & (j[None, :] >= npre_blk)
        extra[:, :npre_blk] = NEG
    rep = lambda a: np.broadcast_to(a[None], (128, 16, 16)).copy()
    return (rep(np.where(past, 0.0, -1e30).astype(f)), rep(past.astype(f)), rep(extra))


def prep_core(inputs, shared, b, sidx, NT=NTILE):
    f = np.float32
    m = dict(shared)
    npre = NT // 2
    half = npre * TT
    xb = np.asarray(inputs["x"][b], f)
    xT = np.zeros((D, T), f)
    if sidx == 1:
        xT[:, 0:2 * half] = xb[0:2 * half].T
    else:
        xT[:, half:2 * half] = xb[0:half].T
    m["xT"] = xT
    m["cT"] = np.ascontiguousarray(np.asarray(inputs["c"][b], f).reshape(-1, 128).T)
    m["flagv"] = np.full((128, 1), float(sidx), f)
    m["pastneg"], m["pastflag"], m["extram"] = seq_masks(sidx, npre * 2)
    return m


_NC = {}


def kernel(**inputs):
    if "nc" not in _NC:
        _NC["nc"] = build(99, NTILE)
    nc = _NC["nc"]
    shared = prep_shared(inputs)
    maps = [prep_core(inputs, shared, c % 4, c // 4) for c in range(8)]
    res = run_bass_kernel_spmd(nc, maps, core_ids=list(range(8)))
    out = np.empty((4, T, D), np.float32)
    h = T // 2
    for c in range(8):
        out[c % 4, (c // 4) * h:(c // 4 + 1) * h, :] = res.results[c]["outT"].T
    return out
```
